# Optimizing a Trainium2 kernel written in Bass

```python
import math
import jax, jax.numpy as jnp
from jax import lax
import numpy as np

D_MODEL = 1024
BATCH = 2
SEQ = 8192
DEPTH = 2

GRID_W = 64
CTX_LEN = 256
N_EVEN = (DEPTH + 1) // 2
N_ODD = DEPTH // 2
RMS_EPS = 1e-6
F32 = jnp.float32

DA_HEADS = 8
DA_HEAD_DIM = 64
DA_V_DIM = 2 * DA_HEAD_DIM
DA_QK_W = DA_HEADS * 2 * DA_HEAD_DIM
DA_V_W = DA_HEADS * DA_V_DIM
Q_BLOCK = 128
ROPE_BASE = 10000.0

SSM_D_INNER = D_MODEL
SSM_HEAD_DIM = 64
SSM_HEADS = SSM_D_INNER // SSM_HEAD_DIM
SSM_GROUPS = 2
SSM_HPG = SSM_HEADS // SSM_GROUPS
SSM_STATE = 128
SSM_CONV = 5
SSM_CHUNK = 128
SSM_BC_W = SSM_GROUPS * SSM_STATE
SSM_XB_W = SSM_D_INNER + SSM_BC_W

COL_Q = 0
COL_Z = COL_Q + DA_QK_W
COL_C = COL_Z + SSM_D_INNER
COL_K = COL_C + SSM_BC_W
COL_V = COL_K + DA_QK_W
COL_XB = COL_V + DA_V_W
COL_DT = COL_XB + SSM_XB_W
IN_W = COL_DT + 2 * SSM_HEADS
MIX_W = DA_V_W + SSM_D_INNER

FOURIER_GROUPS = 4
FOURIER_GW = D_MODEL // FOURIER_GROUPS

N_EXPERTS = 32
TOP_K = 4
D_EXPERT = D_MODEL
SWIGLU_LIMIT = 7.0
SWIGLU_ALPHA = 1.702
MOE_BLOCK = 128

kernel_name = 'hybrid_diffattn_ssd_fourier_moe_dit'


def rmsnorm(x, g):
    xf = x.astype(F32)
    xf = xf * lax.rsqrt(jnp.mean(xf * xf, axis=-1, keepdims=True) + RMS_EPS)
    return (xf * g.astype(F32)).astype(x.dtype)


def ada_mod(cvec, w, b):
    return jnp.split(jax.nn.silu(cvec) @ w + b, 6, axis=-1)


def modulate(h, shift, scale):
    return h * (1 + scale) + shift


def cols_of(p, start, width, col0=0):
    return p[..., start - col0:start - col0 + width]


def rotate_pairs(x, ang):
    h = x.shape[-1] // 2
    x1, x2 = x[..., :h], x[..., h:]
    cos, sin = jnp.cos(ang), jnp.sin(ang)
    return jnp.concatenate([x1 * cos - x2 * sin, x1 * sin + x2 * cos], axis=-1)


def axial_rope(x, rows, cols):
    half = x.shape[-1] // 2
    n_freq = half // 2
    freqs = ROPE_BASE ** (-jnp.arange(n_freq, dtype=F32) / n_freq)
    ang_r = (rows.astype(F32)[:, None] * freqs)[None, :, None, None, :]
    ang_c = (cols.astype(F32)[:, None] * freqs)[None, :, None, None, :]
    xf = x.astype(F32)
    out = jnp.concatenate([rotate_pairs(xf[..., :half], ang_r),
                           rotate_pairs(xf[..., half:], ang_c)], axis=-1)
    return out.astype(x.dtype)


def qk_heads(u, g):
    return rmsnorm(u.reshape(u.shape[:-1] + (DA_HEADS, 2, DA_HEAD_DIM)), g)


def v_heads(u):
    return u.reshape(u.shape[:-1] + (DA_HEADS, DA_V_DIM))


def diff_attention(q, k, v, lam, subln_g, lam_init):
    b, sq, h, _, d = q.shape
    nb = sq // Q_BLOCK
    qb = jnp.moveaxis(q.reshape(b, nb, Q_BLOCK, h, 2, d), 1, 0)
    scale = d ** -0.5

    def one_block(q_blk):
        sc = jnp.einsum('bqhcd,bkhcd->bhcqk', q_blk, k, preferred_element_type=F32) * scale
        p = jax.nn.softmax(sc, axis=-1)
        w = (p[:, :, 0] - lam * p[:, :, 1]).astype(v.dtype)
        return jnp.einsum('bhqk,bkhe->bqhe', w, v)

    o = jnp.moveaxis(lax.map(one_block, qb), 0, 1).reshape(b, sq, h, v.shape[-1])
    o = rmsnorm(o, subln_g) * (1.0 - lam_init)
    return o.reshape(b, sq, h * v.shape[-1])


def dwconv_silu(u, w, bias):
    pad = (SSM_CONV - 1) // 2
    y = lax.conv_general_dilated(u, w[:, None, :].astype(u.dtype), window_strides=(1,),
                                 padding=[(pad, pad)], dimension_numbers=('NWC', 'WIO', 'NWC'),
                                 feature_group_count=u.shape[-1])
    return jax.nn.silu(y + bias)


def ssm_xb_dt(pr, col0, conv_w, conv_b, dt_bias):
    xb = dwconv_silu(cols_of(pr, COL_XB, SSM_XB_W, col0), conv_w, conv_b)
    bsz, l, _ = xb.shape
    xs = xb[..., :SSM_D_INNER].reshape(bsz, l, SSM_GROUPS, SSM_HPG, SSM_HEAD_DIM)
    bm = xb[..., SSM_D_INNER:].reshape(bsz, l, SSM_GROUPS, SSM_STATE)
    dt_raw = cols_of(pr, COL_DT, 2 * SSM_HEADS, col0).astype(F32).reshape(bsz, l, 2, SSM_HEADS)
    dt = jax.nn.softplus(dt_raw + dt_bias.astype(F32)).reshape(bsz, l, 2, SSM_GROUPS, SSM_HPG)
    return xs, bm, dt


def ssm_c(pr, col0, conv_w, conv_b):
    cm = dwconv_silu(cols_of(pr, COL_C, SSM_BC_W, col0), conv_w, conv_b)
    return cm.reshape(cm.shape[:2] + (SSM_GROUPS, SSM_STATE))


def ssd_final_state(xs, dt, a, bm):
    acs = jnp.cumsum(dt * a, axis=1)
    wx = (jnp.exp(acs[:, -1:] - acs) * dt)[..., None] * xs.astype(F32)
    return jnp.einsum('blgn,blgjp->bgjpn', bm.astype(F32), wx)


def ssd_chunked(xs, dt, a, bm, cm, h0):
    b, l, g, j, p = xs.shape
    n = bm.shape[-1]
    q = SSM_CHUNK
    nc = l // q
    xdt = (xs.astype(F32) * dt[..., None]).reshape(b, nc, q, g, j, p)
    bc = bm.astype(F32).reshape(b, nc, q, g, n)
    cc = cm.astype(F32).reshape(b, nc, q, g, n)
    acs = jnp.cumsum((dt * a).reshape(b, nc, q, g, j), axis=2)
    lower = jnp.tril(jnp.ones((q, q), bool))[None, None, :, :, None, None]
    seg = acs[:, :, :, None] - acs[:, :, None, :]
    decay = jnp.exp(jnp.where(lower, seg, -jnp.inf))
    cb = jnp.einsum('bclgn,bcsgn->bclsg', cc, bc)
    y_diag = jnp.einsum('bclsgj,bcsgjp->bclgjp', cb[..., None] * decay, xdt)
    wx = jnp.exp(acs[:, :, -1:] - acs)[..., None] * xdt
    states = jnp.einsum('bcsgn,bcsgjp->bcgjpn', bc, wx)
    chunk_decay = jnp.exp(acs[:, :, -1])

    def step(h, inp):
        dec, st = inp
        return dec[..., None, None] * h + st, h

    _, h_in = lax.scan(step, h0.astype(F32),
                       (jnp.moveaxis(chunk_decay, 1, 0), jnp.moveaxis(states, 1, 0)))
    h_in = jnp.moveaxis(h_in, 0, 1)
    y_off = jnp.einsum('bclgn,bcgjpn->bclgjp', cc, h_in) * jnp.exp(acs)[..., None]
    return (y_diag + y_off).reshape(b, l, g, j, p).astype(xs.dtype)


def ssd_bidir(xs, dt, a, bm, cm, h_f, h_b, d_skip, z, norm_g):
    b, l = xs.shape[:2]
    y_f = ssd_chunked(xs, dt[:, :, 0], a[0], bm, cm, h_f)
    y_b = ssd_chunked(xs[:, ::-1], dt[:, ::-1, 1], a[1], bm[:, ::-1], cm[:, ::-1], h_b)[:, ::-1]
    y = y_f + y_b + xs * d_skip.reshape(SSM_GROUPS, SSM_HPG)[..., None].astype(xs.dtype)
    y = y.reshape(b, l, SSM_D_INNER)
    return rmsnorm(y * jax.nn.silu(z), norm_g)


def even_mixer(hx, hc, w_in, w_out, q_norm_g, k_norm_g, da_lambda, da_subln_g,
               conv_xb_w, conv_xb_b, conv_c_w, conv_c_b, dt_bias, a_log, d_skip, ssm_norm_g,
               rows, cols, lam_init, ctx_out):
    px = hx @ w_in
    col0 = 0 if ctx_out else COL_K
    pc = hc @ w_in[:, col0:]
    lam = (jnp.exp(jnp.sum(da_lambda[0] * da_lambda[1])) -
           jnp.exp(jnp.sum(da_lambda[2] * da_lambda[3]))).astype(F32) + lam_init
    a = -jnp.exp(a_log.astype(F32)).reshape(2, SSM_GROUPS, SSM_HPG)

    k_c = qk_heads(cols_of(pc, COL_K, DA_QK_W, col0), k_norm_g)
    v_c = v_heads(cols_of(pc, COL_V, DA_V_W, col0))
    xs_c, bm_c, dt_c = ssm_xb_dt(pc, col0, conv_xb_w, conv_xb_b, dt_bias)
    h_f = ssd_final_state(xs_c, dt_c[:, :, 0], a[0], bm_c)
    h_b = ssd_final_state(xs_c[:, ::-1], dt_c[:, ::-1, 1], a[1], bm_c[:, ::-1])

    q_x = axial_rope(qk_heads(cols_of(px, COL_Q, DA_QK_W), q_norm_g), rows, cols)
    k_x = axial_rope(qk_heads(cols_of(px, COL_K, DA_QK_W), k_norm_g), rows, cols)
    v_x = v_heads(cols_of(px, COL_V, DA_V_W))
    o_att = diff_attention(q_x, jnp.concatenate([k_x, k_c], axis=1),
                           jnp.concatenate([v_x, v_c], axis=1), lam, da_subln_g, lam_init)
    xs, bm, dt = ssm_xb_dt(px, 0, conv_xb_w, conv_xb_b, dt_bias)
    cm = ssm_c(px, 0, conv_c_w, conv_c_b)
    y = ssd_bidir(xs, dt, a, bm, cm, h_f, h_b, d_skip, cols_of(px, COL_Z, SSM_D_INNER), ssm_norm_g)
    out_x = jnp.concatenate([o_att, y], axis=-1) @ w_out
    if not ctx_out:
        return out_x, None

    q_c = qk_heads(cols_of(pc, COL_Q, DA_QK_W), q_norm_g)
    o_att_c = diff_attention(q_c, k_c, v_c, lam, da_subln_g, lam_init)
    cm_c = ssm_c(pc, 0, conv_c_w, conv_c_b)
    zero = jnp.zeros_like(h_f)
    y_c = ssd_bidir(xs_c, dt_c, a, bm_c, cm_c, zero, zero, d_skip,
                    cols_of(pc, COL_Z, SSM_D_INNER), ssm_norm_g)
    out_c = jnp.concatenate([o_att_c, y_c], axis=-1) @ w_out
    return out_x, out_c


def fourier_mix(h, w):
    b, l, d = h.shape
    hg = h.astype(F32).reshape(b, l, FOURIER_GROUPS, FOURIER_GW)
    f = jnp.fft.fft2(hg, axes=(1, 3), norm='ortho').real
    return f.reshape(b, l, d).astype(h.dtype) @ w


def moe(xs, w_router, b_router, w_gu, b_gu, w_dn, b_dn):
    t, d = xs.shape
    logits = (xs @ w_router + b_router).astype(F32)
    top_v, top_i = lax.top_k(logits, TOP_K)
    gates = jax.nn.softmax(top_v, axis=-1)
    n_asg = t * TOP_K
    e_flat = top_i.reshape(-1).astype(jnp.int32)
    order = jnp.argsort(e_flat)
    e_s = e_flat[order]
    tok_s = (jnp.arange(n_asg, dtype=jnp.int32) // TOP_K)[order]
    g_s = gates.reshape(-1)[order]
    counts = jnp.bincount(e_flat, length=N_EXPERTS)
    padded = (counts + MOE_BLOCK - 1) // MOE_BLOCK * MOE_BLOCK
    pend = jnp.cumsum(padded)
    pstart = pend - padded
    ustart = jnp.cumsum(counts) - counts
    dest = pstart[e_s] + jnp.arange(n_asg, dtype=jnp.int32) - ustart[e_s]
    n_rows = -(-(n_asg + N_EXPERTS * (MOE_BLOCK - 1)) // MOE_BLOCK) * MOE_BLOCK
    n_blocks = n_rows // MOE_BLOCK
    row_tok = jnp.full((n_rows,), t, jnp.int32).at[dest].set(tok_s)
    row_gate = jnp.zeros((n_rows,), F32).at[dest].set(g_s)
    blk_e = jnp.minimum(jnp.searchsorted(pend, jnp.arange(n_blocks, dtype=jnp.int32) * MOE_BLOCK,
                                         side='right'), N_EXPERTS - 1)
    xs_pad = jnp.concatenate([xs, jnp.zeros((1, d), xs.dtype)], axis=0)
    xb = xs_pad[row_tok].reshape(n_blocks, MOE_BLOCK, d)

    def expert_block(args):
        x_blk, e = args
        gu = x_blk @ w_gu[e] + b_gu[e]
        glu = jnp.minimum(gu[..., ::2], SWIGLU_LIMIT)
        lin = jnp.clip(gu[..., 1::2], -SWIGLU_LIMIT, SWIGLU_LIMIT)
        act = glu * jax.nn.sigmoid(SWIGLU_ALPHA * glu) * (lin + 1)
        return act @ w_dn[e] + b_dn[e]

    yb = lax.map(expert_block, (xb, blk_e)).reshape(n_rows, d)
    yb = yb * row_gate[:, None].astype(yb.dtype)
    return jnp.zeros((t + 1, d), yb.dtype).at[row_tok].add(yb)[:t]


def setup_inputs(seed: int = 0) -> dict:
    key = jax.random.key(seed)
    ks = jax.random.split(key, 28)
    nrm = lambda k, shp, s: jax.random.normal(k, shp, F32) * s
    D = D_MODEL
    dt0 = jnp.exp(jax.random.uniform(ks[17], (N_EVEN, 2, SSM_HEADS), F32,
                                     math.log(1e-3), math.log(1e-1)))
    return {
        'x': nrm(ks[0], (BATCH, SEQ, D), 1.0),
        'c': nrm(ks[1], (BATCH, D), 1.0),
        'ctx': nrm(ks[2], (BATCH, CTX_LEN, D), 1.0),
        'c_ctx': nrm(ks[3], (D,), 1.0),
        'ada_w': nrm(ks[4], (DEPTH, D, 6 * D), 0.5 * D ** -0.5),
        'ada_b': nrm(ks[5], (DEPTH, 6 * D), 0.02),
        'norm_g': 1.0 + nrm(ks[6], (DEPTH, 2, D), 0.02),
        'w_in': nrm(ks[7], (N_EVEN, D, IN_W), D ** -0.5),
        'w_out': nrm(ks[8], (N_EVEN, MIX_W, D), MIX_W ** -0.5),
        'q_norm_g': 1.0 + nrm(ks[9], (N_EVEN, DA_HEAD_DIM), 0.02),
        'k_norm_g': 1.0 + nrm(ks[10], (N_EVEN, DA_HEAD_DIM), 0.02),
        'da_lambda': nrm(ks[11], (N_EVEN, 4, DA_HEAD_DIM), 0.1),
        'da_subln_g': 1.0 + nrm(ks[12], (N_EVEN, DA_V_DIM), 0.02),
        'conv_xb_w': nrm(ks[13], (N_EVEN, SSM_CONV, SSM_XB_W), SSM_CONV ** -0.5),
        'conv_xb_b': nrm(ks[14], (N_EVEN, SSM_XB_W), 0.02),
        'conv_c_w': nrm(ks[15], (N_EVEN, SSM_CONV, SSM_BC_W), SSM_CONV ** -0.5),
        'conv_c_b': nrm(ks[16], (N_EVEN, SSM_BC_W), 0.02),
        'dt_bias': dt0 + jnp.log(-jnp.expm1(-dt0)),
        'a_log': jnp.log(jax.random.uniform(ks[18], (N_EVEN, 2, SSM_HEADS), F32, 1.0, 16.0)),
        'd_skip': 1.0 + nrm(ks[19], (N_EVEN, SSM_HEADS), 0.1),
        'ssm_norm_g': 1.0 + nrm(ks[20], (N_EVEN, SSM_D_INNER), 0.02),
        'w_fourier': nrm(ks[21], (N_ODD, D, D), D ** -0.5),
        'w_router': nrm(ks[22], (DEPTH, D, N_EXPERTS), D ** -0.5),
        'b_router': nrm(ks[23], (DEPTH, N_EXPERTS), 0.01),
        'w_gate_up': nrm(ks[24], (DEPTH, N_EXPERTS, D, 2 * D_EXPERT), D ** -0.5),
        'b_gate_up': nrm(ks[25], (DEPTH, N_EXPERTS, 2 * D_EXPERT), 0.02),
        'w_down': nrm(ks[26], (DEPTH, N_EXPERTS, D_EXPERT, D), D_EXPERT ** -0.5),
        'b_down': nrm(ks[27], (DEPTH, N_EXPERTS, D), 0.02),
    }


def reference(x, c, ctx, c_ctx, ada_w, ada_b, norm_g, w_in, w_out, q_norm_g, k_norm_g,
              da_lambda, da_subln_g, conv_xb_w, conv_xb_b, conv_c_w, conv_c_b, dt_bias,
              a_log, d_skip, ssm_norm_g, w_fourier, w_router, b_router, w_gate_up,
              b_gate_up, w_down, b_down):
    b, s, d = x.shape
    n_grid_rows = s // GRID_W
    rows = jnp.repeat(jnp.arange(n_grid_rows, dtype=jnp.int32), GRID_W)
    cols = jnp.arange(n_grid_rows * GRID_W, dtype=jnp.int32) % GRID_W
    for i in range(DEPTH):
        even = i % 2 == 0
        ctx_later = any(j % 2 == 0 for j in range(i + 1, DEPTH))
        sx1, cx1, gx1, sx2, cx2, gx2 = [m[:, None, :] for m in ada_mod(c, ada_w[i], ada_b[i])]
        if even or ctx_later:
            sc1, cc1, gc1, sc2, cc2, gc2 = ada_mod(c_ctx, ada_w[i], ada_b[i])
            hc = modulate(rmsnorm(ctx, norm_g[i, 0]), sc1, cc1)
        hx = modulate(rmsnorm(x, norm_g[i, 0]), sx1, cx1)
        if even:
            e = i // 2
            lam_init = 0.8 - 0.6 * math.exp(-0.3 * i)
            ox, oc = even_mixer(hx, hc, w_in[e], w_out[e], q_norm_g[e], k_norm_g[e],
                                da_lambda[e], da_subln_g[e], conv_xb_w[e], conv_xb_b[e],
                                conv_c_w[e], conv_c_b[e], dt_bias[e], a_log[e], d_skip[e],
                                ssm_norm_g[e], rows, cols, lam_init, ctx_later)
        else:
            o = i // 2
            ox = fourier_mix(hx, w_fourier[o])
            oc = fourier_mix(hc, w_fourier[o]) if ctx_later else None
        x = x + gx1 * ox
        hx2 = modulate(rmsnorm(x, norm_g[i, 1]), sx2, cx2)
        moe_args = (w_router[i], b_router[i], w_gate_up[i], b_gate_up[i], w_down[i], b_down[i])
        if ctx_later:
            ctx = ctx + gc1 * oc
            hc2 = modulate(rmsnorm(ctx, norm_g[i, 1]), sc2, cc2)
            f = moe(jnp.concatenate([hx2.reshape(-1, d), hc2.reshape(-1, d)], axis=0), *moe_args)
            x = x + gx2 * f[:b * s].reshape(b, s, d)
            ctx = ctx + gc2 * f[b * s:].reshape(ctx.shape)
        else:
            x = x + gx2 * moe(hx2.reshape(-1, d), *moe_args).reshape(b, s, d)
    return x
```

```python
import math
from contextlib import ExitStack

import numpy as np
import concourse.bass as bass
import concourse.mybir as mybir
from concourse.bass_utils import run_bass_kernel_spmd

F32 = mybir.dt.float32
BF16 = mybir.dt.bfloat16
AF = mybir.ActivationFunctionType
ALU = mybir.AluOpType
AX = mybir.AxisListType
NCORES = 8


class Dep:
    __slots__ = ("w", "r")

    def __init__(self):
        self.w = None
        self.r = {}


class KB:
    DMA_RING = 8

    def __init__(self, nc, es):
        self.nc = nc
        self.es = es
        self.root_es = es
        self.E = {"pe": nc.tensor, "act": nc.scalar, "dve": nc.vector, "pool": nc.gpsimd, "sp": nc.sync}
        self.sem = {}
        for e in ("pe", "act", "dve", "pool"):
            self.sem[e] = es.enter_context(nc.semaphore(e))
        self.cnt = {e: 0 for e in ("pe", "act", "dve", "pool")}
        self.seen = {e: {} for e in self.E}
        self.dmaq = {}
        self.ntile = 0

    def sb(self, shape, dt, name=None):
        self.ntile += 1
        return self.es.enter_context(self.nc.sbuf_tensor(name or f"t{self.ntile}", list(shape), dt))

    def ps(self, shape, dt=F32, name=None):
        self.ntile += 1
        return self.es.enter_context(self.nc.psum_tensor(name or f"p{self.ntile}", list(shape), dt))

    def _wait(self, e, key, val):
        if self.seen[e].get(key, 0) >= val:
            return
        self.E[e].wait_ge(self.sem[key], val)
        self.seen[e][key] = val

    def _deps(self, e, reads, writes):
        need = {}
        for d in reads:
            if d.w:
                k, v = d.w
                need[k] = max(need.get(k, 0), v)
        for d in writes:
            if d.w:
                k, v = d.w
                need[k] = max(need.get(k, 0), v)
            for k, v in d.r.items():
                need[k] = max(need.get(k, 0), v)
        for k, v in need.items():
            if k == "pe" and e == "pe":
                continue
            self._wait(e, k, v)

    def op(self, e, fn, reads=(), writes=()):
        self._deps(e, reads, writes)
        inst = fn(self.E[e])
        self.cnt[e] += 1
        v = self.cnt[e]
        inst.then_inc(self.sem[e], 1)
        for d in reads:
            d.r[e] = v
        for d in writes:
            d.w = (e, v)
            d.r = {}
        return inst

    def dma(self, q, out, in_, reads=(), writes=(), **kw):
        ring = self.dmaq.setdefault(q, {"n": 0, "keys": []})
        n = ring["n"]
        slot = n % self.DMA_RING
        if len(ring["keys"]) <= slot:
            key = f"dma_{q}_{slot}"
            self.sem[key] = self.root_es.enter_context(self.nc.semaphore(key))
            ring["keys"].append(key)
        key = ring["keys"][slot]
        val = 16 * (n // self.DMA_RING + 1)
        if n >= self.DMA_RING:
            self._wait(q, key, val - 16)
        self._deps(q, reads, writes)
        inst = self.E[q].dma_start(out=out, in_=in_, **kw)
        inst.then_inc(self.sem[key], 16)
        ring["n"] += 1
        for d in reads:
            d.r[key] = val
        for d in writes:
            d.w = (key, val)
            d.r = {}
        return inst

    def coll(self, kind, op, ins, outs, reads=(), writes=()):
        q = "pool"
        ring = self.dmaq.setdefault("coll", {"n": 0, "keys": []})
        if not ring["keys"]:
            self.sem["coll"] = self.root_es.enter_context(self.nc.semaphore("coll"))
            ring["keys"].append("coll")
        n = ring["n"]
        val = 16 * (n + 1)
        if n >= 1:
            self._wait(q, "coll", val - 16)
        self._deps(q, reads, writes)
        inst = self.nc.gpsimd.collective_compute(kind, op, replica_groups=[list(range(NCORES))], ins=ins, outs=outs)
        inst.then_inc(self.sem["coll"], 16)
        ring["n"] += 1
        for d in reads:
            d.r["coll"] = val
        for d in writes:
            d.w = ("coll", val)
            d.r = {}
        return inst

    def finish(self, deps):
        for d in deps:
            if d.w:
                self._wait("sp", d.w[0], d.w[1])
        for q, ring in self.dmaq.items():
            n = ring["n"]
            for slot, key in enumerate(ring["keys"]):
                uses = n if q == "coll" else (n - slot + self.DMA_RING - 1) // self.DMA_RING
                if uses > 0:
                    self._wait("sp", key, 16 * uses)


def new_nc():
    return bass.Bass("TRN2", target_bir_lowering=False)


def run_spmd(nc, in_maps):
    import time, sys
    t0 = time.time()
    res = run_bass_kernel_spmd(nc, in_maps, core_ids=list(range(NCORES)))
    print(f'[launch] {time.time() - t0:.1f}s', file=sys.stderr, flush=True)
    return res.results


def build_l0():
    nc = new_nc()
    adaw = nc.dram_tensor("adaw", [2, 1024, 768], F32, kind="ExternalInput").ap()
    adab = nc.dram_tensor("adab", [128, 12], F32, kind="ExternalInput").ap()
    cT = nc.dram_tensor("cT", [1024, 3], F32, kind="ExternalInput").ap()
    out = nc.dram_tensor("modp", [128, 36], F32, kind="ExternalOutput").ap()
    with ExitStack() as es:
        k = KB(nc, es)
        w_sb = k.sb([128, 2, 8, 768], F32)
        b_sb = k.sb([128, 12], F32)
        c_sb = k.sb([128, 8, 3], F32)
        sc_sb = k.sb([128, 8, 3], F32)
        o_sb = k.sb([128, 12, 3], F32)
        pp = k.ps([128, 512], F32)
        dw = [Dep(), Dep()]
        db, dc, dsc, dps, do = Dep(), Dep(), Dep(), Dep(), Dep()
        for l in range(2):
            k.dma("sp", w_sb[:, l], adaw[l].rearrange("(k p) m -> p k m", p=128), writes=[dw[l]])
        k.dma("sp", b_sb[:], adab, writes=[db])
        k.dma("sp", c_sb[:], cT.rearrange("(k p) v -> p k v", p=128), writes=[dc])
        k.op("act", lambda e: e.activation(out=sc_sb[:], in_=c_sb[:], func=AF.Silu), reads=[dc], writes=[dsc])
        for l in range(2):
            for mc in range(6):
                for kc in range(8):
                    k.op("pe", lambda e: e.matmul(pp[:, (l * 6 + mc) * 3:(l * 6 + mc) * 3 + 3],
                                                  lhsT=w_sb[:, l, kc, mc * 128:(mc + 1) * 128],
                                                  rhs=sc_sb[:, kc, :], start=(kc == 0), stop=(kc == 7)),
                         reads=[dw[l], dsc], writes=[dps])
        for v in range(3):
            k.op("dve", lambda e: e.tensor_tensor(out=o_sb[:, :, v], in0=pp[:, 0:36].rearrange("p (a v) -> p a v", v=3)[:, :, v],
                                                  in1=b_sb[:], op=ALU.add), reads=[dps, db], writes=[do])
        k.dma("sp", out, o_sb[:].rearrange("p a v -> p (a v)"), reads=[do], writes=[Dep()])
        k.finish([])
    return nc


def run_l0(inp):
    ada_w, ada_b = inp["ada_w"], inp["ada_b"]
    cT = np.ascontiguousarray(np.concatenate([inp["c"], inp["c_ctx"][None]], 0).T)
    in_maps = []
    for j in range(NCORES):
        sl = slice(768 * j, 768 * (j + 1))
        adab = ada_b[:, sl].reshape(2, 6, 128).transpose(2, 0, 1).reshape(128, 12)
        in_maps.append({"adaw": np.ascontiguousarray(ada_w[:, :, sl]), "adab": np.ascontiguousarray(adab), "cT": cT})
    res = run_spmd(build_l0(), in_maps)
    mod = np.zeros((2, 3, 6144), np.float32)
    for j in range(NCORES):
        o = res[j]["modp"].reshape(128, 2, 6, 3)
        mod[:, :, 768 * j:768 * (j + 1)] = o.transpose(1, 3, 2, 0).reshape(2, 3, 768)
    return mod


RMS_EPS = 1e-6


def build_t1(front, nA=0, nB=0, gate_row=2, norm=None, h_layout="T", router=False, out_x=True, T=2048):
    nc = new_nc()
    NT = T // 128
    x = nc.dram_tensor("x", [T, 1024], F32, kind="ExternalInput").ap()
    mod = nc.dram_tensor("mod", [6, 1024], F32, kind="ExternalInput").ap()
    ident_d = nc.dram_tensor("ident", [128, 128], F32, kind="ExternalInput").ap()
    if front == "mix":
        mixA = nc.dram_tensor("mixA", [nA * 128, T], BF16, kind="ExternalInput").ap()
        wA = nc.dram_tensor("wA", [nA * 128, 1024], F32, kind="ExternalInput").ap()
        if nB:
            mixB = nc.dram_tensor("mixB", [nB * 128, T], BF16, kind="ExternalInput").ap()
            wB = nc.dram_tensor("wB", [nB * 128, 1024], F32, kind="ExternalInput").ap()
            ssd = nc.dram_tensor("ss", [T, 8], F32, kind="ExternalInput").ap()
    elif front == "parts":
        parts = nc.dram_tensor("parts", [8, T, 1024], BF16, kind="ExternalInput").ap()
    if norm is not None:
        ng = nc.dram_tensor("ng", [1024], F32, kind="ExternalInput").ap()
        if h_layout == "T":
            hout = nc.dram_tensor("hT", [1024, T], BF16, kind="ExternalOutput").ap()
        else:
            hout = nc.dram_tensor("hN", [T, 1024], BF16, kind="ExternalOutput").ap()
    if router:
        wr = nc.dram_tensor("wr", [1024, 32], F32, kind="ExternalInput").ap()
        br = nc.dram_tensor("br", [32], F32, kind="ExternalInput").ap()
        gout = nc.dram_tensor("gatesT", [32, T], F32, kind="ExternalOutput").ap()
    if out_x:
        xout = nc.dram_tensor("xo", [T, 1024], F32, kind="ExternalOutput").ap()

    with ExitStack() as es:
        k = KB(nc, es)
        outd = []
        ident = k.sb([128, 128], F32)
        d_id = Dep()
        k.dma("sp", ident[:], ident_d, writes=[d_id])
        g_bc = k.sb([128, 1024], F32)
        d_g = Dep()
        if front is not None:
            k.dma("sp", g_bc[:], mod[gate_row].partition_broadcast(128), writes=[d_g])
        if norm is not None:
            A_bc = k.sb([128, 1024], F32)
            B_bc = k.sb([128, 1024], F32)
            ng_bc = k.sb([128, 1024], F32)
            d_A, d_B, d_ng = Dep(), Dep(), Dep()
            k.dma("sp", A_bc[:], mod[norm[1]].partition_broadcast(128), writes=[d_A])
            k.dma("sp", B_bc[:], mod[norm[0]].partition_broadcast(128), writes=[d_B])
            k.dma("sp", ng_bc[:], ng.partition_broadcast(128), writes=[d_ng])
            k.op("dve", lambda e: e.scalar_tensor_tensor(out=A_bc[:], in0=A_bc[:], scalar=1.0, in1=ng_bc[:],
                                                         op0=ALU.add, op1=ALU.mult), reads=[d_ng], writes=[d_A])
        if router:
            wr_sb = k.sb([128, 8, 32], F32)
            br_bc = k.sb([128, 32], F32)
            d_wr, d_br = Dep(), Dep()
            k.dma("sp", wr_sb[:], wr.rearrange("(k p) e -> p k e", p=128), writes=[d_wr])
            k.dma("sp", br_bc[:], br.partition_broadcast(128), writes=[d_br])
            wr_hi = k.sb([128, 8, 32], BF16)
            wr_lo = k.sb([128, 8, 32], BF16)
            k.op("dve", lambda e: e.tensor_copy(out=wr_hi[:], in_=wr_sb[:]), reads=[d_wr], writes=[d_wr])
            k.op("dve", lambda e: e.tensor_tensor(out=wr_lo[:], in0=wr_sb[:], in1=wr_hi[:], op=ALU.subtract), reads=[d_wr], writes=[d_wr])
        if front == "mix":
            stg = [k.sb([128, 1024], F32) for _ in range(2)]
            d_stg = [Dep(), Dep()]
            wA_bf = k.sb([128, nA, 1024], BF16)
            d_wA = Dep()
            n = 0
            for c in range(nA):
                k.dma("sp", stg[n % 2][:], wA[c * 128:(c + 1) * 128, :], writes=[d_stg[n % 2]])
                k.op("act", lambda e: e.activation(out=wA_bf[:, c, :], in_=stg[n % 2][:], func=AF.Copy),
                     reads=[d_stg[n % 2]], writes=[d_wA])
                n += 1
            if nB:
                wB_bf = k.sb([128, nB, 1024], BF16)
                d_wB = Dep()
                for c in range(nB):
                    k.dma("sp", stg[n % 2][:], wB[c * 128:(c + 1) * 128, :], writes=[d_stg[n % 2]])
                    k.op("act", lambda e: e.activation(out=wB_bf[:, c, :], in_=stg[n % 2][:], func=AF.Copy),
                         reads=[d_stg[n % 2]], writes=[d_wB])
                    n += 1
        NB_ = 2
        xt = [k.sb([128, 1024], F32) for _ in range(NB_)]
        d_x = [Dep() for _ in range(NB_)]
        tmp = [k.sb([128, 1024], F32) for _ in range(NB_)]
        d_tmp = [Dep() for _ in range(NB_)]
        if front == "mix":
            mA = [k.sb([128, nA, 128], BF16) for _ in range(NB_)]
            d_mA = [Dep() for _ in range(NB_)]
            if nB:
                mB = [k.sb([128, nB, 128], BF16) for _ in range(NB_)]
                d_mB = [Dep() for _ in range(NB_)]
                sst = [k.sb([128, 8], F32) for _ in range(NB_)]
                d_ss = [Dep() for _ in range(NB_)]
                rs = [k.sb([128, 4], F32) for _ in range(NB_)]
                d_rs = [Dep() for _ in range(NB_)]
                asb = [k.sb([128, 512], F32) for _ in range(NB_)]
                d_asb = [Dep() for _ in range(NB_)]
        if front == "parts":
            pt = [k.sb([128, 8, 1024], BF16) for _ in range(NB_)]
            d_pt = [Dep() for _ in range(NB_)]
        if norm is not None:
            junk = [k.sb([128, 1024], BF16) for _ in range(NB_)]
            d_junk = [Dep() for _ in range(NB_)]
            st = [k.sb([128, 4], F32) for _ in range(NB_)]
            d_st = [Dep() for _ in range(NB_)]
            hf = [k.sb([128, 1024], F32) for _ in range(NB_)]
            d_hf = [Dep() for _ in range(NB_)]
            hb = [k.sb([128, 1024], BF16) for _ in range(NB_)]
            d_hb = [Dep() for _ in range(NB_)]
        if router:
            hTf = [k.sb([128, 8, 128], BF16) for _ in range(NB_)]
            d_hTf = [Dep() for _ in range(NB_)]
            lg = [k.sb([128, 32], F32) for _ in range(NB_)]
            d_lg = [Dep() for _ in range(NB_)]
            t8 = [k.sb([128, 8], F32) for _ in range(NB_)]
            d_t8 = [Dep() for _ in range(NB_)]
            ex = [k.sb([128, 32], F32) for _ in range(NB_)]
            d_ex = [Dep() for _ in range(NB_)]
            gt = [k.sb([128, 32], F32) for _ in range(NB_)]
            d_gt = [Dep() for _ in range(NB_)]
            gT = [k.sb([32, 128], F32) for _ in range(NB_)]
            d_gT = [Dep() for _ in range(NB_)]
        pa = [k.ps([128, 512]) for _ in range(2)]
        d_pa = [Dep(), Dep()]
        pb = [k.ps([128, 512]) for _ in range(2)]
        d_pb = [Dep(), Dep()]
        ptr = k.ps([128, 1024])
        d_ptr = Dep()
        plg = k.ps([128, 512])
        d_plg = Dep()

        for t in range(NT):
            b = t % NB_
            tok = slice(t * 128, (t + 1) * 128)
            k.dma("sp", xt[b][:], x[tok, :], writes=[d_x[b]])
            xn, d_xn = xt[b], d_x[b]
            if front == "mix":
                k.dma("sp", mA[b][:], mixA[:, tok].rearrange("(c p) t -> p c t", p=128), writes=[d_mA[b]])
                if nB:
                    k.dma("sp", mB[b][:], mixB[:, tok].rearrange("(c p) t -> p c t", p=128), writes=[d_mB[b]])
                    k.dma("sp", sst[b][:], ssd[tok, :], writes=[d_ss[b]])
                    k.op("dve", lambda e: e.tensor_reduce(out=rs[b][:, 0:1], in_=sst[b][:], axis=AX.X, op=ALU.add),
                         reads=[d_ss[b]], writes=[d_rs[b]])
                    k.op("act", lambda e: e.activation(out=rs[b][:, 1:2], in_=rs[b][:, 0:1], func=AF.Sqrt,
                                                       scale=1.0 / 1024, bias=eps_ap(k)), reads=[d_rs[b]], writes=[d_rs[b]])
                    k.op("dve", lambda e: e.reciprocal(out=rs[b][:, 2:3], in_=rs[b][:, 1:2]), reads=[d_rs[b]], writes=[d_rs[b]])
                for h in range(2):
                    cs = slice(h * 512, (h + 1) * 512)
                    for c in range(nA):
                        k.op("pe", lambda e: e.matmul(pa[h][:], lhsT=mA[b][:, c, :], rhs=wA_bf[:, c, cs],
                                                      start=(c == 0), stop=(c == nA - 1)),
                             reads=[d_mA[b], d_wA], writes=[d_pa[h]])
                    if nB:
                        for c in range(nB):
                            k.op("pe", lambda e: e.matmul(pb[h][:], lhsT=mB[b][:, c, :], rhs=wB_bf[:, c, cs],
                                                          start=(c == 0), stop=(c == nB - 1)),
                                 reads=[d_mB[b], d_wB], writes=[d_pb[h]])
                        k.op("act", lambda e: e.activation(out=asb[b][:], in_=pa[h][:], func=AF.Copy),
                             reads=[d_pa[h]], writes=[d_asb[b]])
                        k.op("dve", lambda e: e.scalar_tensor_tensor(out=tmp[b][:, cs], in0=pb[h][:], scalar=rs[b][:, 2:3],
                                                                     in1=asb[b][:], op0=ALU.mult, op1=ALU.add),
                             reads=[d_pb[h], d_rs[b], d_asb[b]], writes=[d_tmp[b]])
                        k.op("dve", lambda e: e.tensor_tensor(out=tmp[b][:, cs], in0=tmp[b][:, cs], in1=g_bc[:, cs], op=ALU.mult),
                             reads=[d_g], writes=[d_tmp[b]])
                    else:
                        k.op("dve", lambda e: e.tensor_tensor(out=tmp[b][:, cs], in0=pa[h][:], in1=g_bc[:, cs], op=ALU.mult),
                             reads=[d_pa[h], d_g], writes=[d_tmp[b]])
                k.op("dve", lambda e: e.tensor_tensor(out=xt[b][:], in0=xt[b][:], in1=tmp[b][:], op=ALU.add),
                     reads=[d_tmp[b]], writes=[d_x[b]])
            elif front == "parts":
                k.dma("sp", pt[b][:], parts[:, tok, :].rearrange("j t d -> t j d"), writes=[d_pt[b]])
                k.op("dve", lambda e: e.tensor_tensor(out=tmp[b][:], in0=pt[b][:, 0, :], in1=pt[b][:, 1, :], op=ALU.add),
                     reads=[d_pt[b]], writes=[d_tmp[b]])
                for j in range(2, 8):
                    k.op("dve", lambda e: e.tensor_tensor(out=tmp[b][:], in0=tmp[b][:], in1=pt[b][:, j, :], op=ALU.add),
                         reads=[d_pt[b]], writes=[d_tmp[b]])
                k.op("dve", lambda e: e.tensor_tensor(out=tmp[b][:], in0=tmp[b][:], in1=g_bc[:], op=ALU.mult),
                     reads=[d_g], writes=[d_tmp[b]])
                k.op("dve", lambda e: e.tensor_tensor(out=xt[b][:], in0=xt[b][:], in1=tmp[b][:], op=ALU.add),
                     reads=[d_tmp[b]], writes=[d_x[b]])
            if out_x:
                do = Dep()
                k.dma("sp", xout[tok, :], xt[b][:], reads=[d_x[b]], writes=[do])
            if norm is not None:
                k.op("act", lambda e: e.activation(out=junk[b][:], in_=xt[b][:], func=AF.Square, accum_out=st[b][:, 0:1]),
                     reads=[d_x[b]], writes=[d_junk[b], d_st[b]])
                k.op("act", lambda e: e.activation(out=st[b][:, 1:2], in_=st[b][:, 0:1], func=AF.Sqrt,
                                                   scale=1.0 / 1024, bias=eps_ap(k)), reads=[d_st[b]], writes=[d_st[b]])
                k.op("dve", lambda e: e.reciprocal(out=st[b][:, 2:3], in_=st[b][:, 1:2]), reads=[d_st[b]], writes=[d_st[b]])
                k.op("dve", lambda e: e.scalar_tensor_tensor(out=hf[b][:], in0=xt[b][:], scalar=st[b][:, 2:3], in1=A_bc[:],
                                                             op0=ALU.mult, op1=ALU.mult),
                     reads=[d_x[b], d_st[b], d_A], writes=[d_hf[b]])
                k.op("dve", lambda e: e.tensor_tensor(out=hf[b][:], in0=hf[b][:], in1=B_bc[:], op=ALU.add),
                     reads=[d_B], writes=[d_hf[b]])
                if h_layout == "N":
                    k.op("act", lambda e: e.activation(out=hb[b][:], in_=hf[b][:], func=AF.Copy), reads=[d_hf[b]], writes=[d_hb[b]])
                    k.dma("sp", hout[tok, :], hb[b][:], reads=[d_hb[b]], writes=[Dep()])
                else:
                    for c in range(8):
                        k.op("pe", lambda e: e.transpose(out=ptr[:, c * 128:(c + 1) * 128], in_=hf[b][:, c * 128:(c + 1) * 128],
                                                         identity=ident[:]), reads=[d_hf[b], d_id], writes=[d_ptr])
                    k.op("dve", lambda e: e.tensor_copy(out=hb[b][:], in_=ptr[:]), reads=[d_ptr], writes=[d_hb[b]])
                    if router:
                        k.op("dve", lambda e: e.tensor_tensor(out=hTf[b][:].rearrange("p c t -> p (c t)"), in0=ptr[:], in1=hb[b][:], op=ALU.subtract),
                             reads=[d_ptr, d_hb[b]], writes=[d_hTf[b]])
                    k.dma("sp", hout[:, tok].rearrange("(c p) t -> p c t", p=128), hb[b][:].rearrange("p (c t) -> p c t", c=8),
                          reads=[d_hb[b]], writes=[Dep()])
                import os
                RL = int(os.environ.get("RL", "9"))
                if router and RL >= 1:
                    hbv = hb[b][:].rearrange("p (c t) -> p c t", c=8)
                    for c in range(8):
                        k.op("pe", lambda e: e.matmul(plg[:, 0:32], lhsT=hbv[:, c, :], rhs=wr_hi[:, c, :],
                                                      start=(c == 0), stop=False), reads=[d_hb[b], d_wr], writes=[d_plg])
                        k.op("pe", lambda e: e.matmul(plg[:, 0:32], lhsT=hbv[:, c, :], rhs=wr_lo[:, c, :],
                                                      start=False, stop=False), reads=[d_hb[b], d_wr], writes=[d_plg])
                        k.op("pe", lambda e: e.matmul(plg[:, 0:32], lhsT=hTf[b][:, c, :], rhs=wr_hi[:, c, :],
                                                      start=False, stop=(c == 7)), reads=[d_hTf[b], d_wr], writes=[d_plg])
                    k.op("dve", lambda e: e.tensor_tensor(out=lg[b][:], in0=plg[:, 0:32], in1=br_bc[:], op=ALU.add),
                         reads=[d_plg, d_br], writes=[d_lg[b]])
                if router and RL >= 2:
                    k.op("dve", lambda e: e.max(out=t8[b][:], in_=lg[b][:]), reads=[d_lg[b]], writes=[d_t8[b]])
                    k.op("dve", lambda e: e.tensor_scalar(out=ex[b][:], in0=lg[b][:], scalar1=t8[b][:, 0:1], scalar2=None,
                                                          op0=ALU.subtract), reads=[d_lg[b], d_t8[b]], writes=[d_ex[b]])
                    k.op("act", lambda e: e.activation(out=ex[b][:], in_=ex[b][:], func=AF.Exp), reads=[], writes=[d_ex[b]])
                    k.op("dve", lambda e: e.scalar_tensor_tensor(out=gt[b][:], in0=lg[b][:], scalar=t8[b][:, 3:4], in1=ex[b][:],
                                                                 op0=ALU.is_ge, op1=ALU.mult),
                         reads=[d_lg[b], d_t8[b], d_ex[b]], writes=[d_gt[b]])
                    k.op("dve", lambda e: e.tensor_reduce(out=t8[b][:, 4:5], in_=gt[b][:], axis=AX.X, op=ALU.add),
                         reads=[d_gt[b]], writes=[d_t8[b]])
                    k.op("dve", lambda e: e.reciprocal(out=t8[b][:, 5:6], in_=t8[b][:, 4:5]), reads=[], writes=[d_t8[b]])
                    k.op("dve", lambda e: e.tensor_scalar(out=gt[b][:], in0=gt[b][:], scalar1=t8[b][:, 5:6], scalar2=None,
                                                          op0=ALU.mult), reads=[d_t8[b]], writes=[d_gt[b]])
                if router and RL >= 3:
                    k.op("pe", lambda e: e.transpose(out=plg[0:32, 128:256], in_=gt[b][:], identity=ident[:]),
                         reads=[d_gt[b], d_id], writes=[d_plg])
                    k.op("dve", lambda e: e.tensor_copy(out=gT[b][:], in_=plg[0:32, 128:256]), reads=[d_plg], writes=[d_gT[b]])
                    k.dma("sp", gout[:, tok], gT[b][:], reads=[d_gT[b]], writes=[Dep()])
        k.finish([])
    return nc


def eps_ap(k):
    if not hasattr(k, "_eps"):
        k._eps = k.sb([128, 1], F32)
        k._eps_d = Dep()
        k.op("pool", lambda e: e.memset(k._eps[:], RMS_EPS), writes=[k._eps_d])
        for eng in ("act", "dve", "pe"):
            k._wait(eng, "pool", k.cnt["pool"])
    return k._eps[:]


def kb_barrier(k):
    cur = {}
    for e in ("pe", "act", "dve", "pool"):
        if k.cnt[e]:
            cur[e] = k.cnt[e]
    for q, ring in k.dmaq.items():
        n = ring["n"]
        for slot, key in enumerate(ring["keys"]):
            uses = n if q == "coll" else (n - slot + k.DMA_RING - 1) // k.DMA_RING
            if uses > 0:
                cur[key] = 16 * uses
    for e in ("pe", "act", "dve", "pool", "sp"):
        for key, v in cur.items():
            if key == e and e in ("pe", "sp"):
                continue
            k._wait(e, key, v)


KB.barrier = kb_barrier

SEQ = 8192
CTX = 256
SA = SEQ + CTX


def b_consts(SEQ=SEQ):
    p = np.arange(128)
    d = p % 64
    hi = d // 32
    dp = d % 32
    i = dp % 16
    freq = (10000.0 ** (-(i.astype(np.float32)) / 16)).astype(np.float32)
    t = np.arange(SEQ)
    pos = np.where(hi[:, None] == 0, (t // 64)[None, :], (t % 64)[None, :]).astype(np.float32)
    ang = pos * freq[:, None]
    cos = np.cos(ang).astype(np.float32)
    sin = np.sin(ang).astype(np.float32)
    prot = np.zeros((128, 128), np.float32)
    for m in range(128):
        if dp[m] < 16:
            prot[m + 16, m] = -1.0
        else:
            prot[m - 16, m] = 1.0
    blk = np.zeros((128, 128), np.float32)
    blk[:64, :64] = 1.0 / 64
    blk[64:, 64:] = 1.0 / 64
    masks = np.zeros((8, 128, 512), np.float32)
    tl = np.arange(512)[None, :]
    for q in range(4):
        s = (128 * q + np.arange(128))[:, None]
        masks[q] = np.where(s <= tl, 0.0, -30000.0)
        masks[4 + q] = np.where(s >= tl, 0.0, -30000.0)
    return {"cos": cos, "sin": sin, "prot": prot, "blk": blk, "masks": masks,
            "ident": np.eye(128, dtype=np.float32)}


def build_b(lam_init, nbatch=2, SEQ=SEQ, phases="PAS"):
    nc = new_nc()
    SA = SEQ + CTX
    NTOK = 2 * SEQ
    hT = nc.dram_tensor("hT", [1024, NTOK], BF16, kind="ExternalInput").ap()
    hcT = nc.dram_tensor("hcT", [1024, 2 * CTX], BF16, kind="ExternalInput").ap()
    W = nc.dram_tensor("W", [1024, 900], F32, kind="ExternalInput").ap()
    pp = nc.dram_tensor("pp", [128, 32], F32, kind="ExternalInput").ap()
    lamd = nc.dram_tensor("lam4", [4, 64], F32, kind="ExternalInput").ap()
    p4 = nc.dram_tensor("p4", [2, 4], F32, kind="ExternalInput").ap()
    cosd = nc.dram_tensor("cos", [128, SEQ], F32, kind="ExternalInput").ap()
    sind = nc.dram_tensor("sin", [128, SEQ], F32, kind="ExternalInput").ap()
    protd = nc.dram_tensor("prot", [128, 128], F32, kind="ExternalInput").ap()
    blkd = nc.dram_tensor("blk", [128, 128], F32, kind="ExternalInput").ap()
    trid = nc.dram_tensor("tri", [128, 128], F32, kind="ExternalInput").ap()
    maskd = nc.dram_tensor("masks", [8, 128, 512], F32, kind="ExternalInput").ap()
    identd = nc.dram_tensor("ident", [128, 128], F32, kind="ExternalInput").ap()
    oatt = nc.dram_tensor("oT", [128, NTOK], BF16, kind="ExternalOutput").ap()
    ygo = nc.dram_tensor("ygT", [128, NTOK], BF16, kind="ExternalOutput").ap()
    sso = nc.dram_tensor("ss", [1, NTOK], F32, kind="ExternalOutput").ap()
    urow = nc.dram_tensor("urow", [4, SEQ], F32, kind="Internal").ap()
    WQ, WK, WV, WDT, WZ, WX, WBm, WC = 0, 128, 256, 384, 388, 516, 644, 772
    NS = SA // 128
    NL = SEQ // 128
    BUFW = SEQ + 276
    LATR, CTXR = 6, SEQ + 14
    LATC, CTXC = 2, SEQ + 10

    def ccol(s):
        return LATC + 128 * s if s < NL else CTXC + 128 * (s - NL)

    with ExitStack() as es:
        k = KB(nc, es)
        eps = eps_ap(k)
        ident = k.sb([128, 128], F32)
        identb = k.sb([128, 128], BF16)
        prot = k.sb([128, 128], BF16)
        blk = k.sb([128, 128], BF16)
        ones = k.sb([128, 128], BF16)
        onesf = k.sb([128, 128], F32)
        tri = k.sb([128, 128], F32)
        pps = k.sb([128, 32], F32)
        lam = k.sb([128, 8], F32)
        lam_in = k.sb([128, 4, 64], F32)
        p4s = k.sb([128, 2, 4], F32)
        Wb = k.sb([128, 8, 900], BF16)
        dC = Dep()
        with ExitStack() as es2:
            k.es = es2
            stg = k.sb([128, 900], F32)
            k.dma("sp", ident[:], identd, writes=[dC])
            k.dma("sp", tri[:], trid, writes=[dC])
            k.dma("sp", stg[:, 0:128], protd, writes=[dC])
            k.op("dve", lambda e: e.tensor_copy(out=prot[:], in_=stg[:, 0:128]), reads=[dC], writes=[dC])
            k.dma("sp", stg[:, 0:128], blkd, writes=[dC])
            k.op("dve", lambda e: e.tensor_copy(out=blk[:], in_=stg[:, 0:128]), reads=[dC], writes=[dC])
            k.op("dve", lambda e: e.tensor_copy(out=identb[:], in_=ident[:]), reads=[dC], writes=[dC])
            k.op("dve", lambda e: e.memset(ones[:], 1.0), writes=[dC])
            k.op("dve", lambda e: e.memset(onesf[:], 1.0), writes=[dC])
            k.dma("sp", pps[:], pp, writes=[dC])
            k.dma("sp", p4s[:].rearrange("p a r -> p (a r)"), p4.rearrange("a r -> (a r)").partition_broadcast(128), writes=[dC])
            k.dma("sp", lam_in[:].rearrange("p a d -> p (a d)"), lamd.rearrange("a d -> (a d)").partition_broadcast(128), writes=[dC])
            k.op("dve", lambda e: e.tensor_tensor(out=lam_in[:, 0, :], in0=lam_in[:, 0, :], in1=lam_in[:, 1, :], op=ALU.mult), reads=[dC], writes=[dC])
            k.op("dve", lambda e: e.tensor_tensor(out=lam_in[:, 2, :], in0=lam_in[:, 2, :], in1=lam_in[:, 3, :], op=ALU.mult), reads=[dC], writes=[dC])
            k.op("dve", lambda e: e.tensor_reduce(out=lam[:, 0:1], in_=lam_in[:, 0, :], axis=AX.X, op=ALU.add), reads=[dC], writes=[dC])
            k.op("dve", lambda e: e.tensor_reduce(out=lam[:, 1:2], in_=lam_in[:, 2, :], axis=AX.X, op=ALU.add), reads=[dC], writes=[dC])
            k.op("act", lambda e: e.activation(out=lam[:, 0:2], in_=lam[:, 0:2], func=AF.Exp), reads=[dC], writes=[dC])
            k.op("dve", lambda e: e.tensor_tensor(out=lam[:, 2:3], in0=lam[:, 1:2], in1=lam[:, 0:1], op=ALU.subtract), reads=[dC], writes=[dC])
            k.op("dve", lambda e: e.tensor_scalar(out=lam[:, 3:4], in0=lam[:, 2:3], scalar1=-float(lam_init), scalar2=None, op0=ALU.add), reads=[dC], writes=[dC])
            k.op("dve", lambda e: e.tensor_scalar(out=pps[:, 23:24], in0=pps[:, 2:3], scalar1=float(1.0 - lam_init), scalar2=None, op0=ALU.mult), reads=[dC], writes=[dC])
            k.op("act", lambda e: e.activation(out=p4s[:, 1, :], in_=p4s[:, 1, :], func=AF.Exp), reads=[dC], writes=[dC])
            k.op("dve", lambda e: e.tensor_scalar(out=p4s[:, 1, :], in0=p4s[:, 1, :], scalar1=-1.0, scalar2=None, op0=ALU.mult), reads=[dC], writes=[dC])
            for c in range(8):
                k.dma("sp", stg[:], W[c * 128:(c + 1) * 128, :], writes=[dC])
                k.op("dve", lambda e: e.tensor_copy(out=Wb[:, c, :], in_=stg[:]), reads=[dC], writes=[dC])
            k.barrier()
        k.es = es

        xr = k.sb([128, BUFW], BF16)
        Br = k.sb([128, BUFW], BF16)
        Cr = k.sb([128, BUFW], BF16)
        sz = k.sb([128, SEQ], BF16)
        dtc = k.sb([128, NS, 4], F32)
        for b in range(nbatch):
            d_q, d_k, d_v, d_x, d_B, d_Cc, d_z, d_dt = (Dep() for _ in range(8))
            with ExitStack() as esm:
                k.es = esm
                qT = k.sb([128, SEQ], BF16)
                kT = k.sb([128, SA], BF16)
                vS = k.sb([128, NS, 128], BF16)
                with ExitStack() as es2:
                    k.es = es2
                    hb = [k.sb([128, 8, 512], BF16) for _ in range(2)]
                    d_hb = [Dep(), Dep()]
                    cs_ = [k.sb([128, 512], F32) for _ in range(2)]
                    sn_ = [k.sb([128, 512], F32) for _ in range(2)]
                    d_cs = [Dep(), Dep()]
                    sq = k.sb([128, 512], BF16); d_sq = Dep()
                    rst = k.sb([128, 512], F32); d_rst = Dep()
                    qn = k.sb([128, 512], F32); d_qn = Dep()
                    qnb = k.sb([128, 512], BF16); d_qnb = Dep()
                    t1 = k.sb([128, 512], F32); d_t1 = Dep()
                    pf = [k.ps([128, 512]) for _ in range(3)]
                    d_pf = [Dep() for _ in range(3)]
                    pms = k.ps([128, 512]); d_pms = Dep()
                    prt = k.ps([128, 512]); d_prt = Dep()
                    pv = k.ps([128, 512]); d_pv = Dep()
                    pdt = k.ps([128, 512]); d_pdt = Dep()
                    for buf, dd in ((xr, d_x), (Br, d_B), (Cr, d_Cc)):
                        k.op("dve", lambda e: e.memset(buf[:, 0:LATR], 0.0), writes=[dd])
                        k.op("dve", lambda e: e.memset(buf[:, LATR + SEQ:CTXR], 0.0), writes=[dd])
                        k.op("dve", lambda e: e.memset(buf[:, CTXR + CTX:BUFW], 0.0), writes=[dd])
                    nblk = SEQ // 512 + 1
                    ipf = [0]
                    for blk_i in range(nblk):
                        isctx = blk_i == nblk - 1
                        n = 256 if isctx else 512
                        hbuf = hb[blk_i % 2]
                        dh = d_hb[blk_i % 2]
                        if isctx:
                            src = hcT[:, b * CTX:(b + 1) * CTX]
                        else:
                            src = hT[:, b * SEQ + blk_i * 512: b * SEQ + (blk_i + 1) * 512]
                        k.dma("sp", hbuf[:, :, 0:n], src.rearrange("(c p) t -> p c t", p=128), writes=[dh])
                        t0 = blk_i * 512
                        dcs = d_cs[blk_i % 2]
                        if not isctx:
                            k.dma("sp", cs_[blk_i % 2][:], cosd[:, t0:t0 + 512], writes=[dcs])
                            k.dma("sp", sn_[blk_i % 2][:], sind[:, t0:t0 + 512], writes=[dcs])

                        def proj(wcol, width=128):
                            j = ipf[0] % 3
                            ipf[0] += 1
                            for c in range(8):
                                k.op("pe", lambda e: e.matmul(pf[j][0:width, 0:n], lhsT=Wb[:, c, wcol:wcol + width], rhs=hbuf[:, c, 0:n],
                                                              start=(c == 0), stop=(c == 7)), reads=[dh], writes=[d_pf[j]])
                            return pf[j], d_pf[j]

                        for which in ("q", "k"):
                            if which == "q" and isctx:
                                continue
                            ps_, dps_ = proj(WQ if which == "q" else WK)
                            gcol = 0 if which == "q" else 1
                            k.op("act", lambda e: e.activation(out=sq[:, 0:n], in_=ps_[:, 0:n], func=AF.Square), reads=[dps_], writes=[d_sq])
                            k.op("pe", lambda e: e.matmul(pms[:, 0:n], lhsT=blk[:], rhs=sq[:, 0:n], start=True, stop=True), reads=[d_sq], writes=[d_pms])
                            k.op("act", lambda e: e.activation(out=rst[:, 0:n], in_=pms[:, 0:n], func=AF.Sqrt, bias=eps, scale=1.0), reads=[d_pms], writes=[d_rst])
                            k.op("dve", lambda e: e.reciprocal(out=rst[:, 0:n], in_=rst[:, 0:n]), reads=[], writes=[d_rst])
                            if isctx:
                                k.op("dve", lambda e: e.scalar_tensor_tensor(out=kT[:, SEQ:SA], in0=ps_[:, 0:n], scalar=pps[:, gcol:gcol + 1], in1=rst[:, 0:n],
                                                                             op0=ALU.mult, op1=ALU.mult), reads=[dps_, d_rst], writes=[d_k])
                                continue
                            k.op("dve", lambda e: e.scalar_tensor_tensor(out=qn[:], in0=ps_[:], scalar=pps[:, gcol:gcol + 1], in1=rst[:],
                                                                         op0=ALU.mult, op1=ALU.mult), reads=[dps_, d_rst], writes=[d_qn])
                            k.op("act", lambda e: e.activation(out=qnb[:], in_=qn[:], func=AF.Copy), reads=[d_qn], writes=[d_qnb])
                            k.op("pe", lambda e: e.matmul(prt[:], lhsT=prot[:], rhs=qnb[:], start=True, stop=True), reads=[d_qnb], writes=[d_prt])
                            k.op("dve", lambda e: e.tensor_tensor(out=t1[:], in0=prt[:], in1=sn_[blk_i % 2][:], op=ALU.mult), reads=[d_prt, dcs], writes=[d_t1])
                            k.op("dve", lambda e: e.tensor_tensor(out=qn[:], in0=qn[:], in1=cs_[blk_i % 2][:], op=ALU.mult), reads=[dcs], writes=[d_qn])
                            dst = qT if which == "q" else kT
                            dd = d_q if which == "q" else d_k
                            k.op("dve", lambda e: e.tensor_tensor(out=dst[:, t0:t0 + 512], in0=qn[:], in1=t1[:], op=ALU.add), reads=[d_qn, d_t1], writes=[dd])
                        c0 = (CTXR if isctx else LATR + t0)
                        for wcol, buf, dd in ((WX, xr, d_x), (WBm, Br, d_B), (WC, Cr, d_Cc)):
                            ps_, dps_ = proj(wcol)
                            k.op("act", lambda e: e.activation(out=buf[:, c0:c0 + n], in_=ps_[:, 0:n], func=AF.Copy), reads=[dps_], writes=[dd])
                        if not isctx:
                            ps_, dps_ = proj(WZ)
                            k.op("act", lambda e: e.activation(out=sz[:, t0:t0 + 512], in_=ps_[:], func=AF.Silu), reads=[dps_], writes=[d_z])
                        ti = (NL if isctx else blk_i * 4)
                        for tt in range(n // 128):
                            for c in range(8):
                                k.op("pe", lambda e: e.matmul(pv[:, tt * 128:(tt + 1) * 128], lhsT=hbuf[:, c, tt * 128:(tt + 1) * 128], rhs=Wb[:, c, WV:WV + 128],
                                                              start=(c == 0), stop=(c == 7)), reads=[dh], writes=[d_pv])
                            for c in range(8):
                                k.op("pe", lambda e: e.matmul(pdt[:, tt * 4:(tt + 1) * 4], lhsT=hbuf[:, c, tt * 128:(tt + 1) * 128], rhs=Wb[:, c, WDT:WDT + 4],
                                                              start=(c == 0), stop=(c == 7)), reads=[dh], writes=[d_pdt])
                        k.op("dve", lambda e: e.tensor_copy(out=vS[:, ti:ti + n // 128, :].rearrange("p a e -> p (a e)"), in_=pv[:, 0:n]), reads=[d_pv], writes=[d_v])
                        k.op("dve", lambda e: e.tensor_copy(out=dtc[:, ti:ti + n // 128, :].rearrange("p a r -> p (a r)"), in_=pdt[:, 0:n // 32]), reads=[d_pdt], writes=[d_dt])
                    k.barrier()
                k.es = esm
                with ExitStack() as es2:
                    k.es = es2
                    pS = [k.ps([128, 512]) for _ in range(3)]
                    d_pS = [Dep() for _ in range(3)]
                    po = [k.ps([128, 512]) for _ in range(2)]
                    psm = [k.ps([128, 512]) for _ in range(2)]
                    d_po = [Dep(), Dep()]
                    d_psm = [Dep(), Dep()]
                    Pt = [k.sb([128, 512], BF16) for _ in range(3)]
                    d_Pt = [Dep() for _ in range(3)]
                    r_ = [k.sb([128, 512], F32) for _ in range(2)]
                    d_r = [Dep(), Dep()]
                    o_ = k.sb([128, 512], F32); d_o = Dep()
                    osq = k.sb([128, 512], BF16); d_osq = Dep()
                    ob = k.sb([128, 512], BF16); d_ob = Dep()
                    for tb in range(SEQ // 512 if "A" in phases else 0):
                        tsl = slice(tb * 512, (tb + 1) * 512)
                        items = [(c, s) for c in range(2) for s in range(NS)]

                        def qk(i):
                            c, s = items[i]
                            j = i % 3
                            k.op("pe", lambda e: e.matmul(pS[j][:], lhsT=kT[c * 64:(c + 1) * 64, s * 128:(s + 1) * 128], rhs=qT[c * 64:(c + 1) * 64, tsl],
                                                          start=True, stop=True), reads=[d_q, d_k], writes=[d_pS[j]])
                            k.op("act", lambda e: e.activation(out=Pt[j][:], in_=pS[j][:], func=AF.Exp, scale=0.125), reads=[d_pS[j]], writes=[d_Pt[j]])

                        def pvm(i):
                            c, s = items[i]
                            j = i % 3
                            k.op("pe", lambda e: e.matmul(po[c][:], lhsT=vS[:, s, :], rhs=Pt[j][:], start=(s == 0), stop=(s == NS - 1)),
                                 reads=[d_v, d_Pt[j]], writes=[d_po[c]])
                            k.op("pe", lambda e: e.matmul(psm[c][:], lhsT=ones[:], rhs=Pt[j][:], start=(s == 0), stop=(s == NS - 1)),
                                 reads=[d_Pt[j]], writes=[d_psm[c]])

                        qk(0)
                        for i in range(len(items)):
                            if i + 1 < len(items):
                                qk(i + 1)
                            pvm(i)
                        for c in range(2):
                            k.op("dve", lambda e: e.reciprocal(out=r_[c][:], in_=psm[c][:]), reads=[d_psm[c]], writes=[d_r[c]])
                            k.op("dve", lambda e: e.tensor_tensor(out=r_[c][:], in0=po[c][:], in1=r_[c][:], op=ALU.mult), reads=[d_po[c]], writes=[d_r[c]])
                        k.op("dve", lambda e: e.scalar_tensor_tensor(out=o_[:], in0=r_[1][:], scalar=lam[:, 3:4], in1=r_[0][:], op0=ALU.mult, op1=ALU.add),
                             reads=[d_r[0], d_r[1]], writes=[d_o])
                        k.op("act", lambda e: e.activation(out=osq[:], in_=o_[:], func=AF.Square), reads=[d_o], writes=[d_osq])
                        k.op("pe", lambda e: e.matmul(pS[0][:], lhsT=ones[:], rhs=osq[:], start=True, stop=True), reads=[d_osq], writes=[d_pS[0]])
                        k.op("act", lambda e: e.activation(out=r_[0][:], in_=pS[0][:], func=AF.Sqrt, bias=eps, scale=1.0 / 128), reads=[d_pS[0]], writes=[d_r[0]])
                        k.op("dve", lambda e: e.reciprocal(out=r_[0][:], in_=r_[0][:]), reads=[], writes=[d_r[0]])
                        k.op("dve", lambda e: e.scalar_tensor_tensor(out=ob[:], in0=o_[:], scalar=pps[:, 23:24], in1=r_[0][:], op0=ALU.mult, op1=ALU.mult),
                             reads=[d_o, d_r[0]], writes=[d_ob])
                        k.dma("sp", oatt[:, b * SEQ + tb * 512: b * SEQ + (tb + 1) * 512], ob[:], reads=[d_ob], writes=[Dep()])
                    k.barrier()
                k.es = esm
            k.es = es
            with ExitStack() as es2:
              if "S" in phases:
                  k.es = es2
                  xtok = k.sb([128, NS, 128], BF16); d_xt = Dep()
                  acc = [k.sb([128, 2048], F32) for _ in range(2)]
                  d_acc = [Dep(), Dep()]
                  for (buf, wc0, dd) in ((xr, 3, d_x), (Br, 9, d_B), (Cr, 15, d_Cc)):
                      segs = [(LATR + s0, min(2048, SEQ)) for s0 in range(0, SEQ, 2048)] + [(CTXR, CTX)]
                      for si, (rc, n) in enumerate(segs):
                          a_ = acc[si % 2]; da = d_acc[si % 2]
                          k.op("dve", lambda e: e.tensor_scalar(out=a_[:, 0:n], in0=buf[:, rc - 2:rc - 2 + n], scalar1=pps[:, wc0:wc0 + 1], scalar2=None, op0=ALU.mult),
                               reads=[dd], writes=[da])
                          for tap in range(1, 5):
                              k.op("dve", lambda e: e.scalar_tensor_tensor(out=a_[:, 0:n], in0=buf[:, rc - 2 + tap:rc - 2 + tap + n], scalar=pps[:, wc0 + tap:wc0 + tap + 1],
                                                                           in1=a_[:, 0:n], op0=ALU.mult, op1=ALU.add), reads=[dd], writes=[da])
                          k.op("act", lambda e: e.activation(out=buf[:, rc - 4:rc - 4 + n], in_=a_[:, 0:n], func=AF.Silu, bias=pps[:, wc0 + 5:wc0 + 6], scale=1.0),
                               reads=[da], writes=[dd])
                  xc, Bc, Cc = xr, Br, Cr
                  dA = k.sb([128, NS, 4], F32); d_dA = Dep()
                  wi = k.sb([128, NS, 4], F32); d_wi = Dep()
                  tot = k.sb([128, NS, 4], F32); d_tot = Dep()
                  inc = k.sb([128, NS, 4], F32); d_inc = Dep()
                  ncol = k.sb([128, NS, 4], F32); d_nc = Dep()
                  on66 = k.sb([128, NS], F32); d_on = Dep()
                  pw = k.ps([128, 512]); d_pw = Dep()
                  pt_ = k.ps([128, 512]); d_pt = Dep()
                  k.op("dve", lambda e: e.memset(on66[:], 1.0), writes=[d_on])
                  for r in range(4):
                      k.op("dve", lambda e: e.tensor_scalar(out=dtc[:, :, r], in0=dtc[:, :, r], scalar1=p4s[:, 0, r:r + 1], scalar2=None, op0=ALU.add), reads=[d_dt], writes=[d_dt])
                  k.op("act", lambda e: e.activation(out=dtc[:].rearrange("p a r -> p (a r)"), in_=dtc[:].rearrange("p a r -> p (a r)"), func=AF.Exp), reads=[], writes=[d_dt])
                  k.op("act", lambda e: e.activation(out=dtc[:].rearrange("p a r -> p (a r)"), in_=dtc[:].rearrange("p a r -> p (a r)"), func=AF.Ln, bias=1.0, scale=1.0), reads=[], writes=[d_dt])
                  for r in range(4):
                      k.op("dve", lambda e: e.tensor_scalar(out=dA[:, :, r], in0=dtc[:, :, r], scalar1=p4s[:, 1, r:r + 1], scalar2=None, op0=ALU.mult), reads=[d_dt], writes=[d_dA])
                  k.op("pe", lambda e: e.matmul(pw[:, 0:NS * 4], lhsT=tri[:], rhs=dA[:].rearrange("p a r -> p (a r)"), start=True, stop=True), reads=[d_dA], writes=[d_pw])
                  k.op("pe", lambda e: e.matmul(pt_[:, 0:NS * 4], lhsT=onesf[:], rhs=dA[:].rearrange("p a r -> p (a r)"), start=True, stop=True), reads=[d_dA], writes=[d_pt])
                  k.op("dve", lambda e: e.tensor_copy(out=wi[:].rearrange("p a r -> p (a r)"), in_=pw[:, 0:NS * 4]), reads=[d_pw], writes=[d_wi])
                  k.op("dve", lambda e: e.tensor_copy(out=tot[:].rearrange("p a r -> p (a r)"), in_=pt_[:, 0:NS * 4]), reads=[d_pt], writes=[d_tot])
                  for r in range(2):
                      k.op("dve", lambda e: e.tensor_tensor_scan(out=inc[:, NL:NS, r], data0=on66[:, NL:NS], data1=tot[:, NL:NS, r], initial=0.0, op0=ALU.mult, op1=ALU.add),
                           reads=[d_tot, d_on], writes=[d_inc])
                      k.op("dve", lambda e: e.tensor_tensor_scan(out=inc[:, 0:NL, r], data0=on66[:, 0:NL], data1=tot[:, 0:NL, r], initial=inc[:, NS - 1, r:r + 1], op0=ALU.mult, op1=ALU.add),
                           reads=[d_tot, d_on], writes=[d_inc])
                  for r in range(2, 4):
                      k.op("dve", lambda e: e.tensor_tensor_scan(out=inc[:, :, r], data0=on66[:], data1=tot[:, :, r], initial=0.0, op0=ALU.mult, op1=ALU.add),
                           reads=[d_tot, d_on], writes=[d_inc])
                  k.op("dve", lambda e: e.tensor_tensor(out=ncol[:], in0=tot[:], in1=inc[:], op=ALU.subtract), reads=[d_tot, d_inc], writes=[d_nc])
                  k.op("dve", lambda e: e.tensor_tensor(out=ncol[:, :, 0:2], in0=ncol[:, :, 0:2], in1=wi[:, :, 0:2], op=ALU.subtract), reads=[d_wi], writes=[d_nc])
                  k.op("dve", lambda e: e.tensor_tensor(out=ncol[:, :, 2:4], in0=wi[:, :, 2:4], in1=ncol[:, :, 2:4], op=ALU.subtract), reads=[d_wi], writes=[d_nc])
                  k.op("dve", lambda e: e.tensor_tensor(out=ncol[:, :, 2:4], in0=ncol[:, :, 2:4], in1=dA[:, :, 2:4], op=ALU.subtract), reads=[d_dA], writes=[d_nc])
                  k.op("dve", lambda e: e.tensor_scalar(out=wi[:], in0=ncol[:], scalar1=-1.0, scalar2=None, op0=ALU.mult), reads=[d_nc], writes=[d_wi])
                  ur = [k.sb([4, 512], F32) for _ in range(2)]
                  d_ur = [Dep(), Dep()]
                  d_urow = Dep()
                  for g in range(NL // 4):
                      for q in range(4):
                          k.op("pe", lambda e: e.transpose(out=pw[0:4, q * 128:(q + 1) * 128], in_=wi[:, 4 * g + q, :], identity=ident[:]), reads=[d_wi], writes=[d_pw])
                      k.op("dve", lambda e: e.tensor_copy(out=ur[g % 2][:], in_=pw[0:4, :]), reads=[d_pw], writes=[d_ur[g % 2]])
                      k.dma("sp", urow[:, g * 512:(g + 1) * 512], ur[g % 2][:], reads=[d_ur[g % 2]], writes=[d_urow])
                  ptx = k.ps([128, 1024], BF16); d_ptx = Dep()
                  for s0 in range(0, NS, 8):
                      ns_ = min(8, NS - s0)
                      for s in range(ns_):
                          cc = ccol(s0 + s)
                          k.op("pe", lambda e: e.transpose(out=ptx[:, s * 128:(s + 1) * 128], in_=xc[:, cc:cc + 128], identity=identb[:]),
                               reads=[d_x], writes=[d_ptx])
                      k.op("dve", lambda e: e.tensor_copy(out=xtok[:, s0:s0 + ns_, :].rearrange("p a e -> p (a e)"), in_=ptx[:, 0:ns_ * 128]),
                           reads=[d_ptx], writes=[d_xt])
                  msk = k.sb([128, 8, 512], BF16); d_msk = Dep()
                  for q in range(8):
                      k.dma("sp", acc[q % 2][:, 0:512], maskd[q], writes=[d_acc[q % 2]])
                      k.op("dve", lambda e: e.tensor_copy(out=msk[:, q, :], in_=acc[q % 2][:, 0:512]), reads=[d_acc[q % 2]], writes=[d_msk])
                  ub = [k.sb([128, 4, 512], F32) for _ in range(2)]
                  d_ub = [Dep(), Dep()]
                  pCB = [k.ps([128, 512]) for _ in range(2)]
                  d_pCB = [Dep(), Dep()]
                  py = [k.ps([128, 512]) for _ in range(2)]
                  d_py = [Dep(), Dep()]
                  pss = k.ps([128, 512]); d_pss = Dep()
                  Et = [k.sb([128, 512], F32) for _ in range(2)]
                  d_Et = [Dep(), Dep()]
                  Wt = [k.sb([128, 512], BF16) for _ in range(2)]
                  d_Wt = [Dep(), Dep()]
                  mt = [k.sb([128, 512], F32) for _ in range(2)]
                  d_mt = [Dep(), Dep()]
                  yv = k.sb([128, 512], F32); d_yv = Dep()
                  ygb = k.sb([128, 512], BF16); d_ygb = Dep()
                  ysq = k.sb([128, 512], BF16); d_ysq = Dep()
                  ssr = k.sb([1, 512], F32); d_ssr = Dep()
                  for tb in range(SEQ // 512):
                      tsl = slice(tb * 512, (tb + 1) * 512)
                      csl = slice(LATC + tb * 512, LATC + (tb + 1) * 512)
                      u_ = ub[tb % 2]; du = d_ub[tb % 2]
                      for r in range(4):
                          k.dma("sp", u_[:, r, :], urow[r, tsl].partition_broadcast(128), reads=[d_urow], writes=[du])
                      items = []
                      for s in list(range(NL, NS)) + list(range(0, 4 * tb + 4)):
                          mq = (s - 4 * tb) if (s < NL and s >= 4 * tb) else None
                          items.append((0, s, mq))
                      for s in list(range(4 * tb, NL)) + list(range(NL, NS)):
                          mq = (4 + s - 4 * tb) if (s < 4 * tb + 4) else None
                          items.append((1, s, mq))
                      nit = len(items)

                      def cb(i):
                          dr_, s, mq = items[i]
                          j = i % 2
                          cc = ccol(s)
                          k.op("pe", lambda e: e.matmul(pCB[j][:], lhsT=Bc[:, cc:cc + 128], rhs=Cc[:, csl], start=True, stop=True),
                               reads=[d_B, d_Cc], writes=[d_pCB[j]])

                      def rest(i):
                          dr_, s, mq = items[i]
                          j = i % 2
                          for hd in range(2):
                              r = dr_ * 2 + hd
                              jj = hd
                              src = u_[:, r, :]
                              rd = [du]
                              if mq is not None:
                                  k.op("pool", lambda e: e.tensor_tensor(out=mt[jj][:], in0=u_[:, r, :], in1=msk[:, mq, :], op=ALU.add),
                                       reads=[du, d_msk], writes=[d_mt[jj]])
                                  src = mt[jj][:]
                                  rd = [d_mt[jj]]
                              k.op("act", lambda e: e.activation(out=Et[jj][:], in_=src, func=AF.Exp, bias=ncol[:, s, r:r + 1], scale=1.0),
                                   reads=rd + [d_nc], writes=[d_Et[jj]])
                              k.op("dve", lambda e: e.scalar_tensor_tensor(out=Wt[jj][:], in0=Et[jj][:], scalar=dtc[:, s, r:r + 1], in1=pCB[j][:],
                                                                           op0=ALU.mult, op1=ALU.mult), reads=[d_Et[jj], d_dt, d_pCB[j]], writes=[d_Wt[jj]])
                              k.op("pe", lambda e: e.matmul(py[hd][hd * 64:(hd + 1) * 64, :], lhsT=xtok[:, s, hd * 64:(hd + 1) * 64], rhs=Wt[jj][:],
                                                            start=(i == 0), stop=(i == nit - 1)),
                                   reads=[d_xt, d_Wt[jj]], writes=[d_py[hd]])

                      cb(0)
                      for i in range(nit):
                          if i + 1 < nit:
                              cb(i + 1)
                          rest(i)
                      for hd in range(2):
                          hs = slice(hd * 64, (hd + 1) * 64)
                          k.op("dve", lambda e: e.scalar_tensor_tensor(out=yv[hs, :], in0=xc[hs, csl], scalar=pps[hs, 21:22], in1=py[hd][hs, :],
                                                                       op0=ALU.mult, op1=ALU.add), reads=[d_x, d_py[hd]], writes=[d_yv])
                      k.op("dve", lambda e: e.tensor_tensor(out=yv[:], in0=yv[:], in1=sz[:, tsl], op=ALU.mult), reads=[d_z], writes=[d_yv])
                      k.op("act", lambda e: e.activation(out=ysq[:], in_=yv[:], func=AF.Square), reads=[d_yv], writes=[d_ysq])
                      k.op("pe", lambda e: e.matmul(pss[0:1, :], lhsT=ones[:, 0:1], rhs=ysq[:], start=True, stop=True), reads=[d_ysq], writes=[d_pss])
                      k.op("dve", lambda e: e.tensor_copy(out=ssr[:], in_=pss[0:1, :]), reads=[d_pss], writes=[d_ssr])
                      k.op("dve", lambda e: e.tensor_scalar(out=ygb[:], in0=yv[:], scalar1=pps[:, 22:23], scalar2=None, op0=ALU.mult), reads=[d_yv], writes=[d_ygb])
                      k.dma("sp", ygo[:, b * SEQ + tb * 512: b * SEQ + (tb + 1) * 512], ygb[:], reads=[d_ygb], writes=[Dep()])
                      k.dma("sp", sso[:, b * SEQ + tb * 512: b * SEQ + (tb + 1) * 512], ssr[:], reads=[d_ssr], writes=[Dep()])
                  k.barrier()
            k.es = es
        k.finish([])
    return nc


SW_LIMIT = 7.0
SW_ALPHA = 1.702


def build_d(NTOK=16384):
    nc = new_nc()
    h2T = nc.dram_tensor("h2T", [1024, NTOK], BF16, kind="ExternalInput").ap()
    gT = nc.dram_tensor("gT", [4, NTOK], F32, kind="ExternalInput").ap()
    wgu = nc.dram_tensor("wgu", [4, 1024, 2048], F32, kind="ExternalInput").ap()
    bgu = nc.dram_tensor("bgu", [128, 64], F32, kind="ExternalInput").ap()
    wdn = nc.dram_tensor("wdn", [4, 1024, 1024], F32, kind="ExternalInput").ap()
    bdn = nc.dram_tensor("bdn", [4, 1024], F32, kind="ExternalInput").ap()
    part = nc.dram_tensor("part", [NTOK, 1024], BF16, kind="ExternalOutput").ap()
    scr = nc.dram_tensor("scr", [NTOK, 1024], F32, kind="Internal").ap()
    NBLK = NTOK // 512
    with ExitStack() as es:
        k = KB(nc, es)
        wg = k.sb([128, 2, 8, 2048], BF16); d_wg = Dep()
        wd = k.sb([128, 2, 8, 1024], BF16); d_wd = Dep()
        bd = k.sb([128, 2, 1024], BF16); d_bd = Dep()
        bg = k.sb([128, 64], F32); d_bg = Dep()
        stg = [k.sb([128, 2048], F32) for _ in range(2)]
        d_stg = [Dep(), Dep()]
        hb = [k.sb([128, 8, 512], BF16) for _ in range(2)]
        d_hb = [Dep(), Dep()]
        gb = [k.sb([128, 2, 512], F32) for _ in range(2)]
        d_gb = [Dep(), Dep()]
        gr = [k.sb([1, 2, 512], F32) for _ in range(2)]
        grb = [k.sb([128, 2, 512], BF16) for _ in range(2)]
        d_gr = [Dep(), Dep()]
        d_grb = [Dep(), Dep()]
        act = k.sb([128, 16, 512], BF16)
        d_act = [Dep() for _ in range(16)]
        tg = [k.sb([128, 512], F32) for _ in range(2)]
        ts_ = [k.sb([128, 512], F32) for _ in range(2)]
        tl = [k.sb([128, 512], F32) for _ in range(2)]
        d_tg = [Dep(), Dep()]
        d_ts = [Dep(), Dep()]
        d_tl = [Dep(), Dep()]
        ot = [k.sb([128, 1024], F32) for _ in range(2)]
        d_ot = [Dep(), Dep()]
        otb = [k.sb([128, 1024], BF16) for _ in range(2)]
        d_otb = [Dep(), Dep()]
        pvt = [k.sb([128, 1024], F32) for _ in range(2)]
        d_pvt = [Dep(), Dep()]
        pG = [k.ps([128, 512]) for _ in range(2)]
        pL = [k.ps([128, 512]) for _ in range(2)]
        d_pG = [Dep(), Dep()]
        d_pL = [Dep(), Dep()]
        pY = [k.ps([128, 512]) for _ in range(2)]
        d_pY = [Dep(), Dep()]
        d_part = [Dep() for _ in range(NBLK * 4)]
        k.dma("sp", bg[:], bgu, writes=[d_bg])
        k.op("dve", lambda e: e.memset(bd[:], 0.0), writes=[d_bd])
        for b_ in range(2):
            k.op("dve", lambda e: e.memset(grb[b_][:], 0.0), writes=[d_grb[b_]])
        nst = 0
        for ps_ in range(2):
            for el in range(2):
                e_ = 2 * ps_ + el
                for kc in range(8):
                    s_, ds_ = stg[nst % 2], d_stg[nst % 2]
                    nst += 1
                    k.dma("sp", s_[:], wgu[e_, kc * 128:(kc + 1) * 128, :], writes=[ds_])
                    v = s_[:].rearrange("p (f two) -> p two f", two=2)
                    k.op("act", lambda e: e.activation(out=wg[:, el, kc, 0:1024], in_=v[:, 0, :], func=AF.Copy), reads=[ds_], writes=[d_wg])
                    k.op("pool", lambda e: e.tensor_copy(out=wg[:, el, kc, 1024:2048], in_=v[:, 1, :]), reads=[ds_], writes=[d_wg])
                for fc in range(0, 8, 2):
                    s_, ds_ = stg[nst % 2], d_stg[nst % 2]
                    nst += 1
                    k.dma("sp", s_[:].rearrange("p (a d) -> p a d", a=2), wdn[e_, fc * 128:(fc + 2) * 128, :].rearrange("(a p) d -> p a d", p=128), writes=[ds_])
                    k.op("dve", lambda e: e.tensor_copy(out=wd[:, el, fc:fc + 2, :].rearrange("p a d -> p (a d)"), in_=s_[:]), reads=[ds_], writes=[d_wd])
                s_, ds_ = stg[nst % 2], d_stg[nst % 2]
                nst += 1
                k.dma("sp", s_[0:1, 0:1024], bdn[e_:e_ + 1, :], writes=[ds_])
                k.op("dve", lambda e: e.tensor_copy(out=bd[0:1, el, :], in_=s_[0:1, 0:1024]), reads=[ds_], writes=[d_bd])
            for blk_i in range(NBLK):
                b = blk_i % 2
                tsl = slice(blk_i * 512, (blk_i + 1) * 512)
                k.dma("sp", hb[b][:], h2T[:, tsl].rearrange("(c p) t -> p c t", p=128), writes=[d_hb[b]])
                for el in range(2):
                    k.dma("sp", gb[b][:, el, :], gT[2 * ps_ + el, tsl].partition_broadcast(128), writes=[d_gb[b]])
                k.dma("sp", gr[b][:].rearrange("o a t -> o (a t)").rearrange("o (a t) -> o a t", a=2), gT[2 * ps_:2 * ps_ + 2, tsl].rearrange("(o a) t -> o a t", o=1), writes=[d_gr[b]])
                k.op("dve", lambda e: e.tensor_copy(out=grb[b][0:1, :, :], in_=gr[b][:]), reads=[d_gr[b]], writes=[d_grb[b]])
                gi = 0
                for el in range(2):
                    for fc in range(8):
                        j = gi % 2
                        gi += 1
                        for kc in range(8):
                            k.op("pe", lambda e: e.matmul(pG[j][:], lhsT=wg[:, el, kc, fc * 128:(fc + 1) * 128], rhs=hb[b][:, kc, :],
                                                          start=(kc == 0), stop=(kc == 7)), reads=[d_wg, d_hb[b]], writes=[d_pG[j]])
                        for kc in range(8):
                            k.op("pe", lambda e: e.matmul(pL[j][:], lhsT=wg[:, el, kc, 1024 + fc * 128:1024 + (fc + 1) * 128], rhs=hb[b][:, kc, :],
                                                          start=(kc == 0), stop=(kc == 7)), reads=[d_wg, d_hb[b]], writes=[d_pL[j]])
                        bc = (2 * ps_ + el) * 16 + fc
                        k.op("dve", lambda e: e.tensor_scalar(out=tg[j][:], in0=pG[j][:], scalar1=bg[:, bc:bc + 1], scalar2=SW_LIMIT, op0=ALU.add, op1=ALU.min),
                             reads=[d_pG[j], d_bg], writes=[d_tg[j]])
                        k.op("act", lambda e: e.activation(out=ts_[j][:], in_=tg[j][:], func=AF.Sigmoid, scale=SW_ALPHA), reads=[d_tg[j]], writes=[d_ts[j]])
                        k.op("dve", lambda e: e.tensor_scalar(out=tl[j][:], in0=pL[j][:], scalar1=bg[:, bc + 8:bc + 9], scalar2=SW_LIMIT, op0=ALU.add, op1=ALU.min),
                             reads=[d_pL[j], d_bg], writes=[d_tl[j]])
                        k.op("dve", lambda e: e.tensor_scalar(out=tl[j][:], in0=tl[j][:], scalar1=-SW_LIMIT, scalar2=1.0, op0=ALU.max, op1=ALU.add),
                             reads=[], writes=[d_tl[j]])
                        k.op("pool", lambda e: e.tensor_tensor(out=ts_[j][:], in0=ts_[j][:], in1=tg[j][:], op=ALU.mult), reads=[d_tg[j]], writes=[d_ts[j]])
                        k.op("pool", lambda e: e.tensor_tensor(out=ts_[j][:], in0=ts_[j][:], in1=tl[j][:], op=ALU.mult), reads=[d_tl[j]], writes=[d_ts[j]])
                        ai = el * 8 + fc
                        k.op("dve", lambda e: e.tensor_tensor(out=act[:, ai, :], in0=ts_[j][:], in1=gb[b][:, el, :], op=ALU.mult),
                             reads=[d_ts[j], d_gb[b]], writes=[d_act[ai]])
                for tt in range(4):
                    o_, do_ = (ot[tt % 2], d_ot[tt % 2]) if ps_ == 0 else (otb[tt % 2], d_otb[tt % 2])
                    row = slice(blk_i * 512 + tt * 128, blk_i * 512 + (tt + 1) * 128)
                    dp = d_part[blk_i * 4 + tt]
                    if ps_ == 1:
                        k.dma("sp", pvt[tt % 2][:], scr[row, :], reads=[dp], writes=[d_pvt[tt % 2]])
                    for h in range(2):
                        hs = slice(h * 512, (h + 1) * 512)
                        n = 0
                        for el in range(2):
                            for fc in range(8):
                                ai = el * 8 + fc
                                k.op("pe", lambda e: e.matmul(pY[h][:], lhsT=act[:, ai, tt * 128:(tt + 1) * 128], rhs=wd[:, el, fc, hs],
                                                              start=(n == 0), stop=False), reads=[d_act[ai], d_wd], writes=[d_pY[h]])
                                n += 1
                        for el in range(2):
                            k.op("pe", lambda e: e.matmul(pY[h][:], lhsT=grb[b][:, el, tt * 128:(tt + 1) * 128], rhs=bd[:, el, hs],
                                                          start=False, stop=(el == 1)), reads=[d_grb[b], d_bd], writes=[d_pY[h]])
                        if ps_ == 0:
                            k.op("act", lambda e: e.activation(out=o_[:, hs], in_=pY[h][:], func=AF.Copy), reads=[d_pY[h]], writes=[do_])
                        else:
                            k.op("dve", lambda e: e.tensor_tensor(out=o_[:, hs], in0=pY[h][:], in1=pvt[tt % 2][:, hs], op=ALU.add),
                                 reads=[d_pY[h], d_pvt[tt % 2]], writes=[do_])
                    k.dma("sp", (scr if ps_ == 0 else part)[row, :], o_[:], reads=[do_], writes=[dp])
        k.finish([])
    return nc


def d_inputs(inp, layer, j):
    es = slice(4 * j, 4 * j + 4)
    bgu = inp["b_gate_up"][layer][es]
    b = bgu.reshape(4, 8, 128, 2)
    bg = np.concatenate([b[..., 0], b[..., 1]], 1)
    bg = np.ascontiguousarray(bg.transpose(2, 0, 1).reshape(128, 64))
    return {"wgu": np.ascontiguousarray(inp["w_gate_up"][layer][es]), "bgu": bg,
            "wdn": np.ascontiguousarray(inp["w_down"][layer][es]), "bdn": np.ascontiguousarray(inp["b_down"][layer][es])}


def f_consts():
    import ml_dtypes
    bf = ml_dtypes.bfloat16
    c = np.arange(256)[:, None]
    m = np.arange(256)[None, :]
    ang = 2 * np.pi * ((c * m) % 256) / 256.0
    G = np.concatenate([np.cos(ang), -np.sin(ang)], 1).astype(np.float32)
    l1 = np.arange(64)[:, None]
    k1 = np.arange(64)[None, :]
    a = 2 * np.pi * ((l1 * k1) % 64) / 64.0
    cc, ss = np.cos(a), np.sin(a)
    F64 = np.block([[cc, -ss], [ss, cc]]).astype(np.float32)
    l2 = np.arange(128)[:, None]
    k2 = np.arange(128)[None, :]
    a2 = 2 * np.pi * ((l2 * k2) % 128) / 128.0
    C128, S128 = np.cos(a2).astype(np.float32), np.sin(a2).astype(np.float32)
    tw = 2 * np.pi * (np.arange(128)[:, None] * np.arange(64)[None, :]) / 8192.0
    tc = np.repeat(np.cos(tw)[:, None, :], 4, 1).astype(np.float32)
    ts = np.repeat(np.sin(tw)[:, None, :], 4, 1).astype(np.float32)
    return {"G": G.astype(bf), "F64": F64.astype(bf), "C128": C128.astype(bf), "S128": S128.astype(bf), "tc": tc, "ts": ts}


def build_f():
    nc = new_nc()
    L = 8192
    xT = nc.dram_tensor("xT", [256, L], BF16, kind="ExternalInput").ap()
    Gd = nc.dram_tensor("G", [256, 512], BF16, kind="ExternalInput").ap()
    F64d = nc.dram_tensor("F64", [128, 128], BF16, kind="ExternalInput").ap()
    C128d = nc.dram_tensor("C128", [128, 128], BF16, kind="ExternalInput").ap()
    S128d = nc.dram_tensor("S128", [128, 128], BF16, kind="ExternalInput").ap()
    tcd = nc.dram_tensor("tc", [128, 4, 64], F32, kind="ExternalInput").ap()
    tsd = nc.dram_tensor("ts", [128, 4, 64], F32, kind="ExternalInput").ap()
    fo = nc.dram_tensor("f", [L, 256], BF16, kind="ExternalOutput").ap()
    scale = 1.0 / math.sqrt(L * 256.0)
    with ExitStack() as es:
        k = KB(nc, es)
        xs = k.sb([128, 2, L], BF16); d_x = Dep()
        G = k.sb([128, 2, 512], BF16); d_G = Dep()
        F64 = k.sb([128, 128], BF16)
        C128 = k.sb([128, 128], BF16)
        S128 = k.sb([128, 128], BF16)
        tc = k.sb([128, 4, 64], F32)
        ts = k.sb([128, 4, 64], F32)
        d_c = Dep()
        Wl = k.sb([128, 128, 256], BF16); d_W = Dep()
        Tp = k.sb([128, 2, 64, 256], BF16); d_T = Dep()
        k.dma("sp", xs[:], xT.rearrange("(c p) l -> p c l", p=128), writes=[d_x])
        k.dma("sp", G[:], Gd.rearrange("(c p) n -> p c n", p=128), writes=[d_G])
        for t_, s_ in ((F64, F64d), (C128, C128d), (S128, S128d), (tc, tcd), (ts, tsd)):
            k.dma("sp", t_[:], s_, writes=[d_c])
        p0 = [k.ps([128, 512]) for _ in range(2)]
        d_p0 = [Dep(), Dep()]
        pA = [k.ps([128, 512]) for _ in range(2)]
        d_pA = [Dep(), Dep()]
        pB = [k.ps([128, 512]) for _ in range(2)]
        d_pB = [Dep(), Dep()]
        xv = xs[:].rearrange("p c (a b) -> p c b a", b=128)
        for l2 in range(128):
            j = (l2 // 2) % 2
            col = (l2 % 2) * 256
            for ri in range(2):
                for cc in range(2):
                    k.op("pe", lambda e: e.matmul(p0[j][ri * 64:(ri + 1) * 64, col:col + 256], lhsT=xv[:, cc, l2, :], rhs=G[:, cc, ri * 256:(ri + 1) * 256],
                                                  start=(cc == 0), stop=(cc == 1)), reads=[d_x, d_G], writes=[d_p0[j]])
            if l2 % 2 == 1:
                eng = "act" if (l2 // 2) % 2 == 0 else "dve"
                if eng == "act":
                    k.op("act", lambda e: e.activation(out=Wl[:, l2 - 1:l2 + 1, :].rearrange("p a m -> p (a m)"), in_=p0[j][:], func=AF.Copy), reads=[d_p0[j]], writes=[d_W])
                else:
                    k.op("dve", lambda e: e.tensor_copy(out=Wl[:, l2 - 1:l2 + 1, :].rearrange("p a m -> p (a m)"), in_=p0[j][:]), reads=[d_p0[j]], writes=[d_W])
        ta = [k.sb([128, 4, 64], F32) for _ in range(2)]
        tb_ = [k.sb([128, 4, 64], F32) for _ in range(2)]
        d_ta = [Dep(), Dep()]
        d_tb = [Dep(), Dep()]
        for g in range(64):
            j = g % 2
            for q in range(4):
                m = 4 * g + q
                k.op("pe", lambda e: e.matmul(pA[j][:, q * 128:(q + 1) * 128], lhsT=Wl[:, :, m], rhs=F64[:], start=True, stop=True),
                     reads=[d_W, d_c], writes=[d_pA[j]])
            pv = pA[j][:].rearrange("p (q r k) -> p q r k", q=4, r=2)
            Tre, Tim = pv[:, :, 0, :], pv[:, :, 1, :]
            o_re = Tp[:, 0, :, 4 * g:4 * g + 4].rearrange("p k m -> p m k")
            o_im = Tp[:, 1, :, 4 * g:4 * g + 4].rearrange("p k m -> p m k")
            k.op("dve", lambda e: e.tensor_tensor(out=ta[j][:], in0=Tre, in1=tc[:], op=ALU.mult), reads=[d_pA[j], d_c], writes=[d_ta[j]])
            k.op("dve", lambda e: e.tensor_tensor(out=tb_[j][:], in0=Tim, in1=ts[:], op=ALU.mult), reads=[d_pA[j], d_c], writes=[d_tb[j]])
            k.op("pool", lambda e: e.tensor_tensor(out=o_re, in0=ta[j][:], in1=tb_[j][:], op=ALU.add), reads=[d_ta[j], d_tb[j]], writes=[d_T])
            k.op("dve", lambda e: e.tensor_tensor(out=ta[j][:], in0=Tim, in1=tc[:], op=ALU.mult), reads=[d_pA[j], d_c], writes=[d_ta[j]])
            k.op("dve", lambda e: e.tensor_tensor(out=tb_[j][:], in0=Tre, in1=ts[:], op=ALU.mult), reads=[d_pA[j], d_c], writes=[d_tb[j]])
            k.op("pool", lambda e: e.tensor_tensor(out=o_im, in0=ta[j][:], in1=tb_[j][:], op=ALU.subtract), reads=[d_ta[j], d_tb[j]], writes=[d_T])
        ob = [k.sb([128, 512], BF16) for _ in range(2)]
        d_ob = [Dep(), Dep()]
        fv = fo.rearrange("(k2 k1) m -> k2 k1 m", k1=64)
        Tf = Tp[:].rearrange("p r k m -> p r (k m)")
        for blk_i in range(32):
            j = blk_i % 2
            cs = slice(blk_i * 512, (blk_i + 1) * 512)
            k.op("pe", lambda e: e.matmul(pB[j][:], lhsT=C128[:], rhs=Tf[:, 0, cs], start=True, stop=False), reads=[d_T, d_c], writes=[d_pB[j]])
            k.op("pe", lambda e: e.matmul(pB[j][:], lhsT=S128[:], rhs=Tf[:, 1, cs], start=False, stop=True), reads=[d_T, d_c], writes=[d_pB[j]])
            k.op("act", lambda e: e.activation(out=ob[j][:], in_=pB[j][:], func=AF.Copy, scale=scale), reads=[d_pB[j]], writes=[d_ob[j]])
            k.dma("sp", fv[:, 2 * blk_i:2 * blk_i + 2, :], ob[j][:].rearrange("p (k m) -> p k m", k=2), reads=[d_ob[j]], writes=[Dep()])
        k.finish([])
    return nc


D_MODEL = 1024
COL_Q, COL_Z, COL_C, COL_K, COL_V, COL_XB, COL_DT = 0, 1024, 2048, 2304, 3328, 4352, 5632


def _b_inputs(inp, j):
    g = j // 4
    w = inp["w_in"][0]
    dtc = [COL_DT + d * 16 + 2 * j + hd for d in range(2) for hd in range(2)]
    cols = [w[:, COL_Q + j * 128: COL_Q + (j + 1) * 128], w[:, COL_K + j * 128: COL_K + (j + 1) * 128],
            w[:, COL_V + j * 128:COL_V + (j + 1) * 128], w[:, dtc],
            w[:, COL_Z + j * 128:COL_Z + (j + 1) * 128], w[:, COL_XB + j * 128:COL_XB + (j + 1) * 128],
            w[:, COL_XB + 1024 + g * 128:COL_XB + 1024 + (g + 1) * 128], w[:, COL_C + g * 128:COL_C + (g + 1) * 128]]
    W = np.ascontiguousarray(np.concatenate(cols, 1))
    pp = np.zeros((128, 32), np.float32)
    p = np.arange(128)
    pp[:, 0] = inp["q_norm_g"][0][p % 64]
    pp[:, 1] = inp["k_norm_g"][0][p % 64]
    pp[:, 2] = inp["da_subln_g"][0]
    pp[:, 3:8] = inp["conv_xb_w"][0][:, j * 128:(j + 1) * 128].T
    pp[:, 8] = inp["conv_xb_b"][0][j * 128:(j + 1) * 128]
    pp[:, 9:14] = inp["conv_xb_w"][0][:, 1024 + g * 128:1024 + (g + 1) * 128].T
    pp[:, 14] = inp["conv_xb_b"][0][1024 + g * 128:1024 + (g + 1) * 128]
    pp[:, 15:20] = inp["conv_c_w"][0][:, g * 128:(g + 1) * 128].T
    pp[:, 20] = inp["conv_c_b"][0][g * 128:(g + 1) * 128]
    pp[:, 21] = inp["d_skip"][0][2 * j + p // 64]
    pp[:, 22] = inp["ssm_norm_g"][0][j * 128:(j + 1) * 128]
    p4 = np.zeros((2, 4), np.float32)
    for d in range(2):
        for hd in range(2):
            p4[0, d * 2 + hd] = inp["dt_bias"][0][d, 2 * j + hd]
            p4[1, d * 2 + hd] = inp["a_log"][0][d, 2 * j + hd]
    return {"W": W, "pp": pp, "p4": p4, "lam4": np.ascontiguousarray(inp["da_lambda"][0])}


def _moe_stage(inp, layer, h2T, gatesT):
    nc = build_d(16384)
    in_maps = []
    for j in range(NCORES):
        d = d_inputs(inp, layer, j)
        d["h2T"] = h2T
        d["gT"] = np.ascontiguousarray(gatesT[4 * j:4 * j + 4])
        in_maps.append(d)
    res = run_spmd(nc, in_maps)
    return [res[j]["part"] for j in range(NCORES)]


def kernel(**inp):
    inp = {k_: np.asarray(v) for k_, v in inp.items()}
    ident = np.eye(128, dtype=np.float32)
    TPC = 2048
    x0 = np.ascontiguousarray(inp["x"].reshape(16384, 1024))
    mod = run_l0(inp)
    modr = mod.reshape(2, 3, 6, 1024)
    nc = build_t1(front=None, norm=(0, 1), h_layout="T", router=False, out_x=False)
    res = run_spmd(nc, [{"x": x0[r * TPC:(r + 1) * TPC], "mod": np.ascontiguousarray(modr[0, r // 4]), "ident": ident,
                         "ng": inp["norm_g"][0, 0]} for r in range(NCORES)])
    hT0 = np.concatenate([res[r]["hT"] for r in range(NCORES)], 1)
    ctxf = np.ascontiguousarray(inp["ctx"].reshape(512, 1024))
    nc = build_t1(front=None, norm=(0, 1), h_layout="T", router=False, out_x=False, T=128)
    res = run_spmd(nc, [{"x": ctxf[(r % 4) * 128:(r % 4 + 1) * 128], "mod": np.ascontiguousarray(modr[0, 2]), "ident": ident,
                         "ng": inp["norm_g"][0, 0]} for r in range(NCORES)])
    hcT = np.concatenate([res[r]["hT"] for r in range(4)], 1)
    lam_init = 0.8 - 0.6 * math.exp(-0.3 * 0)
    cst = b_consts()
    cst["tri"] = np.triu(np.ones((128, 128), np.float32))
    nc = build_b(lam_init)
    in_maps = []
    for j in range(NCORES):
        d = {"hT": hT0, "hcT": hcT}
        d.update(_b_inputs(inp, j))
        d.update(cst)
        in_maps.append(d)
    res = run_spmd(nc, in_maps)
    oT = np.concatenate([res[j]["oT"] for j in range(NCORES)], 0)
    ygT = np.concatenate([res[j]["ygT"] for j in range(NCORES)], 0)
    ss = np.ascontiguousarray(np.concatenate([res[j]["ss"] for j in range(NCORES)], 0).T)
    del res
    nc = build_t1(front="mix", nA=8, nB=8, gate_row=2, norm=(3, 4), h_layout="T", router=True, out_x=True)
    in_maps = []
    for r in range(NCORES):
        ts_ = slice(r * TPC, (r + 1) * TPC)
        in_maps.append({"x": x0[ts_], "mod": np.ascontiguousarray(modr[0, r // 4]), "ident": ident,
                        "mixA": np.ascontiguousarray(oT[:, ts_]), "wA": np.ascontiguousarray(inp["w_out"][0][0:1024]),
                        "mixB": np.ascontiguousarray(ygT[:, ts_]), "wB": np.ascontiguousarray(inp["w_out"][0][1024:2048]),
                        "ss": np.ascontiguousarray(ss[ts_]), "ng": inp["norm_g"][0, 1],
                        "wr": inp["w_router"][0], "br": inp["b_router"][0]})
    res = run_spmd(nc, in_maps)
    x1 = [res[r]["xo"] for r in range(NCORES)]
    h2T = np.concatenate([res[r]["hT"] for r in range(NCORES)], 1)
    gatesT = np.concatenate([res[r]["gatesT"] for r in range(NCORES)], 1)
    parts = _moe_stage(inp, 0, h2T, gatesT)
    nc = build_t1(front="parts", gate_row=5, norm=(0, 1), h_layout="T", router=False, out_x=True)
    in_maps = []
    for r in range(NCORES):
        ts_ = slice(r * TPC, (r + 1) * TPC)
        in_maps.append({"x": x1[r], "mod": np.ascontiguousarray(np.concatenate([modr[1, r // 4][0:5], modr[0, r // 4][5:6]], 0)),
                        "ident": ident, "parts": np.ascontiguousarray(np.stack([parts[j][ts_] for j in range(NCORES)], 0)),
                        "ng": inp["norm_g"][1, 0]})
    res = run_spmd(nc, in_maps)
    del parts
    x2 = [res[r]["xo"] for r in range(NCORES)]
    hT1 = np.concatenate([res[r]["hT"] for r in range(NCORES)], 1)
    fc = f_consts()
    nc = build_f()
    in_maps = []
    for r in range(NCORES):
        b, g = r // 4, r % 4
        d = dict(fc)
        d["xT"] = np.ascontiguousarray(hT1[g * 256:(g + 1) * 256, b * 8192:(b + 1) * 8192])
        in_maps.append(d)
    res = run_spmd(nc, in_maps)
    fT = np.concatenate([np.concatenate([res[b * 4 + g]["f"].T for g in range(4)], 0) for b in range(2)], 1)
    nc = build_t1(front="mix", nA=8, nB=0, gate_row=2, norm=(3, 4), h_layout="T", router=True, out_x=True)
    in_maps = []
    for r in range(NCORES):
        ts_ = slice(r * TPC, (r + 1) * TPC)
        in_maps.append({"x": x2[r], "mod": np.ascontiguousarray(modr[1, r // 4]), "ident": ident,
                        "mixA": np.ascontiguousarray(fT[:, ts_]), "wA": inp["w_fourier"][0], "ng": inp["norm_g"][1, 1],
                        "wr": inp["w_router"][1], "br": inp["b_router"][1]})
    res = run_spmd(nc, in_maps)
    x3 = [res[r]["xo"] for r in range(NCORES)]
    h2T = np.concatenate([res[r]["hT"] for r in range(NCORES)], 1)
    gatesT = np.concatenate([res[r]["gatesT"] for r in range(NCORES)], 1)
    parts = _moe_stage(inp, 1, h2T, gatesT)
    nc = build_t1(front="parts", gate_row=5, norm=None, router=False, out_x=True)
    in_maps = []
    for r in range(NCORES):
        ts_ = slice(r * TPC, (r + 1) * TPC)
        in_maps.append({"x": x3[r], "mod": np.ascontiguousarray(modr[1, r // 4]), "ident": ident,
                        "parts": np.ascontiguousarray(np.stack([parts[j][ts_] for j in range(NCORES)], 0))})
    res = run_spmd(nc, in_maps)
    out = np.concatenate([res[r]["xo"] for r in range(NCORES)], 0).reshape(2, 8192, 1024)
    return out.astype(np.float32)
```

```python
import math
from contextlib import ExitStack

import numpy as np
import concourse.bass as bass
import concourse.mybir as mybir
from concourse.bass_utils import run_bass_kernel_spmd

F32 = mybir.dt.float32
BF16 = mybir.dt.bfloat16
AF = mybir.ActivationFunctionType
ALU = mybir.AluOpType
AX = mybir.AxisListType
NCORES = 8


class Dep:
    __slots__ = ("w", "r")

    def __init__(self):
        self.w = None
        self.r = {}


class KB:
    DMA_RING = 8

    def __init__(self, nc, es):
        self.nc = nc
        self.es = es
        self.root_es = es
        self.E = {"pe": nc.tensor, "act": nc.scalar, "dve": nc.vector, "pool": nc.gpsimd, "sp": nc.sync}
        self.sem = {}
        for e in ("pe", "act", "dve", "pool"):
            self.sem[e] = es.enter_context(nc.semaphore(e))
        self.cnt = {e: 0 for e in ("pe", "act", "dve", "pool")}
        self.seen = {e: {} for e in self.E}
        self.dmaq = {}
        self.ntile = 0

    def sb(self, shape, dt, name=None):
        self.ntile += 1
        return self.es.enter_context(self.nc.sbuf_tensor(name or f"t{self.ntile}", list(shape), dt))

    def ps(self, shape, dt=F32, name=None):
        self.ntile += 1
        return self.es.enter_context(self.nc.psum_tensor(name or f"p{self.ntile}", list(shape), dt))

    def _wait(self, e, key, val):
        if self.seen[e].get(key, 0) >= val:
            return
        self.E[e].wait_ge(self.sem[key], val)
        self.seen[e][key] = val

    def _deps(self, e, reads, writes):
        need = {}
        for d in reads:
            if d.w:
                k, v = d.w
                need[k] = max(need.get(k, 0), v)
        for d in writes:
            if d.w:
                k, v = d.w
                need[k] = max(need.get(k, 0), v)
            for k, v in d.r.items():
                need[k] = max(need.get(k, 0), v)
        for k, v in need.items():
            if k == "pe" and e == "pe":
                continue
            self._wait(e, k, v)

    def op(self, e, fn, reads=(), writes=()):
        self._deps(e, reads, writes)
        inst = fn(self.E[e])
        self.cnt[e] += 1
        v = self.cnt[e]
        inst.then_inc(self.sem[e], 1)
        for d in reads:
            d.r[e] = v
        for d in writes:
            d.w = (e, v)
            d.r = {}
        return inst

    def dma(self, q, out, in_, reads=(), writes=(), **kw):
        ring = self.dmaq.setdefault(q, {"n": 0, "keys": []})
        n = ring["n"]
        slot = n % self.DMA_RING
        if len(ring["keys"]) <= slot:
            key = f"dma_{q}_{slot}"
            self.sem[key] = self.root_es.enter_context(self.nc.semaphore(key))
            ring["keys"].append(key)
        key = ring["keys"][slot]
        val = 16 * (n // self.DMA_RING + 1)
        if n >= self.DMA_RING:
            self._wait(q, key, val - 16)
        self._deps(q, reads, writes)
        inst = self.E[q].dma_start(out=out, in_=in_, **kw)
        inst.then_inc(self.sem[key], 16)
        ring["n"] += 1
        for d in reads:
            d.r[key] = val
        for d in writes:
            d.w = (key, val)
            d.r = {}
        return inst

    def coll(self, kind, op, ins, outs, reads=(), writes=()):
        q = "pool"
        ring = self.dmaq.setdefault("coll", {"n": 0, "keys": []})
        if not ring["keys"]:
            self.sem["coll"] = self.root_es.enter_context(self.nc.semaphore("coll"))
            ring["keys"].append("coll")
        n = ring["n"]
        val = 16 * (n + 1)
        if n >= 1:
            self._wait(q, "coll", val - 16)
        self._deps(q, reads, writes)
        inst = self.nc.gpsimd.collective_compute(kind, op, replica_groups=[list(range(NCORES))], ins=ins, outs=outs)
        inst.then_inc(self.sem["coll"], 16)
        ring["n"] += 1
        for d in reads:
            d.r["coll"] = val
        for d in writes:
            d.w = ("coll", val)
            d.r = {}
        return inst

    def finish(self, deps):
        for d in deps:
            if d.w:
                self._wait("sp", d.w[0], d.w[1])
        for q, ring in self.dmaq.items():
            n = ring["n"]
            for slot, key in enumerate(ring["keys"]):
                uses = n if q == "coll" else (n - slot + self.DMA_RING - 1) // self.DMA_RING
                if uses > 0:
                    self._wait("sp", key, 16 * uses)


def new_nc():
    return bass.Bass("TRN2", target_bir_lowering=False)


def run_spmd(nc, in_maps):
    import time, sys
    t0 = time.time()
    res = run_bass_kernel_spmd(nc, in_maps, core_ids=list(range(NCORES)))
    print(f'[launch] {time.time() - t0:.1f}s', file=sys.stderr, flush=True)
    return res.results


def build_l0():
    nc = new_nc()
    adaw = nc.dram_tensor("adaw", [2, 1024, 768], F32, kind="ExternalInput").ap()
    adab = nc.dram_tensor("adab", [128, 12], F32, kind="ExternalInput").ap()
    cT = nc.dram_tensor("cT", [1024, 3], F32, kind="ExternalInput").ap()
    out = nc.dram_tensor("modp", [128, 36], F32, kind="ExternalOutput").ap()
    with ExitStack() as es:
        k = KB(nc, es)
        w_sb = k.sb([128, 2, 8, 768], F32)
        b_sb = k.sb([128, 12], F32)
        c_sb = k.sb([128, 8, 3], F32)
        sc_sb = k.sb([128, 8, 3], F32)
        o_sb = k.sb([128, 12, 3], F32)
        pp = k.ps([128, 512], F32)
        dw = [Dep(), Dep()]
        db, dc, dsc, dps, do = Dep(), Dep(), Dep(), Dep(), Dep()
        for l in range(2):
            k.dma("sp", w_sb[:, l], adaw[l].rearrange("(k p) m -> p k m", p=128), writes=[dw[l]])
        k.dma("sp", b_sb[:], adab, writes=[db])
        k.dma("sp", c_sb[:], cT.rearrange("(k p) v -> p k v", p=128), writes=[dc])
        k.op("act", lambda e: e.activation(out=sc_sb[:], in_=c_sb[:], func=AF.Silu), reads=[dc], writes=[dsc])
        for l in range(2):
            for mc in range(6):
                for kc in range(8):
                    k.op("pe", lambda e: e.matmul(pp[:, (l * 6 + mc) * 3:(l * 6 + mc) * 3 + 3],
                                                  lhsT=w_sb[:, l, kc, mc * 128:(mc + 1) * 128],
                                                  rhs=sc_sb[:, kc, :], start=(kc == 0), stop=(kc == 7)),
                         reads=[dw[l], dsc], writes=[dps])
        for v in range(3):
            k.op("dve", lambda e: e.tensor_tensor(out=o_sb[:, :, v], in0=pp[:, 0:36].rearrange("p (a v) -> p a v", v=3)[:, :, v],
                                                  in1=b_sb[:], op=ALU.add), reads=[dps, db], writes=[do])
        k.dma("sp", out, o_sb[:].rearrange("p a v -> p (a v)"), reads=[do], writes=[Dep()])
        k.finish([])
    return nc


def run_l0(inp):
    ada_w, ada_b = inp["ada_w"], inp["ada_b"]
    cT = np.ascontiguousarray(np.concatenate([inp["c"], inp["c_ctx"][None]], 0).T)
    in_maps = []
    for j in range(NCORES):
        sl = slice(768 * j, 768 * (j + 1))
        adab = ada_b[:, sl].reshape(2, 6, 128).transpose(2, 0, 1).reshape(128, 12)
        in_maps.append({"adaw": np.ascontiguousarray(ada_w[:, :, sl]), "adab": np.ascontiguousarray(adab), "cT": cT})
    res = run_spmd(build_l0(), in_maps)
    mod = np.zeros((2, 3, 6144), np.float32)
    for j in range(NCORES):
        o = res[j]["modp"].reshape(128, 2, 6, 3)
        mod[:, :, 768 * j:768 * (j + 1)] = o.transpose(1, 3, 2, 0).reshape(2, 3, 768)
    return mod


RMS_EPS = 1e-6


def build_t1(front, nA=0, nB=0, gate_row=2, norm=None, h_layout="T", router=False, out_x=True, T=2048):
    nc = new_nc()
    NT = T // 128
    x = nc.dram_tensor("x", [T, 1024], F32, kind="ExternalInput").ap()
    mod = nc.dram_tensor("mod", [6, 1024], F32, kind="ExternalInput").ap()
    ident_d = nc.dram_tensor("ident", [128, 128], F32, kind="ExternalInput").ap()
    if front == "mix":
        mixA = nc.dram_tensor("mixA", [nA * 128, T], BF16, kind="ExternalInput").ap()
        wA = nc.dram_tensor("wA", [nA * 128, 1024], F32, kind="ExternalInput").ap()
        if nB:
            mixB = nc.dram_tensor("mixB", [nB * 128, T], BF16, kind="ExternalInput").ap()
            wB = nc.dram_tensor("wB", [nB * 128, 1024], F32, kind="ExternalInput").ap()
            ssd = nc.dram_tensor("ss", [T, 8], F32, kind="ExternalInput").ap()
    elif front == "parts":
        parts = nc.dram_tensor("parts", [8, T, 1024], BF16, kind="ExternalInput").ap()
    if norm is not None:
        ng = nc.dram_tensor("ng", [1024], F32, kind="ExternalInput").ap()
        if h_layout == "T":
            hout = nc.dram_tensor("hT", [T // 128, 128, 1024], BF16, kind="ExternalOutput").ap()
        else:
            hout = nc.dram_tensor("hN", [T, 1024], BF16, kind="ExternalOutput").ap()
    if router:
        wr = nc.dram_tensor("wr", [1024, 32], F32, kind="ExternalInput").ap()
        br = nc.dram_tensor("br", [32], F32, kind="ExternalInput").ap()
        gout = nc.dram_tensor("gatesT", [32, T], F32, kind="ExternalOutput").ap()
    if out_x:
        xout = nc.dram_tensor("xo", [T, 1024], F32, kind="ExternalOutput").ap()

    with ExitStack() as es:
        k = KB(nc, es)
        outd = []
        ident = k.sb([128, 128], F32)
        d_id = Dep()
        k.dma("sp", ident[:], ident_d, writes=[d_id])
        g_bc = k.sb([128, 1024], F32)
        d_g = Dep()
        if front is not None:
            k.dma("sp", g_bc[:], mod[gate_row].partition_broadcast(128), writes=[d_g])
        if norm is not None:
            A_bc = k.sb([128, 1024], F32)
            B_bc = k.sb([128, 1024], F32)
            ng_bc = k.sb([128, 1024], F32)
            d_A, d_B, d_ng = Dep(), Dep(), Dep()
            k.dma("sp", A_bc[:], mod[norm[1]].partition_broadcast(128), writes=[d_A])
            k.dma("sp", B_bc[:], mod[norm[0]].partition_broadcast(128), writes=[d_B])
            k.dma("sp", ng_bc[:], ng.partition_broadcast(128), writes=[d_ng])
            k.op("dve", lambda e: e.scalar_tensor_tensor(out=A_bc[:], in0=A_bc[:], scalar=1.0, in1=ng_bc[:],
                                                         op0=ALU.add, op1=ALU.mult), reads=[d_ng], writes=[d_A])
        if router:
            wr_sb = k.sb([128, 8, 32], F32)
            br_bc = k.sb([128, 32], F32)
            d_wr, d_br = Dep(), Dep()
            k.dma("sp", wr_sb[:], wr.rearrange("(k p) e -> p k e", p=128), writes=[d_wr])
            k.dma("sp", br_bc[:], br.partition_broadcast(128), writes=[d_br])
            wr_hi = k.sb([128, 8, 32], BF16)
            wr_lo = k.sb([128, 8, 32], BF16)
            k.op("dve", lambda e: e.tensor_copy(out=wr_hi[:], in_=wr_sb[:]), reads=[d_wr], writes=[d_wr])
            k.op("dve", lambda e: e.tensor_tensor(out=wr_lo[:], in0=wr_sb[:], in1=wr_hi[:], op=ALU.subtract), reads=[d_wr], writes=[d_wr])
        if front == "mix":
            stg = [k.sb([128, 1024], F32) for _ in range(2)]
            d_stg = [Dep(), Dep()]
            wA_bf = k.sb([128, nA, 1024], BF16)
            d_wA = Dep()
            n = 0
            for c in range(nA):
                k.dma("sp", stg[n % 2][:], wA[c * 128:(c + 1) * 128, :], writes=[d_stg[n % 2]])
                k.op("act", lambda e: e.activation(out=wA_bf[:, c, :], in_=stg[n % 2][:], func=AF.Copy),
                     reads=[d_stg[n % 2]], writes=[d_wA])
                n += 1
            if nB:
                wB_bf = k.sb([128, nB, 1024], BF16)
                d_wB = Dep()
                for c in range(nB):
                    k.dma("sp", stg[n % 2][:], wB[c * 128:(c + 1) * 128, :], writes=[d_stg[n % 2]])
                    k.op("act", lambda e: e.activation(out=wB_bf[:, c, :], in_=stg[n % 2][:], func=AF.Copy),
                         reads=[d_stg[n % 2]], writes=[d_wB])
                    n += 1
        NB_ = 2
        xt = [k.sb([128, 1024], F32) for _ in range(NB_)]
        d_x = [Dep() for _ in range(NB_)]
        tmp = [k.sb([128, 1024], F32) for _ in range(NB_)]
        d_tmp = [Dep() for _ in range(NB_)]
        if front == "mix":
            mA = [k.sb([128, nA, 128], BF16) for _ in range(NB_)]
            d_mA = [Dep() for _ in range(NB_)]
            if nB:
                mB = [k.sb([128, nB, 128], BF16) for _ in range(NB_)]
                d_mB = [Dep() for _ in range(NB_)]
                sst = [k.sb([128, 8], F32) for _ in range(NB_)]
                d_ss = [Dep() for _ in range(NB_)]
                rs = [k.sb([128, 4], F32) for _ in range(NB_)]
                d_rs = [Dep() for _ in range(NB_)]
                asb = [k.sb([128, 512], F32) for _ in range(NB_)]
                d_asb = [Dep() for _ in range(NB_)]
        if front == "parts":
            pt = [k.sb([128, 8, 1024], BF16) for _ in range(NB_)]
            d_pt = [Dep() for _ in range(NB_)]
        if norm is not None:
            junk = [k.sb([128, 1024], BF16) for _ in range(NB_)]
            d_junk = [Dep() for _ in range(NB_)]
            st = [k.sb([128, 4], F32) for _ in range(NB_)]
            d_st = [Dep() for _ in range(NB_)]
            hf = [k.sb([128, 1024], F32) for _ in range(NB_)]
            d_hf = [Dep() for _ in range(NB_)]
            hb = [k.sb([128, 1024], BF16) for _ in range(NB_)]
            d_hb = [Dep() for _ in range(NB_)]
        if router:
            hTf = [k.sb([128, 8, 128], BF16) for _ in range(NB_)]
            d_hTf = [Dep() for _ in range(NB_)]
            lg = [k.sb([128, 32], F32) for _ in range(NB_)]
            d_lg = [Dep() for _ in range(NB_)]
            t8 = [k.sb([128, 8], F32) for _ in range(NB_)]
            d_t8 = [Dep() for _ in range(NB_)]
            ex = [k.sb([128, 32], F32) for _ in range(NB_)]
            d_ex = [Dep() for _ in range(NB_)]
            gt = [k.sb([128, 32], F32) for _ in range(NB_)]
            d_gt = [Dep() for _ in range(NB_)]
            gT = [k.sb([32, 128], F32) for _ in range(NB_)]
            d_gT = [Dep() for _ in range(NB_)]
        pa = [k.ps([128, 512]) for _ in range(2)]
        d_pa = [Dep(), Dep()]
        pb = [k.ps([128, 512]) for _ in range(2)]
        d_pb = [Dep(), Dep()]
        ptr = k.ps([128, 1024])
        d_ptr = Dep()
        plg = k.ps([128, 512])
        d_plg = Dep()

        for t in range(NT):
            b = t % NB_
            tok = slice(t * 128, (t + 1) * 128)
            k.dma("sp", xt[b][:], x[tok, :], writes=[d_x[b]])
            xn, d_xn = xt[b], d_x[b]
            if front == "mix":
                k.dma("sp", mA[b][:], mixA[:, tok].rearrange("(c p) t -> p c t", p=128), writes=[d_mA[b]])
                if nB:
                    k.dma("sp", mB[b][:], mixB[:, tok].rearrange("(c p) t -> p c t", p=128), writes=[d_mB[b]])
                    k.dma("sp", sst[b][:], ssd[tok, :], writes=[d_ss[b]])
                    k.op("dve", lambda e: e.tensor_reduce(out=rs[b][:, 0:1], in_=sst[b][:], axis=AX.X, op=ALU.add),
                         reads=[d_ss[b]], writes=[d_rs[b]])
                    k.op("act", lambda e: e.activation(out=rs[b][:, 1:2], in_=rs[b][:, 0:1], func=AF.Sqrt,
                                                       scale=1.0 / 1024, bias=eps_ap(k)), reads=[d_rs[b]], writes=[d_rs[b]])
                    k.op("dve", lambda e: e.reciprocal(out=rs[b][:, 2:3], in_=rs[b][:, 1:2]), reads=[d_rs[b]], writes=[d_rs[b]])
                for h in range(2):
                    cs = slice(h * 512, (h + 1) * 512)
                    for c in range(nA):
                        k.op("pe", lambda e: e.matmul(pa[h][:], lhsT=mA[b][:, c, :], rhs=wA_bf[:, c, cs],
                                                      start=(c == 0), stop=(c == nA - 1)),
                             reads=[d_mA[b], d_wA], writes=[d_pa[h]])
                    if nB:
                        for c in range(nB):
                            k.op("pe", lambda e: e.matmul(pb[h][:], lhsT=mB[b][:, c, :], rhs=wB_bf[:, c, cs],
                                                          start=(c == 0), stop=(c == nB - 1)),
                                 reads=[d_mB[b], d_wB], writes=[d_pb[h]])
                        k.op("act", lambda e: e.activation(out=asb[b][:], in_=pa[h][:], func=AF.Copy),
                             reads=[d_pa[h]], writes=[d_asb[b]])
                        k.op("dve", lambda e: e.scalar_tensor_tensor(out=tmp[b][:, cs], in0=pb[h][:], scalar=rs[b][:, 2:3],
                                                                     in1=asb[b][:], op0=ALU.mult, op1=ALU.add),
                             reads=[d_pb[h], d_rs[b], d_asb[b]], writes=[d_tmp[b]])
                        k.op("dve", lambda e: e.tensor_tensor(out=tmp[b][:, cs], in0=tmp[b][:, cs], in1=g_bc[:, cs], op=ALU.mult),
                             reads=[d_g], writes=[d_tmp[b]])
                    else:
                        k.op("dve", lambda e: e.tensor_tensor(out=tmp[b][:, cs], in0=pa[h][:], in1=g_bc[:, cs], op=ALU.mult),
                             reads=[d_pa[h], d_g], writes=[d_tmp[b]])
                k.op("dve", lambda e: e.tensor_tensor(out=xt[b][:], in0=xt[b][:], in1=tmp[b][:], op=ALU.add),
                     reads=[d_tmp[b]], writes=[d_x[b]])
            elif front == "parts":
                k.dma("sp", pt[b][:], parts[:, tok, :].rearrange("j t d -> t j d"), writes=[d_pt[b]])
                k.op("dve", lambda e: e.tensor_tensor(out=tmp[b][:], in0=pt[b][:, 0, :], in1=pt[b][:, 1, :], op=ALU.add),
                     reads=[d_pt[b]], writes=[d_tmp[b]])
                for j in range(2, 8):
                    k.op("dve", lambda e: e.tensor_tensor(out=tmp[b][:], in0=tmp[b][:], in1=pt[b][:, j, :], op=ALU.add),
                         reads=[d_pt[b]], writes=[d_tmp[b]])
                k.op("dve", lambda e: e.tensor_tensor(out=tmp[b][:], in0=tmp[b][:], in1=g_bc[:], op=ALU.mult),
                     reads=[d_g], writes=[d_tmp[b]])
                k.op("dve", lambda e: e.tensor_tensor(out=xt[b][:], in0=xt[b][:], in1=tmp[b][:], op=ALU.add),
                     reads=[d_tmp[b]], writes=[d_x[b]])
            if out_x:
                do = Dep()
                k.dma("sp", xout[tok, :], xt[b][:], reads=[d_x[b]], writes=[do])
            if norm is not None:
                k.op("act", lambda e: e.activation(out=junk[b][:], in_=xt[b][:], func=AF.Square, accum_out=st[b][:, 0:1]),
                     reads=[d_x[b]], writes=[d_junk[b], d_st[b]])
                k.op("act", lambda e: e.activation(out=st[b][:, 1:2], in_=st[b][:, 0:1], func=AF.Sqrt,
                                                   scale=1.0 / 1024, bias=eps_ap(k)), reads=[d_st[b]], writes=[d_st[b]])
                k.op("dve", lambda e: e.reciprocal(out=st[b][:, 2:3], in_=st[b][:, 1:2]), reads=[d_st[b]], writes=[d_st[b]])
                k.op("dve", lambda e: e.scalar_tensor_tensor(out=hf[b][:], in0=xt[b][:], scalar=st[b][:, 2:3], in1=A_bc[:],
                                                             op0=ALU.mult, op1=ALU.mult),
                     reads=[d_x[b], d_st[b], d_A], writes=[d_hf[b]])
                k.op("dve", lambda e: e.tensor_tensor(out=hf[b][:], in0=hf[b][:], in1=B_bc[:], op=ALU.add),
                     reads=[d_B], writes=[d_hf[b]])
                if h_layout == "N":
                    k.op("act", lambda e: e.activation(out=hb[b][:], in_=hf[b][:], func=AF.Copy), reads=[d_hf[b]], writes=[d_hb[b]])
                    k.dma("sp", hout[tok, :], hb[b][:], reads=[d_hb[b]], writes=[Dep()])
                else:
                    for c in range(8):
                        k.op("pe", lambda e: e.transpose(out=ptr[:, c * 128:(c + 1) * 128], in_=hf[b][:, c * 128:(c + 1) * 128],
                                                         identity=ident[:]), reads=[d_hf[b], d_id], writes=[d_ptr])
                    k.op("dve", lambda e: e.tensor_copy(out=hb[b][:], in_=ptr[:]), reads=[d_ptr], writes=[d_hb[b]])
                    if router:
                        k.op("dve", lambda e: e.tensor_tensor(out=hTf[b][:].rearrange("p c t -> p (c t)"), in0=ptr[:], in1=hb[b][:], op=ALU.subtract),
                             reads=[d_ptr, d_hb[b]], writes=[d_hTf[b]])
                    k.dma("sp", hout[t], hb[b][:], reads=[d_hb[b]], writes=[Dep()])
                import os
                RL = int(os.environ.get("RL", "9"))
                if router and RL >= 1:
                    hbv = hb[b][:].rearrange("p (c t) -> p c t", c=8)
                    for c in range(8):
                        k.op("pe", lambda e: e.matmul(plg[:, 0:32], lhsT=hbv[:, c, :], rhs=wr_hi[:, c, :],
                                                      start=(c == 0), stop=False), reads=[d_hb[b], d_wr], writes=[d_plg])
                        k.op("pe", lambda e: e.matmul(plg[:, 0:32], lhsT=hbv[:, c, :], rhs=wr_lo[:, c, :],
                                                      start=False, stop=False), reads=[d_hb[b], d_wr], writes=[d_plg])
                        k.op("pe", lambda e: e.matmul(plg[:, 0:32], lhsT=hTf[b][:, c, :], rhs=wr_hi[:, c, :],
                                                      start=False, stop=(c == 7)), reads=[d_hTf[b], d_wr], writes=[d_plg])
                    k.op("dve", lambda e: e.tensor_tensor(out=lg[b][:], in0=plg[:, 0:32], in1=br_bc[:], op=ALU.add),
                         reads=[d_plg, d_br], writes=[d_lg[b]])
                if router and RL >= 2:
                    k.op("dve", lambda e: e.max(out=t8[b][:], in_=lg[b][:]), reads=[d_lg[b]], writes=[d_t8[b]])
                    k.op("dve", lambda e: e.tensor_scalar(out=ex[b][:], in0=lg[b][:], scalar1=t8[b][:, 0:1], scalar2=None,
                                                          op0=ALU.subtract), reads=[d_lg[b], d_t8[b]], writes=[d_ex[b]])
                    k.op("act", lambda e: e.activation(out=ex[b][:], in_=ex[b][:], func=AF.Exp), reads=[], writes=[d_ex[b]])
                    k.op("dve", lambda e: e.scalar_tensor_tensor(out=gt[b][:], in0=lg[b][:], scalar=t8[b][:, 3:4], in1=ex[b][:],
                                                                 op0=ALU.is_ge, op1=ALU.mult),
                         reads=[d_lg[b], d_t8[b], d_ex[b]], writes=[d_gt[b]])
                    k.op("dve", lambda e: e.tensor_reduce(out=t8[b][:, 4:5], in_=gt[b][:], axis=AX.X, op=ALU.add),
                         reads=[d_gt[b]], writes=[d_t8[b]])
                    k.op("dve", lambda e: e.reciprocal(out=t8[b][:, 5:6], in_=t8[b][:, 4:5]), reads=[], writes=[d_t8[b]])
                    k.op("dve", lambda e: e.tensor_scalar(out=gt[b][:], in0=gt[b][:], scalar1=t8[b][:, 5:6], scalar2=None,
                                                          op0=ALU.mult), reads=[d_t8[b]], writes=[d_gt[b]])
                if router and RL >= 3:
                    k.op("pe", lambda e: e.transpose(out=plg[0:32, 128:256], in_=gt[b][:], identity=ident[:]),
                         reads=[d_gt[b], d_id], writes=[d_plg])
                    k.op("dve", lambda e: e.tensor_copy(out=gT[b][:], in_=plg[0:32, 128:256]), reads=[d_plg], writes=[d_gT[b]])
                    k.dma("sp", gout[:, tok], gT[b][:], reads=[d_gT[b]], writes=[Dep()])
        k.finish([])
    return nc


def eps_ap(k):
    if not hasattr(k, "_eps"):
        k._eps = k.sb([128, 1], F32)
        k._eps_d = Dep()
        k.op("pool", lambda e: e.memset(k._eps[:], RMS_EPS), writes=[k._eps_d])
        for eng in ("act", "dve", "pe"):
            k._wait(eng, "pool", k.cnt["pool"])
    return k._eps[:]


def kb_barrier(k):
    cur = {}
    for e in ("pe", "act", "dve", "pool"):
        if k.cnt[e]:
            cur[e] = k.cnt[e]
    for q, ring in k.dmaq.items():
        n = ring["n"]
        for slot, key in enumerate(ring["keys"]):
            uses = n if q == "coll" else (n - slot + k.DMA_RING - 1) // k.DMA_RING
            if uses > 0:
                cur[key] = 16 * uses
    for e in ("pe", "act", "dve", "pool", "sp"):
        for key, v in cur.items():
            if key == e and e in ("pe", "sp"):
                continue
            k._wait(e, key, v)


KB.barrier = kb_barrier

SEQ = 8192
CTX = 256
SA = SEQ + CTX


def b_consts(SEQ=SEQ):
    p = np.arange(128)
    d = p % 64
    hi = d // 32
    dp = d % 32
    i = dp % 16
    freq = (10000.0 ** (-(i.astype(np.float32)) / 16)).astype(np.float32)
    t = np.arange(SEQ)
    pos = np.where(hi[:, None] == 0, (t // 64)[None, :], (t % 64)[None, :]).astype(np.float32)
    ang = pos * freq[:, None]
    cos = np.cos(ang).astype(np.float32)
    sin = np.sin(ang).astype(np.float32)
    prot = np.zeros((128, 128), np.float32)
    for m in range(128):
        if dp[m] < 16:
            prot[m + 16, m] = -1.0
        else:
            prot[m - 16, m] = 1.0
    blk = np.zeros((128, 128), np.float32)
    blk[:64, :64] = 1.0 / 64
    blk[64:, 64:] = 1.0 / 64
    masks = np.zeros((8, 128, 512), np.float32)
    tl = np.arange(512)[None, :]
    for q in range(4):
        s = (128 * q + np.arange(128))[:, None]
        masks[q] = np.where(s <= tl, 0.0, -30000.0)
        masks[4 + q] = np.where(s >= tl, 0.0, -30000.0)
    return {"cos": cos, "sin": sin, "prot": prot, "blk": blk, "masks": masks,
            "ident": np.eye(128, dtype=np.float32)}


def build_b(lam_init, nbatch=2, SEQ=SEQ, phases="PAS"):
    nc = new_nc()
    SA = SEQ + CTX
    NTOK = 2 * SEQ
    hT = nc.dram_tensor("hT", [NTOK // 512, 128, 8, 512], BF16, kind="ExternalInput").ap()
    hcT = nc.dram_tensor("hcT", [2, 128, 8, CTX], BF16, kind="ExternalInput").ap()
    W = nc.dram_tensor("W", [1024, 900], F32, kind="ExternalInput").ap()
    pp = nc.dram_tensor("pp", [128, 32], F32, kind="ExternalInput").ap()
    lamd = nc.dram_tensor("lam4", [4, 64], F32, kind="ExternalInput").ap()
    p4 = nc.dram_tensor("p4", [2, 4], F32, kind="ExternalInput").ap()
    cosd = nc.dram_tensor("cos", [128, SEQ], F32, kind="ExternalInput").ap()
    sind = nc.dram_tensor("sin", [128, SEQ], F32, kind="ExternalInput").ap()
    protd = nc.dram_tensor("prot", [128, 128], F32, kind="ExternalInput").ap()
    blkd = nc.dram_tensor("blk", [128, 128], F32, kind="ExternalInput").ap()
    trid = nc.dram_tensor("tri", [128, 128], F32, kind="ExternalInput").ap()
    maskd = nc.dram_tensor("masks", [8, 128, 512], F32, kind="ExternalInput").ap()
    identd = nc.dram_tensor("ident", [128, 128], F32, kind="ExternalInput").ap()
    oatt = nc.dram_tensor("oT", [128, NTOK], BF16, kind="ExternalOutput").ap()
    ygo = nc.dram_tensor("ygT", [128, NTOK], BF16, kind="ExternalOutput").ap()
    sso = nc.dram_tensor("ss", [1, NTOK], F32, kind="ExternalOutput").ap()
    urow = nc.dram_tensor("urow", [4, SEQ], F32, kind="Internal").ap()
    WQ, WK, WV, WDT, WZ, WX, WBm, WC = 0, 128, 256, 384, 388, 516, 644, 772
    NS = SA // 128
    NL = SEQ // 128
    BUFW = SEQ + 276
    LATR, CTXR = 6, SEQ + 14
    LATC, CTXC = 2, SEQ + 10

    def ccol(s):
        return LATC + 128 * s if s < NL else CTXC + 128 * (s - NL)

    with ExitStack() as es:
        k = KB(nc, es)
        eps = eps_ap(k)
        ident = k.sb([128, 128], F32)
        identb = k.sb([128, 128], BF16)
        prot = k.sb([128, 128], BF16)
        blk = k.sb([128, 128], BF16)
        ones = k.sb([128, 128], BF16)
        onesf = k.sb([128, 128], F32)
        tri = k.sb([128, 128], F32)
        pps = k.sb([128, 32], F32)
        lam = k.sb([128, 8], F32)
        lam_in = k.sb([128, 4, 64], F32)
        p4s = k.sb([128, 2, 4], F32)
        Wb = k.sb([128, 8, 900], BF16)
        dC = Dep()
        with ExitStack() as es2:
            k.es = es2
            stg = k.sb([128, 900], F32)
            k.dma("sp", ident[:], identd, writes=[dC])
            k.dma("sp", tri[:], trid, writes=[dC])
            k.dma("sp", stg[:, 0:128], protd, writes=[dC])
            k.op("dve", lambda e: e.tensor_copy(out=prot[:], in_=stg[:, 0:128]), reads=[dC], writes=[dC])
            k.dma("sp", stg[:, 0:128], blkd, writes=[dC])
            k.op("dve", lambda e: e.tensor_copy(out=blk[:], in_=stg[:, 0:128]), reads=[dC], writes=[dC])
            k.op("dve", lambda e: e.tensor_copy(out=identb[:], in_=ident[:]), reads=[dC], writes=[dC])
            k.op("dve", lambda e: e.memset(ones[:], 1.0), writes=[dC])
            k.op("dve", lambda e: e.memset(onesf[:], 1.0), writes=[dC])
            k.dma("sp", pps[:], pp, writes=[dC])
            k.dma("sp", p4s[:].rearrange("p a r -> p (a r)"), p4.rearrange("a r -> (a r)").partition_broadcast(128), writes=[dC])
            k.dma("sp", lam_in[:].rearrange("p a d -> p (a d)"), lamd.rearrange("a d -> (a d)").partition_broadcast(128), writes=[dC])
            k.op("dve", lambda e: e.tensor_tensor(out=lam_in[:, 0, :], in0=lam_in[:, 0, :], in1=lam_in[:, 1, :], op=ALU.mult), reads=[dC], writes=[dC])
            k.op("dve", lambda e: e.tensor_tensor(out=lam_in[:, 2, :], in0=lam_in[:, 2, :], in1=lam_in[:, 3, :], op=ALU.mult), reads=[dC], writes=[dC])
            k.op("dve", lambda e: e.tensor_reduce(out=lam[:, 0:1], in_=lam_in[:, 0, :], axis=AX.X, op=ALU.add), reads=[dC], writes=[dC])
            k.op("dve", lambda e: e.tensor_reduce(out=lam[:, 1:2], in_=lam_in[:, 2, :], axis=AX.X, op=ALU.add), reads=[dC], writes=[dC])
            k.op("act", lambda e: e.activation(out=lam[:, 0:2], in_=lam[:, 0:2], func=AF.Exp), reads=[dC], writes=[dC])
            k.op("dve", lambda e: e.tensor_tensor(out=lam[:, 2:3], in0=lam[:, 1:2], in1=lam[:, 0:1], op=ALU.subtract), reads=[dC], writes=[dC])
            k.op("dve", lambda e: e.tensor_scalar(out=lam[:, 3:4], in0=lam[:, 2:3], scalar1=-float(lam_init), scalar2=None, op0=ALU.add), reads=[dC], writes=[dC])
            k.op("dve", lambda e: e.tensor_scalar(out=pps[:, 23:24], in0=pps[:, 2:3], scalar1=float(1.0 - lam_init), scalar2=None, op0=ALU.mult), reads=[dC], writes=[dC])
            k.op("act", lambda e: e.activation(out=p4s[:, 1, :], in_=p4s[:, 1, :], func=AF.Exp), reads=[dC], writes=[dC])
            k.op("dve", lambda e: e.tensor_scalar(out=p4s[:, 1, :], in0=p4s[:, 1, :], scalar1=-1.0, scalar2=None, op0=ALU.mult), reads=[dC], writes=[dC])
            for c in range(8):
                k.dma("sp", stg[:], W[c * 128:(c + 1) * 128, :], writes=[dC])
                k.op("dve", lambda e: e.tensor_copy(out=Wb[:, c, :], in_=stg[:]), reads=[dC], writes=[dC])
            k.barrier()
        k.es = es

        xr = k.sb([128, BUFW], BF16)
        Br = k.sb([128, BUFW], BF16)
        Cr = k.sb([128, BUFW], BF16)
        sz = k.sb([128, SEQ], BF16)
        dtc = k.sb([128, NS, 4], F32)
        for b in range(nbatch):
            d_q, d_k, d_v, d_x, d_B, d_Cc, d_z, d_dt = (Dep() for _ in range(8))
            with ExitStack() as esm:
                k.es = esm
                qT = k.sb([128, SEQ], BF16)
                kT = k.sb([128, SA], BF16)
                vS = k.sb([128, NS, 128], BF16)
                with ExitStack() as es2:
                    k.es = es2
                    hb = [k.sb([128, 8, 512], BF16) for _ in range(2)]
                    d_hb = [Dep(), Dep()]
                    cs_ = [k.sb([128, 512], F32) for _ in range(2)]
                    sn_ = [k.sb([128, 512], F32) for _ in range(2)]
                    d_cs = [Dep(), Dep()]
                    sq = k.sb([128, 512], BF16); d_sq = Dep()
                    rst = k.sb([128, 512], F32); d_rst = Dep()
                    qn = k.sb([128, 512], F32); d_qn = Dep()
                    qnb = k.sb([128, 512], BF16); d_qnb = Dep()
                    t1 = k.sb([128, 512], F32); d_t1 = Dep()
                    pf = [k.ps([128, 512]) for _ in range(3)]
                    d_pf = [Dep() for _ in range(3)]
                    pms = k.ps([128, 512]); d_pms = Dep()
                    prt = k.ps([128, 512]); d_prt = Dep()
                    pv = k.ps([128, 512]); d_pv = Dep()
                    pdt = k.ps([128, 512]); d_pdt = Dep()
                    for buf, dd in ((xr, d_x), (Br, d_B), (Cr, d_Cc)):
                        k.op("dve", lambda e: e.memset(buf[:, 0:LATR], 0.0), writes=[dd])
                        k.op("dve", lambda e: e.memset(buf[:, LATR + SEQ:CTXR], 0.0), writes=[dd])
                        k.op("dve", lambda e: e.memset(buf[:, CTXR + CTX:BUFW], 0.0), writes=[dd])
                    nblk = SEQ // 512 + 1
                    ipf = [0]
                    for blk_i in range(nblk):
                        isctx = blk_i == nblk - 1
                        n = 256 if isctx else 512
                        hbuf = hb[blk_i % 2]
                        dh = d_hb[blk_i % 2]
                        if isctx:
                            src = hcT[b]
                        else:
                            src = hT[b * (SEQ // 512) + blk_i]
                        k.dma("sp", hbuf[:, :, 0:n], src, writes=[dh])
                        t0 = blk_i * 512
                        dcs = d_cs[blk_i % 2]
                        if not isctx:
                            k.dma("sp", cs_[blk_i % 2][:], cosd[:, t0:t0 + 512], writes=[dcs])
                            k.dma("sp", sn_[blk_i % 2][:], sind[:, t0:t0 + 512], writes=[dcs])

                        def proj(wcol, width=128):
                            j = ipf[0] % 3
                            ipf[0] += 1
                            for c in range(8):
                                k.op("pe", lambda e: e.matmul(pf[j][0:width, 0:n], lhsT=Wb[:, c, wcol:wcol + width], rhs=hbuf[:, c, 0:n],
                                                              start=(c == 0), stop=(c == 7)), reads=[dh], writes=[d_pf[j]])
                            return pf[j], d_pf[j]

                        for which in ("q", "k"):
                            if which == "q" and isctx:
                                continue
                            ps_, dps_ = proj(WQ if which == "q" else WK)
                            gcol = 0 if which == "q" else 1
                            k.op("act", lambda e: e.activation(out=sq[:, 0:n], in_=ps_[:, 0:n], func=AF.Square), reads=[dps_], writes=[d_sq])
                            k.op("pe", lambda e: e.matmul(pms[:, 0:n], lhsT=blk[:], rhs=sq[:, 0:n], start=True, stop=True), reads=[d_sq], writes=[d_pms])
                            k.op("act", lambda e: e.activation(out=rst[:, 0:n], in_=pms[:, 0:n], func=AF.Sqrt, bias=eps, scale=1.0), reads=[d_pms], writes=[d_rst])
                            k.op("dve", lambda e: e.reciprocal(out=rst[:, 0:n], in_=rst[:, 0:n]), reads=[], writes=[d_rst])
                            if isctx:
                                k.op("dve", lambda e: e.scalar_tensor_tensor(out=kT[:, SEQ:SA], in0=ps_[:, 0:n], scalar=pps[:, gcol:gcol + 1], in1=rst[:, 0:n],
                                                                             op0=ALU.mult, op1=ALU.mult), reads=[dps_, d_rst], writes=[d_k])
                                continue
                            k.op("dve", lambda e: e.scalar_tensor_tensor(out=qn[:], in0=ps_[:], scalar=pps[:, gcol:gcol + 1], in1=rst[:],
                                                                         op0=ALU.mult, op1=ALU.mult), reads=[dps_, d_rst], writes=[d_qn])
                            k.op("act", lambda e: e.activation(out=qnb[:], in_=qn[:], func=AF.Copy), reads=[d_qn], writes=[d_qnb])
                            k.op("pe", lambda e: e.matmul(prt[:], lhsT=prot[:], rhs=qnb[:], start=True, stop=True), reads=[d_qnb], writes=[d_prt])
                            k.op("dve", lambda e: e.tensor_tensor(out=t1[:], in0=prt[:], in1=sn_[blk_i % 2][:], op=ALU.mult), reads=[d_prt, dcs], writes=[d_t1])
                            k.op("dve", lambda e: e.tensor_tensor(out=qn[:], in0=qn[:], in1=cs_[blk_i % 2][:], op=ALU.mult), reads=[dcs], writes=[d_qn])
                            dst = qT if which == "q" else kT
                            dd = d_q if which == "q" else d_k
                            k.op("dve", lambda e: e.tensor_tensor(out=dst[:, t0:t0 + 512], in0=qn[:], in1=t1[:], op=ALU.add), reads=[d_qn, d_t1], writes=[dd])
                        c0 = (CTXR if isctx else LATR + t0)
                        for wcol, buf, dd in ((WX, xr, d_x), (WBm, Br, d_B), (WC, Cr, d_Cc)):
                            ps_, dps_ = proj(wcol)
                            k.op("act", lambda e: e.activation(out=buf[:, c0:c0 + n], in_=ps_[:, 0:n], func=AF.Copy), reads=[dps_], writes=[dd])
                        if not isctx:
                            ps_, dps_ = proj(WZ)
                            k.op("act", lambda e: e.activation(out=sz[:, t0:t0 + 512], in_=ps_[:], func=AF.Silu), reads=[dps_], writes=[d_z])
                        ti = (NL if isctx else blk_i * 4)
                        for tt in range(n // 128):
                            for c in range(8):
                                k.op("pe", lambda e: e.matmul(pv[:, tt * 128:(tt + 1) * 128], lhsT=hbuf[:, c, tt * 128:(tt + 1) * 128], rhs=Wb[:, c, WV:WV + 128],
                                                              start=(c == 0), stop=(c == 7)), reads=[dh], writes=[d_pv])
                            for c in range(8):
                                k.op("pe", lambda e: e.matmul(pdt[:, tt * 4:(tt + 1) * 4], lhsT=hbuf[:, c, tt * 128:(tt + 1) * 128], rhs=Wb[:, c, WDT:WDT + 4],
                                                              start=(c == 0), stop=(c == 7)), reads=[dh], writes=[d_pdt])
                        k.op("dve", lambda e: e.tensor_copy(out=vS[:, ti:ti + n // 128, :].rearrange("p a e -> p (a e)"), in_=pv[:, 0:n]), reads=[d_pv], writes=[d_v])
                        k.op("dve", lambda e: e.tensor_copy(out=dtc[:, ti:ti + n // 128, :].rearrange("p a r -> p (a r)"), in_=pdt[:, 0:n // 32]), reads=[d_pdt], writes=[d_dt])
                    k.barrier()
                k.es = esm
                with ExitStack() as es2:
                    k.es = es2
                    pS = [k.ps([128, 512]) for _ in range(3)]
                    d_pS = [Dep() for _ in range(3)]
                    po = [k.ps([128, 512]) for _ in range(2)]
                    psm = [k.ps([128, 512]) for _ in range(2)]
                    d_po = [Dep(), Dep()]
                    d_psm = [Dep(), Dep()]
                    Pt = [k.sb([128, 512], BF16) for _ in range(3)]
                    d_Pt = [Dep() for _ in range(3)]
                    r_ = [k.sb([128, 512], F32) for _ in range(2)]
                    d_r = [Dep(), Dep()]
                    o_ = k.sb([128, 512], F32); d_o = Dep()
                    osq = k.sb([128, 512], BF16); d_osq = Dep()
                    ob = k.sb([128, 512], BF16); d_ob = Dep()
                    for tb in range(SEQ // 512 if "A" in phases else 0):
                        tsl = slice(tb * 512, (tb + 1) * 512)
                        items = [(c, s) for c in range(2) for s in range(NS)]

                        def qk(i):
                            c, s = items[i]
                            j = i % 3
                            k.op("pe", lambda e: e.matmul(pS[j][:], lhsT=kT[c * 64:(c + 1) * 64, s * 128:(s + 1) * 128], rhs=qT[c * 64:(c + 1) * 64, tsl],
                                                          start=True, stop=True), reads=[d_q, d_k], writes=[d_pS[j]])
                            k.op("act", lambda e: e.activation(out=Pt[j][:], in_=pS[j][:], func=AF.Exp, scale=0.125), reads=[d_pS[j]], writes=[d_Pt[j]])

                        def pvm(i):
                            c, s = items[i]
                            j = i % 3
                            k.op("pe", lambda e: e.matmul(po[c][:], lhsT=vS[:, s, :], rhs=Pt[j][:], start=(s == 0), stop=(s == NS - 1)),
                                 reads=[d_v, d_Pt[j]], writes=[d_po[c]])
                            k.op("pe", lambda e: e.matmul(psm[c][:], lhsT=ones[:], rhs=Pt[j][:], start=(s == 0), stop=(s == NS - 1)),
                                 reads=[d_Pt[j]], writes=[d_psm[c]])

                        qk(0)
                        for i in range(len(items)):
                            if i + 1 < len(items):
                                qk(i + 1)
                            pvm(i)
                        for c in range(2):
                            k.op("dve", lambda e: e.reciprocal(out=r_[c][:], in_=psm[c][:]), reads=[d_psm[c]], writes=[d_r[c]])
                            k.op("dve", lambda e: e.tensor_tensor(out=r_[c][:], in0=po[c][:], in1=r_[c][:], op=ALU.mult), reads=[d_po[c]], writes=[d_r[c]])
                        k.op("dve", lambda e: e.scalar_tensor_tensor(out=o_[:], in0=r_[1][:], scalar=lam[:, 3:4], in1=r_[0][:], op0=ALU.mult, op1=ALU.add),
                             reads=[d_r[0], d_r[1]], writes=[d_o])
                        k.op("act", lambda e: e.activation(out=osq[:], in_=o_[:], func=AF.Square), reads=[d_o], writes=[d_osq])
                        k.op("pe", lambda e: e.matmul(pS[0][:], lhsT=ones[:], rhs=osq[:], start=True, stop=True), reads=[d_osq], writes=[d_pS[0]])
                        k.op("act", lambda e: e.activation(out=r_[0][:], in_=pS[0][:], func=AF.Sqrt, bias=eps, scale=1.0 / 128), reads=[d_pS[0]], writes=[d_r[0]])
                        k.op("dve", lambda e: e.reciprocal(out=r_[0][:], in_=r_[0][:]), reads=[], writes=[d_r[0]])
                        k.op("dve", lambda e: e.scalar_tensor_tensor(out=ob[:], in0=o_[:], scalar=pps[:, 23:24], in1=r_[0][:], op0=ALU.mult, op1=ALU.mult),
                             reads=[d_o, d_r[0]], writes=[d_ob])
                        k.dma("sp", oatt[:, b * SEQ + tb * 512: b * SEQ + (tb + 1) * 512], ob[:], reads=[d_ob], writes=[Dep()])
                    k.barrier()
                k.es = esm
            k.es = es
            with ExitStack() as es2:
              if "S" in phases:
                  k.es = es2
                  xtok = k.sb([128, NS, 128], BF16); d_xt = Dep()
                  acc = [k.sb([128, 2048], F32) for _ in range(2)]
                  d_acc = [Dep(), Dep()]
                  for (buf, wc0, dd) in ((xr, 3, d_x), (Br, 9, d_B), (Cr, 15, d_Cc)):
                      segs = [(LATR + s0, min(2048, SEQ)) for s0 in range(0, SEQ, 2048)] + [(CTXR, CTX)]
                      for si, (rc, n) in enumerate(segs):
                          a_ = acc[si % 2]; da = d_acc[si % 2]
                          k.op("dve", lambda e: e.tensor_scalar(out=a_[:, 0:n], in0=buf[:, rc - 2:rc - 2 + n], scalar1=pps[:, wc0:wc0 + 1], scalar2=None, op0=ALU.mult),
                               reads=[dd], writes=[da])
                          for tap in range(1, 5):
                              k.op("dve", lambda e: e.scalar_tensor_tensor(out=a_[:, 0:n], in0=buf[:, rc - 2 + tap:rc - 2 + tap + n], scalar=pps[:, wc0 + tap:wc0 + tap + 1],
                                                                           in1=a_[:, 0:n], op0=ALU.mult, op1=ALU.add), reads=[dd], writes=[da])
                          k.op("act", lambda e: e.activation(out=buf[:, rc - 4:rc - 4 + n], in_=a_[:, 0:n], func=AF.Silu, bias=pps[:, wc0 + 5:wc0 + 6], scale=1.0),
                               reads=[da], writes=[dd])
                  xc, Bc, Cc = xr, Br, Cr
                  dA = k.sb([128, NS, 4], F32); d_dA = Dep()
                  wi = k.sb([128, NS, 4], F32); d_wi = Dep()
                  tot = k.sb([128, NS, 4], F32); d_tot = Dep()
                  inc = k.sb([128, NS, 4], F32); d_inc = Dep()
                  ncol = k.sb([128, NS, 4], F32); d_nc = Dep()
                  on66 = k.sb([128, NS], F32); d_on = Dep()
                  pw = k.ps([128, 512]); d_pw = Dep()
                  pt_ = k.ps([128, 512]); d_pt = Dep()
                  k.op("dve", lambda e: e.memset(on66[:], 1.0), writes=[d_on])
                  for r in range(4):
                      k.op("dve", lambda e: e.tensor_scalar(out=dtc[:, :, r], in0=dtc[:, :, r], scalar1=p4s[:, 0, r:r + 1], scalar2=None, op0=ALU.add), reads=[d_dt], writes=[d_dt])
                  k.op("act", lambda e: e.activation(out=dtc[:].rearrange("p a r -> p (a r)"), in_=dtc[:].rearrange("p a r -> p (a r)"), func=AF.Exp), reads=[], writes=[d_dt])
                  k.op("act", lambda e: e.activation(out=dtc[:].rearrange("p a r -> p (a r)"), in_=dtc[:].rearrange("p a r -> p (a r)"), func=AF.Ln, bias=1.0, scale=1.0), reads=[], writes=[d_dt])
                  for r in range(4):
                      k.op("dve", lambda e: e.tensor_scalar(out=dA[:, :, r], in0=dtc[:, :, r], scalar1=p4s[:, 1, r:r + 1], scalar2=None, op0=ALU.mult), reads=[d_dt], writes=[d_dA])
                  k.op("pe", lambda e: e.matmul(pw[:, 0:NS * 4], lhsT=tri[:], rhs=dA[:].rearrange("p a r -> p (a r)"), start=True, stop=True), reads=[d_dA], writes=[d_pw])
                  k.op("pe", lambda e: e.matmul(pt_[:, 0:NS * 4], lhsT=onesf[:], rhs=dA[:].rearrange("p a r -> p (a r)"), start=True, stop=True), reads=[d_dA], writes=[d_pt])
                  k.op("dve", lambda e: e.tensor_copy(out=wi[:].rearrange("p a r -> p (a r)"), in_=pw[:, 0:NS * 4]), reads=[d_pw], writes=[d_wi])
                  k.op("dve", lambda e: e.tensor_copy(out=tot[:].rearrange("p a r -> p (a r)"), in_=pt_[:, 0:NS * 4]), reads=[d_pt], writes=[d_tot])
                  for r in range(2):
                      k.op("dve", lambda e: e.tensor_tensor_scan(out=inc[:, NL:NS, r], data0=on66[:, NL:NS], data1=tot[:, NL:NS, r], initial=0.0, op0=ALU.mult, op1=ALU.add),
                           reads=[d_tot, d_on], writes=[d_inc])
                      k.op("dve", lambda e: e.tensor_tensor_scan(out=inc[:, 0:NL, r], data0=on66[:, 0:NL], data1=tot[:, 0:NL, r], initial=inc[:, NS - 1, r:r + 1], op0=ALU.mult, op1=ALU.add),
                           reads=[d_tot, d_on], writes=[d_inc])
                  for r in range(2, 4):
                      k.op("dve", lambda e: e.tensor_tensor_scan(out=inc[:, :, r], data0=on66[:], data1=tot[:, :, r], initial=0.0, op0=ALU.mult, op1=ALU.add),
                           reads=[d_tot, d_on], writes=[d_inc])
                  k.op("dve", lambda e: e.tensor_tensor(out=ncol[:], in0=tot[:], in1=inc[:], op=ALU.subtract), reads=[d_tot, d_inc], writes=[d_nc])
                  k.op("dve", lambda e: e.tensor_tensor(out=ncol[:, :, 0:2], in0=ncol[:, :, 0:2], in1=wi[:, :, 0:2], op=ALU.subtract), reads=[d_wi], writes=[d_nc])
                  k.op("dve", lambda e: e.tensor_tensor(out=ncol[:, :, 2:4], in0=wi[:, :, 2:4], in1=ncol[:, :, 2:4], op=ALU.subtract), reads=[d_wi], writes=[d_nc])
                  k.op("dve", lambda e: e.tensor_tensor(out=ncol[:, :, 2:4], in0=ncol[:, :, 2:4], in1=dA[:, :, 2:4], op=ALU.subtract), reads=[d_dA], writes=[d_nc])
                  k.op("dve", lambda e: e.tensor_scalar(out=wi[:], in0=ncol[:], scalar1=-1.0, scalar2=None, op0=ALU.mult), reads=[d_nc], writes=[d_wi])
                  ur = [k.sb([4, 512], F32) for _ in range(2)]
                  d_ur = [Dep(), Dep()]
                  d_urow = Dep()
                  for g in range(NL // 4):
                      for q in range(4):
                          k.op("pe", lambda e: e.transpose(out=pw[0:4, q * 128:(q + 1) * 128], in_=wi[:, 4 * g + q, :], identity=ident[:]), reads=[d_wi], writes=[d_pw])
                      k.op("dve", lambda e: e.tensor_copy(out=ur[g % 2][:], in_=pw[0:4, :]), reads=[d_pw], writes=[d_ur[g % 2]])
                      k.dma("sp", urow[:, g * 512:(g + 1) * 512], ur[g % 2][:], reads=[d_ur[g % 2]], writes=[d_urow])
                  ptx = k.ps([128, 1024], BF16); d_ptx = Dep()
                  for s0 in range(0, NS, 8):
                      ns_ = min(8, NS - s0)
                      for s in range(ns_):
                          cc = ccol(s0 + s)
                          k.op("pe", lambda e: e.transpose(out=ptx[:, s * 128:(s + 1) * 128], in_=xc[:, cc:cc + 128], identity=identb[:]),
                               reads=[d_x], writes=[d_ptx])
                      k.op("dve", lambda e: e.tensor_copy(out=xtok[:, s0:s0 + ns_, :].rearrange("p a e -> p (a e)"), in_=ptx[:, 0:ns_ * 128]),
                           reads=[d_ptx], writes=[d_xt])
                  msk = k.sb([128, 8, 512], BF16); d_msk = Dep()
                  for q in range(8):
                      k.dma("sp", acc[q % 2][:, 0:512], maskd[q], writes=[d_acc[q % 2]])
                      k.op("dve", lambda e: e.tensor_copy(out=msk[:, q, :], in_=acc[q % 2][:, 0:512]), reads=[d_acc[q % 2]], writes=[d_msk])
                  ub = [k.sb([128, 4, 512], F32) for _ in range(2)]
                  d_ub = [Dep(), Dep()]
                  pCB = [k.ps([128, 512]) for _ in range(2)]
                  d_pCB = [Dep(), Dep()]
                  py = [k.ps([128, 512]) for _ in range(2)]
                  d_py = [Dep(), Dep()]
                  pss = k.ps([128, 512]); d_pss = Dep()
                  Et = [k.sb([128, 512], F32) for _ in range(2)]
                  d_Et = [Dep(), Dep()]
                  Wt = [k.sb([128, 512], BF16) for _ in range(2)]
                  d_Wt = [Dep(), Dep()]
                  mt = [k.sb([128, 512], F32) for _ in range(2)]
                  d_mt = [Dep(), Dep()]
                  yv = k.sb([128, 512], F32); d_yv = Dep()
                  ygb = k.sb([128, 512], BF16); d_ygb = Dep()
                  ysq = k.sb([128, 512], BF16); d_ysq = Dep()
                  ssr = k.sb([1, 512], F32); d_ssr = Dep()
                  for tb in range(SEQ // 512):
                      tsl = slice(tb * 512, (tb + 1) * 512)
                      csl = slice(LATC + tb * 512, LATC + (tb + 1) * 512)
                      u_ = ub[tb % 2]; du = d_ub[tb % 2]
                      for r in range(4):
                          k.dma("sp", u_[:, r, :], urow[r, tsl].partition_broadcast(128), reads=[d_urow], writes=[du])
                      items = []
                      for s in list(range(NL, NS)) + list(range(0, 4 * tb + 4)):
                          mq = (s - 4 * tb) if (s < NL and s >= 4 * tb) else None
                          items.append((0, s, mq))
                      for s in list(range(4 * tb, NL)) + list(range(NL, NS)):
                          mq = (4 + s - 4 * tb) if (s < 4 * tb + 4) else None
                          items.append((1, s, mq))
                      nit = len(items)

                      def cb(i):
                          dr_, s, mq = items[i]
                          j = i % 2
                          cc = ccol(s)
                          k.op("pe", lambda e: e.matmul(pCB[j][:], lhsT=Bc[:, cc:cc + 128], rhs=Cc[:, csl], start=True, stop=True),
                               reads=[d_B, d_Cc], writes=[d_pCB[j]])

                      def rest(i):
                          dr_, s, mq = items[i]
                          j = i % 2
                          for hd in range(2):
                              r = dr_ * 2 + hd
                              jj = hd
                              src = u_[:, r, :]
                              rd = [du]
                              if mq is not None:
                                  k.op("pool", lambda e: e.tensor_tensor(out=mt[jj][:], in0=u_[:, r, :], in1=msk[:, mq, :], op=ALU.add),
                                       reads=[du, d_msk], writes=[d_mt[jj]])
                                  src = mt[jj][:]
                                  rd = [d_mt[jj]]
                              k.op("act", lambda e: e.activation(out=Et[jj][:], in_=src, func=AF.Exp, bias=ncol[:, s, r:r + 1], scale=1.0),
                                   reads=rd + [d_nc], writes=[d_Et[jj]])
                              k.op("dve", lambda e: e.scalar_tensor_tensor(out=Wt[jj][:], in0=Et[jj][:], scalar=dtc[:, s, r:r + 1], in1=pCB[j][:],
                                                                           op0=ALU.mult, op1=ALU.mult), reads=[d_Et[jj], d_dt, d_pCB[j]], writes=[d_Wt[jj]])
                              k.op("pe", lambda e: e.matmul(py[hd][hd * 64:(hd + 1) * 64, :], lhsT=xtok[:, s, hd * 64:(hd + 1) * 64], rhs=Wt[jj][:],
                                                            start=(i == 0), stop=(i == nit - 1)),
                                   reads=[d_xt, d_Wt[jj]], writes=[d_py[hd]])

                      cb(0)
                      for i in range(nit):
                          if i + 1 < nit:
                              cb(i + 1)
                          rest(i)
                      for hd in range(2):
                          hs = slice(hd * 64, (hd + 1) * 64)
                          k.op("dve", lambda e: e.scalar_tensor_tensor(out=yv[hs, :], in0=xc[hs, csl], scalar=pps[hs, 21:22], in1=py[hd][hs, :],
                                                                       op0=ALU.mult, op1=ALU.add), reads=[d_x, d_py[hd]], writes=[d_yv])
                      k.op("dve", lambda e: e.tensor_tensor(out=yv[:], in0=yv[:], in1=sz[:, tsl], op=ALU.mult), reads=[d_z], writes=[d_yv])
                      k.op("act", lambda e: e.activation(out=ysq[:], in_=yv[:], func=AF.Square), reads=[d_yv], writes=[d_ysq])
                      k.op("pe", lambda e: e.matmul(pss[0:1, :], lhsT=ones[:, 0:1], rhs=ysq[:], start=True, stop=True), reads=[d_ysq], writes=[d_pss])
                      k.op("dve", lambda e: e.tensor_copy(out=ssr[:], in_=pss[0:1, :]), reads=[d_pss], writes=[d_ssr])
                      k.op("dve", lambda e: e.tensor_scalar(out=ygb[:], in0=yv[:], scalar1=pps[:, 22:23], scalar2=None, op0=ALU.mult), reads=[d_yv], writes=[d_ygb])
                      k.dma("sp", ygo[:, b * SEQ + tb * 512: b * SEQ + (tb + 1) * 512], ygb[:], reads=[d_ygb], writes=[Dep()])
                      k.dma("sp", sso[:, b * SEQ + tb * 512: b * SEQ + (tb + 1) * 512], ssr[:], reads=[d_ssr], writes=[Dep()])
                  k.barrier()
            k.es = es
        k.finish([])
    return nc


SW_LIMIT = 7.0
SW_ALPHA = 1.702


def build_d(NTOK=16384):
    nc = new_nc()
    h2T = nc.dram_tensor("h2T", [NTOK // 512, 128, 8, 512], BF16, kind="ExternalInput").ap()
    gT = nc.dram_tensor("gT", [4, NTOK], F32, kind="ExternalInput").ap()
    wgu = nc.dram_tensor("wgu", [4, 1024, 2048], F32, kind="ExternalInput").ap()
    bgu = nc.dram_tensor("bgu", [128, 64], F32, kind="ExternalInput").ap()
    wdn = nc.dram_tensor("wdn", [4, 1024, 1024], F32, kind="ExternalInput").ap()
    bdn = nc.dram_tensor("bdn", [4, 1024], F32, kind="ExternalInput").ap()
    part = nc.dram_tensor("part", [NTOK, 1024], BF16, kind="ExternalOutput").ap()
    scr = nc.dram_tensor("scr", [NTOK, 1024], F32, kind="Internal").ap()
    NBLK = NTOK // 512
    with ExitStack() as es:
        k = KB(nc, es)
        wg = k.sb([128, 2, 8, 2048], BF16); d_wg = Dep()
        wd = k.sb([128, 2, 8, 1024], BF16); d_wd = Dep()
        bd = k.sb([128, 2, 1024], BF16); d_bd = Dep()
        bg = k.sb([128, 64], F32); d_bg = Dep()
        stg = [k.sb([128, 2048], F32) for _ in range(2)]
        d_stg = [Dep(), Dep()]
        hb = [k.sb([128, 8, 512], BF16) for _ in range(2)]
        d_hb = [Dep(), Dep()]
        gb = [k.sb([128, 2, 512], F32) for _ in range(2)]
        d_gb = [Dep(), Dep()]
        gr = [k.sb([1, 2, 512], F32) for _ in range(2)]
        grb = [k.sb([128, 2, 512], BF16) for _ in range(2)]
        d_gr = [Dep(), Dep()]
        d_grb = [Dep(), Dep()]
        act = k.sb([128, 16, 512], BF16)
        d_act = [Dep() for _ in range(16)]
        tg = [k.sb([128, 512], F32) for _ in range(2)]
        ts_ = [k.sb([128, 512], F32) for _ in range(2)]
        tl = [k.sb([128, 512], F32) for _ in range(2)]
        d_tg = [Dep(), Dep()]
        d_ts = [Dep(), Dep()]
        d_tl = [Dep(), Dep()]
        ot = [k.sb([128, 1024], F32) for _ in range(2)]
        d_ot = [Dep(), Dep()]
        otb = [k.sb([128, 1024], BF16) for _ in range(2)]
        d_otb = [Dep(), Dep()]
        pvt = [k.sb([128, 1024], F32) for _ in range(2)]
        d_pvt = [Dep(), Dep()]
        pG = [k.ps([128, 512]) for _ in range(2)]
        pL = [k.ps([128, 512]) for _ in range(2)]
        d_pG = [Dep(), Dep()]
        d_pL = [Dep(), Dep()]
        pY = [k.ps([128, 512]) for _ in range(2)]
        d_pY = [Dep(), Dep()]
        d_part = [Dep() for _ in range(NBLK * 4)]
        k.dma("sp", bg[:], bgu, writes=[d_bg])
        k.op("dve", lambda e: e.memset(bd[:], 0.0), writes=[d_bd])
        for b_ in range(2):
            k.op("dve", lambda e: e.memset(grb[b_][:], 0.0), writes=[d_grb[b_]])
        nst = 0
        for ps_ in range(2):
            for el in range(2):
                e_ = 2 * ps_ + el
                for kc in range(8):
                    s_, ds_ = stg[nst % 2], d_stg[nst % 2]
                    nst += 1
                    k.dma("sp", s_[:], wgu[e_, kc * 128:(kc + 1) * 128, :], writes=[ds_])
                    v = s_[:].rearrange("p (f two) -> p two f", two=2)
                    k.op("act", lambda e: e.activation(out=wg[:, el, kc, 0:1024], in_=v[:, 0, :], func=AF.Copy), reads=[ds_], writes=[d_wg])
                    k.op("pool", lambda e: e.tensor_copy(out=wg[:, el, kc, 1024:2048], in_=v[:, 1, :]), reads=[ds_], writes=[d_wg])
                for fc in range(0, 8, 2):
                    s_, ds_ = stg[nst % 2], d_stg[nst % 2]
                    nst += 1
                    k.dma("sp", s_[:].rearrange("p (a d) -> p a d", a=2), wdn[e_, fc * 128:(fc + 2) * 128, :].rearrange("(a p) d -> p a d", p=128), writes=[ds_])
                    k.op("dve", lambda e: e.tensor_copy(out=wd[:, el, fc:fc + 2, :].rearrange("p a d -> p (a d)"), in_=s_[:]), reads=[ds_], writes=[d_wd])
                s_, ds_ = stg[nst % 2], d_stg[nst % 2]
                nst += 1
                k.dma("sp", s_[0:1, 0:1024], bdn[e_:e_ + 1, :], writes=[ds_])
                k.op("dve", lambda e: e.tensor_copy(out=bd[0:1, el, :], in_=s_[0:1, 0:1024]), reads=[ds_], writes=[d_bd])
            for blk_i in range(NBLK):
                b = blk_i % 2
                tsl = slice(blk_i * 512, (blk_i + 1) * 512)
                k.dma("sp", hb[b][:], h2T[blk_i], writes=[d_hb[b]])
                for el in range(2):
                    k.dma("sp", gb[b][:, el, :], gT[2 * ps_ + el, tsl].partition_broadcast(128), writes=[d_gb[b]])
                k.dma("sp", gr[b][:].rearrange("o a t -> o (a t)").rearrange("o (a t) -> o a t", a=2), gT[2 * ps_:2 * ps_ + 2, tsl].rearrange("(o a) t -> o a t", o=1), writes=[d_gr[b]])
                k.op("dve", lambda e: e.tensor_copy(out=grb[b][0:1, :, :], in_=gr[b][:]), reads=[d_gr[b]], writes=[d_grb[b]])
                gi = 0
                for el in range(2):
                    for fc in range(8):
                        j = gi % 2
                        gi += 1
                        for kc in range(8):
                            k.op("pe", lambda e: e.matmul(pG[j][:], lhsT=wg[:, el, kc, fc * 128:(fc + 1) * 128], rhs=hb[b][:, kc, :],
                                                          start=(kc == 0), stop=(kc == 7)), reads=[d_wg, d_hb[b]], writes=[d_pG[j]])
                        for kc in range(8):
                            k.op("pe", lambda e: e.matmul(pL[j][:], lhsT=wg[:, el, kc, 1024 + fc * 128:1024 + (fc + 1) * 128], rhs=hb[b][:, kc, :],
                                                          start=(kc == 0), stop=(kc == 7)), reads=[d_wg, d_hb[b]], writes=[d_pL[j]])
                        bc = (2 * ps_ + el) * 16 + fc
                        k.op("dve", lambda e: e.tensor_scalar(out=tg[j][:], in0=pG[j][:], scalar1=bg[:, bc:bc + 1], scalar2=SW_LIMIT, op0=ALU.add, op1=ALU.min),
                             reads=[d_pG[j], d_bg], writes=[d_tg[j]])
                        k.op("act", lambda e: e.activation(out=ts_[j][:], in_=tg[j][:], func=AF.Sigmoid, scale=SW_ALPHA), reads=[d_tg[j]], writes=[d_ts[j]])
                        k.op("dve", lambda e: e.tensor_scalar(out=tl[j][:], in0=pL[j][:], scalar1=bg[:, bc + 8:bc + 9], scalar2=SW_LIMIT, op0=ALU.add, op1=ALU.min),
                             reads=[d_pL[j], d_bg], writes=[d_tl[j]])
                        k.op("dve", lambda e: e.tensor_scalar(out=tl[j][:], in0=tl[j][:], scalar1=-SW_LIMIT, scalar2=1.0, op0=ALU.max, op1=ALU.add),
                             reads=[], writes=[d_tl[j]])
                        k.op("pool", lambda e: e.tensor_tensor(out=ts_[j][:], in0=ts_[j][:], in1=tg[j][:], op=ALU.mult), reads=[d_tg[j]], writes=[d_ts[j]])
                        k.op("pool", lambda e: e.tensor_tensor(out=ts_[j][:], in0=ts_[j][:], in1=tl[j][:], op=ALU.mult), reads=[d_tl[j]], writes=[d_ts[j]])
                        ai = el * 8 + fc
                        k.op("dve", lambda e: e.tensor_tensor(out=act[:, ai, :], in0=ts_[j][:], in1=gb[b][:, el, :], op=ALU.mult),
                             reads=[d_ts[j], d_gb[b]], writes=[d_act[ai]])
                for tt in range(4):
                    o_, do_ = (ot[tt % 2], d_ot[tt % 2]) if ps_ == 0 else (otb[tt % 2], d_otb[tt % 2])
                    row = slice(blk_i * 512 + tt * 128, blk_i * 512 + (tt + 1) * 128)
                    dp = d_part[blk_i * 4 + tt]
                    if ps_ == 1:
                        k.dma("sp", pvt[tt % 2][:], scr[row, :], reads=[dp], writes=[d_pvt[tt % 2]])
                    for h in range(2):
                        hs = slice(h * 512, (h + 1) * 512)
                        n = 0
                        for el in range(2):
                            for fc in range(8):
                                ai = el * 8 + fc
                                k.op("pe", lambda e: e.matmul(pY[h][:], lhsT=act[:, ai, tt * 128:(tt + 1) * 128], rhs=wd[:, el, fc, hs],
                                                              start=(n == 0), stop=False), reads=[d_act[ai], d_wd], writes=[d_pY[h]])
                                n += 1
                        for el in range(2):
                            k.op("pe", lambda e: e.matmul(pY[h][:], lhsT=grb[b][:, el, tt * 128:(tt + 1) * 128], rhs=bd[:, el, hs],
                                                          start=False, stop=(el == 1)), reads=[d_grb[b], d_bd], writes=[d_pY[h]])
                        if ps_ == 0:
                            k.op("act", lambda e: e.activation(out=o_[:, hs], in_=pY[h][:], func=AF.Copy), reads=[d_pY[h]], writes=[do_])
                        else:
                            k.op("dve", lambda e: e.tensor_tensor(out=o_[:, hs], in0=pY[h][:], in1=pvt[tt % 2][:, hs], op=ALU.add),
                                 reads=[d_pY[h], d_pvt[tt % 2]], writes=[do_])
                    k.dma("sp", (scr if ps_ == 0 else part)[row, :], o_[:], reads=[do_], writes=[dp])
        k.finish([])
    return nc


def d_inputs(inp, layer, j):
    es = slice(4 * j, 4 * j + 4)
    bgu = inp["b_gate_up"][layer][es]
    b = bgu.reshape(4, 8, 128, 2)
    bg = np.concatenate([b[..., 0], b[..., 1]], 1)
    bg = np.ascontiguousarray(bg.transpose(2, 0, 1).reshape(128, 64))
    return {"wgu": np.ascontiguousarray(inp["w_gate_up"][layer][es]), "bgu": bg,
            "wdn": np.ascontiguousarray(inp["w_down"][layer][es]), "bdn": np.ascontiguousarray(inp["b_down"][layer][es])}


def f_consts():
    import ml_dtypes
    bf = ml_dtypes.bfloat16
    c = np.arange(256)[:, None]
    m = np.arange(256)[None, :]
    ang = 2 * np.pi * ((c * m) % 256) / 256.0
    G = np.concatenate([np.cos(ang), -np.sin(ang)], 1).astype(np.float32)
    l1 = np.arange(64)[:, None]
    k1 = np.arange(64)[None, :]
    a = 2 * np.pi * ((l1 * k1) % 64) / 64.0
    cc, ss = np.cos(a), np.sin(a)
    F64 = np.block([[cc, -ss], [ss, cc]]).astype(np.float32)
    l2 = np.arange(128)[:, None]
    k2 = np.arange(128)[None, :]
    a2 = 2 * np.pi * ((l2 * k2) % 128) / 128.0
    C128, S128 = np.cos(a2).astype(np.float32), np.sin(a2).astype(np.float32)
    tw = 2 * np.pi * (np.arange(128)[:, None] * np.arange(64)[None, :]) / 8192.0
    tc = np.repeat(np.cos(tw)[:, None, :], 4, 1).astype(np.float32)
    ts = np.repeat(np.sin(tw)[:, None, :], 4, 1).astype(np.float32)
    return {"G": G.astype(bf), "F64": F64.astype(bf), "C128": C128.astype(bf), "S128": S128.astype(bf), "tc": tc, "ts": ts}


def build_f():
    nc = new_nc()
    L = 8192
    xT = nc.dram_tensor("xT", [256, L], BF16, kind="ExternalInput").ap()
    Gd = nc.dram_tensor("G", [256, 512], BF16, kind="ExternalInput").ap()
    F64d = nc.dram_tensor("F64", [128, 128], BF16, kind="ExternalInput").ap()
    C128d = nc.dram_tensor("C128", [128, 128], BF16, kind="ExternalInput").ap()
    S128d = nc.dram_tensor("S128", [128, 128], BF16, kind="ExternalInput").ap()
    tcd = nc.dram_tensor("tc", [128, 4, 64], F32, kind="ExternalInput").ap()
    tsd = nc.dram_tensor("ts", [128, 4, 64], F32, kind="ExternalInput").ap()
    fo = nc.dram_tensor("f", [L, 256], BF16, kind="ExternalOutput").ap()
    scale = 1.0 / math.sqrt(L * 256.0)
    with ExitStack() as es:
        k = KB(nc, es)
        xs = k.sb([128, 2, L], BF16); d_x = Dep()
        G = k.sb([128, 2, 512], BF16); d_G = Dep()
        F64 = k.sb([128, 128], BF16)
        C128 = k.sb([128, 128], BF16)
        S128 = k.sb([128, 128], BF16)
        tc = k.sb([128, 4, 64], F32)
        ts = k.sb([128, 4, 64], F32)
        d_c = Dep()
        Wl = k.sb([128, 128, 256], BF16); d_W = Dep()
        Tp = k.sb([128, 2, 64, 256], BF16); d_T = Dep()
        k.dma("sp", xs[:], xT.rearrange("(c p) l -> p c l", p=128), writes=[d_x])
        k.dma("sp", G[:], Gd.rearrange("(c p) n -> p c n", p=128), writes=[d_G])
        for t_, s_ in ((F64, F64d), (C128, C128d), (S128, S128d), (tc, tcd), (ts, tsd)):
            k.dma("sp", t_[:], s_, writes=[d_c])
        p0 = [k.ps([128, 512]) for _ in range(2)]
        d_p0 = [Dep(), Dep()]
        pA = [k.ps([128, 512]) for _ in range(2)]
        d_pA = [Dep(), Dep()]
        pB = [k.ps([128, 512]) for _ in range(2)]
        d_pB = [Dep(), Dep()]
        xv = xs[:].rearrange("p c (a b) -> p c b a", b=128)
        for l2 in range(128):
            j = (l2 // 2) % 2
            col = (l2 % 2) * 256
            for ri in range(2):
                for cc in range(2):
                    k.op("pe", lambda e: e.matmul(p0[j][ri * 64:(ri + 1) * 64, col:col + 256], lhsT=xv[:, cc, l2, :], rhs=G[:, cc, ri * 256:(ri + 1) * 256],
                                                  start=(cc == 0), stop=(cc == 1)), reads=[d_x, d_G], writes=[d_p0[j]])
            if l2 % 2 == 1:
                eng = "act" if (l2 // 2) % 2 == 0 else "dve"
                if eng == "act":
                    k.op("act", lambda e: e.activation(out=Wl[:, l2 - 1:l2 + 1, :].rearrange("p a m -> p (a m)"), in_=p0[j][:], func=AF.Copy), reads=[d_p0[j]], writes=[d_W])
                else:
                    k.op("dve", lambda e: e.tensor_copy(out=Wl[:, l2 - 1:l2 + 1, :].rearrange("p a m -> p (a m)"), in_=p0[j][:]), reads=[d_p0[j]], writes=[d_W])
        ta = [k.sb([128, 4, 64], F32) for _ in range(2)]
        tb_ = [k.sb([128, 4, 64], F32) for _ in range(2)]
        d_ta = [Dep(), Dep()]
        d_tb = [Dep(), Dep()]
        for g in range(64):
            j = g % 2
            for q in range(4):
                m = 4 * g + q
                k.op("pe", lambda e: e.matmul(pA[j][:, q * 128:(q + 1) * 128], lhsT=Wl[:, :, m], rhs=F64[:], start=True, stop=True),
                     reads=[d_W, d_c], writes=[d_pA[j]])
            pv = pA[j][:].rearrange("p (q r k) -> p q r k", q=4, r=2)
            Tre, Tim = pv[:, :, 0, :], pv[:, :, 1, :]
            o_re = Tp[:, 0, :, 4 * g:4 * g + 4].rearrange("p k m -> p m k")
            o_im = Tp[:, 1, :, 4 * g:4 * g + 4].rearrange("p k m -> p m k")
            k.op("dve", lambda e: e.tensor_tensor(out=ta[j][:], in0=Tre, in1=tc[:], op=ALU.mult), reads=[d_pA[j], d_c], writes=[d_ta[j]])
            k.op("dve", lambda e: e.tensor_tensor(out=tb_[j][:], in0=Tim, in1=ts[:], op=ALU.mult), reads=[d_pA[j], d_c], writes=[d_tb[j]])
            k.op("pool", lambda e: e.tensor_tensor(out=o_re, in0=ta[j][:], in1=tb_[j][:], op=ALU.add), reads=[d_ta[j], d_tb[j]], writes=[d_T])
            k.op("dve", lambda e: e.tensor_tensor(out=ta[j][:], in0=Tim, in1=tc[:], op=ALU.mult), reads=[d_pA[j], d_c], writes=[d_ta[j]])
            k.op("dve", lambda e: e.tensor_tensor(out=tb_[j][:], in0=Tre, in1=ts[:], op=ALU.mult), reads=[d_pA[j], d_c], writes=[d_tb[j]])
            k.op("pool", lambda e: e.tensor_tensor(out=o_im, in0=ta[j][:], in1=tb_[j][:], op=ALU.subtract), reads=[d_ta[j], d_tb[j]], writes=[d_T])
        ob = [k.sb([128, 512], BF16) for _ in range(2)]
        d_ob = [Dep(), Dep()]
        fv = fo.rearrange("(k2 k1) m -> k2 k1 m", k1=64)
        Tf = Tp[:].rearrange("p r k m -> p r (k m)")
        for blk_i in range(32):
            j = blk_i % 2
            cs = slice(blk_i * 512, (blk_i + 1) * 512)
            k.op("pe", lambda e: e.matmul(pB[j][:], lhsT=C128[:], rhs=Tf[:, 0, cs], start=True, stop=False), reads=[d_T, d_c], writes=[d_pB[j]])
            k.op("pe", lambda e: e.matmul(pB[j][:], lhsT=S128[:], rhs=Tf[:, 1, cs], start=False, stop=True), reads=[d_T, d_c], writes=[d_pB[j]])
            k.op("act", lambda e: e.activation(out=ob[j][:], in_=pB[j][:], func=AF.Copy, scale=scale), reads=[d_pB[j]], writes=[d_ob[j]])
            k.dma("sp", fv[:, 2 * blk_i:2 * blk_i + 2, :], ob[j][:].rearrange("p (k m) -> p k m", k=2), reads=[d_ob[j]], writes=[Dep()])
        k.finish([])
    return nc


D_MODEL = 1024
COL_Q, COL_Z, COL_C, COL_K, COL_V, COL_XB, COL_DT = 0, 1024, 2048, 2304, 3328, 4352, 5632


def _b_inputs(inp, j):
    g = j // 4
    w = inp["w_in"][0]
    dtc = [COL_DT + d * 16 + 2 * j + hd for d in range(2) for hd in range(2)]
    cols = [w[:, COL_Q + j * 128: COL_Q + (j + 1) * 128], w[:, COL_K + j * 128: COL_K + (j + 1) * 128],
            w[:, COL_V + j * 128:COL_V + (j + 1) * 128], w[:, dtc],
            w[:, COL_Z + j * 128:COL_Z + (j + 1) * 128], w[:, COL_XB + j * 128:COL_XB + (j + 1) * 128],
            w[:, COL_XB + 1024 + g * 128:COL_XB + 1024 + (g + 1) * 128], w[:, COL_C + g * 128:COL_C + (g + 1) * 128]]
    W = np.ascontiguousarray(np.concatenate(cols, 1))
    pp = np.zeros((128, 32), np.float32)
    p = np.arange(128)
    pp[:, 0] = inp["q_norm_g"][0][p % 64]
    pp[:, 1] = inp["k_norm_g"][0][p % 64]
    pp[:, 2] = inp["da_subln_g"][0]
    pp[:, 3:8] = inp["conv_xb_w"][0][:, j * 128:(j + 1) * 128].T
    pp[:, 8] = inp["conv_xb_b"][0][j * 128:(j + 1) * 128]
    pp[:, 9:14] = inp["conv_xb_w"][0][:, 1024 + g * 128:1024 + (g + 1) * 128].T
    pp[:, 14] = inp["conv_xb_b"][0][1024 + g * 128:1024 + (g + 1) * 128]
    pp[:, 15:20] = inp["conv_c_w"][0][:, g * 128:(g + 1) * 128].T
    pp[:, 20] = inp["conv_c_b"][0][g * 128:(g + 1) * 128]
    pp[:, 21] = inp["d_skip"][0][2 * j + p // 64]
    pp[:, 22] = inp["ssm_norm_g"][0][j * 128:(j + 1) * 128]
    p4 = np.zeros((2, 4), np.float32)
    for d in range(2):
        for hd in range(2):
            p4[0, d * 2 + hd] = inp["dt_bias"][0][d, 2 * j + hd]
            p4[1, d * 2 + hd] = inp["a_log"][0][d, 2 * j + hd]
    return {"W": W, "pp": pp, "p4": p4, "lam4": np.ascontiguousarray(inp["da_lambda"][0])}


def _moe_stage(inp, layer, h2T, gatesT):
    nc = build_d(16384)
    h2Tb = _to_blocks(h2T, 512)
    in_maps = []
    for j in range(NCORES):
        d = d_inputs(inp, layer, j)
        d["h2T"] = h2Tb
        d["gT"] = np.ascontiguousarray(gatesT[4 * j:4 * j + 4])
        in_maps.append(d)
    res = run_spmd(nc, in_maps)
    return [res[j]["part"] for j in range(NCORES)]


def _hT_from(res, n):
    blks = np.concatenate([res[r]["hT"] for r in range(n)], 0)
    nt = blks.shape[0]
    return np.ascontiguousarray(blks.reshape(nt, 128, 8, 128).transpose(2, 1, 0, 3).reshape(1024, nt * 128))


def _to_blocks(hT, bs):
    return np.ascontiguousarray(hT.reshape(8, 128, -1, bs).transpose(2, 1, 0, 3))


def kernel(**inp):
    inp = {k_: np.asarray(v) for k_, v in inp.items()}
    ident = np.eye(128, dtype=np.float32)
    TPC = 2048
    x0 = np.ascontiguousarray(inp["x"].reshape(16384, 1024))
    mod = run_l0(inp)
    modr = mod.reshape(2, 3, 6, 1024)
    nc = build_t1(front=None, norm=(0, 1), h_layout="T", router=False, out_x=False)
    res = run_spmd(nc, [{"x": x0[r * TPC:(r + 1) * TPC], "mod": np.ascontiguousarray(modr[0, r // 4]), "ident": ident,
                         "ng": inp["norm_g"][0, 0]} for r in range(NCORES)])
    hT0 = _hT_from(res, NCORES)
    ctxf = np.ascontiguousarray(inp["ctx"].reshape(512, 1024))
    nc = build_t1(front=None, norm=(0, 1), h_layout="T", router=False, out_x=False, T=128)
    res = run_spmd(nc, [{"x": ctxf[(r % 4) * 128:(r % 4 + 1) * 128], "mod": np.ascontiguousarray(modr[0, 2]), "ident": ident,
                         "ng": inp["norm_g"][0, 0]} for r in range(NCORES)])
    hcT = _hT_from(res, 4)
    lam_init = 0.8 - 0.6 * math.exp(-0.3 * 0)
    cst = b_consts()
    cst["tri"] = np.triu(np.ones((128, 128), np.float32))
    nc = build_b(lam_init)
    hT0b = _to_blocks(hT0, 512)
    hcTb = _to_blocks(hcT, CTX)
    in_maps = []
    for j in range(NCORES):
        d = {"hT": hT0b, "hcT": hcTb}
        d.update(_b_inputs(inp, j))
        d.update(cst)
        in_maps.append(d)
    res = run_spmd(nc, in_maps)
    oT = np.concatenate([res[j]["oT"] for j in range(NCORES)], 0)
    ygT = np.concatenate([res[j]["ygT"] for j in range(NCORES)], 0)
    ss = np.ascontiguousarray(np.concatenate([res[j]["ss"] for j in range(NCORES)], 0).T)
    del res
    nc = build_t1(front="mix", nA=8, nB=8, gate_row=2, norm=(3, 4), h_layout="T", router=True, out_x=True)
    in_maps = []
    for r in range(NCORES):
        ts_ = slice(r * TPC, (r + 1) * TPC)
        in_maps.append({"x": x0[ts_], "mod": np.ascontiguousarray(modr[0, r // 4]), "ident": ident,
                        "mixA": np.ascontiguousarray(oT[:, ts_]), "wA": np.ascontiguousarray(inp["w_out"][0][0:1024]),
                        "mixB": np.ascontiguousarray(ygT[:, ts_]), "wB": np.ascontiguousarray(inp["w_out"][0][1024:2048]),
                        "ss": np.ascontiguousarray(ss[ts_]), "ng": inp["norm_g"][0, 1],
                        "wr": inp["w_router"][0], "br": inp["b_router"][0]})
    res = run_spmd(nc, in_maps)
    x1 = [res[r]["xo"] for r in range(NCORES)]
    h2T = _hT_from(res, NCORES)
    gatesT = np.concatenate([res[r]["gatesT"] for r in range(NCORES)], 1)
    parts = _moe_stage(inp, 0, h2T, gatesT)
    nc = build_t1(front="parts", gate_row=5, norm=(0, 1), h_layout="T", router=False, out_x=True)
    in_maps = []
    for r in range(NCORES):
        ts_ = slice(r * TPC, (r + 1) * TPC)
        in_maps.append({"x": x1[r], "mod": np.ascontiguousarray(np.concatenate([modr[1, r // 4][0:5], modr[0, r // 4][5:6]], 0)),
                        "ident": ident, "parts": np.ascontiguousarray(np.stack([parts[j][ts_] for j in range(NCORES)], 0)),
                        "ng": inp["norm_g"][1, 0]})
    res = run_spmd(nc, in_maps)
    del parts
    x2 = [res[r]["xo"] for r in range(NCORES)]
    hT1 = _hT_from(res, NCORES)
    fc = f_consts()
    nc = build_f()
    in_maps = []
    for r in range(NCORES):
        b, g = r // 4, r % 4
        d = dict(fc)
        d["xT"] = np.ascontiguousarray(hT1[g * 256:(g + 1) * 256, b * 8192:(b + 1) * 8192])
        in_maps.append(d)
    res = run_spmd(nc, in_maps)
    fT = np.concatenate([np.concatenate([res[b * 4 + g]["f"].T for g in range(4)], 0) for b in range(2)], 1)
    nc = build_t1(front="mix", nA=8, nB=0, gate_row=2, norm=(3, 4), h_layout="T", router=True, out_x=True)
    in_maps = []
    for r in range(NCORES):
        ts_ = slice(r * TPC, (r + 1) * TPC)
        in_maps.append({"x": x2[r], "mod": np.ascontiguousarray(modr[1, r // 4]), "ident": ident,
                        "mixA": np.ascontiguousarray(fT[:, ts_]), "wA": inp["w_fourier"][0], "ng": inp["norm_g"][1, 1],
                        "wr": inp["w_router"][1], "br": inp["b_router"][1]})
    res = run_spmd(nc, in_maps)
    x3 = [res[r]["xo"] for r in range(NCORES)]
    h2T = _hT_from(res, NCORES)
    gatesT = np.concatenate([res[r]["gatesT"] for r in range(NCORES)], 1)
    parts = _moe_stage(inp, 1, h2T, gatesT)
    nc = build_t1(front="parts", gate_row=5, norm=None, router=False, out_x=True)
    in_maps = []
    for r in range(NCORES):
        ts_ = slice(r * TPC, (r + 1) * TPC)
        in_maps.append({"x": x3[r], "mod": np.ascontiguousarray(modr[1, r // 4]), "ident": ident,
                        "parts": np.ascontiguousarray(np.stack([parts[j][ts_] for j in range(NCORES)], 0))})
    res = run_spmd(nc, in_maps)
    out = np.concatenate([res[r]["xo"] for r in range(NCORES)], 0).reshape(2, 8192, 1024)
    return out.astype(np.float32)
```

```python
import math
from contextlib import ExitStack

import numpy as np
import concourse.bass as bass
import concourse.mybir as mybir
from concourse.bass_utils import run_bass_kernel_spmd

F32 = mybir.dt.float32
BF16 = mybir.dt.bfloat16
AF = mybir.ActivationFunctionType
ALU = mybir.AluOpType
AX = mybir.AxisListType
NCORES = 8


class Dep:
    __slots__ = ("w", "r")

    def __init__(self):
        self.w = None
        self.r = {}


class KB:
    DMA_RING = 8

    def __init__(self, nc, es):
        self.nc = nc
        self.es = es
        self.root_es = es
        self.E = {"pe": nc.tensor, "act": nc.scalar, "dve": nc.vector, "pool": nc.gpsimd, "sp": nc.sync}
        self.sem = {}
        for e in ("pe", "act", "dve", "pool"):
            self.sem[e] = es.enter_context(nc.semaphore(e))
        self.cnt = {e: 0 for e in ("pe", "act", "dve", "pool")}
        self.seen = {e: {} for e in self.E}
        self.dmaq = {}
        self.ntile = 0

    def sb(self, shape, dt, name=None):
        self.ntile += 1
        return self.es.enter_context(self.nc.sbuf_tensor(name or f"t{self.ntile}", list(shape), dt))

    def ps(self, shape, dt=F32, name=None):
        self.ntile += 1
        return self.es.enter_context(self.nc.psum_tensor(name or f"p{self.ntile}", list(shape), dt))

    def _wait(self, e, key, val):
        if self.seen[e].get(key, 0) >= val:
            return
        self.E[e].wait_ge(self.sem[key], val)
        self.seen[e][key] = val

    def _deps(self, e, reads, writes):
        need = {}
        for d in reads:
            if d.w:
                k, v = d.w
                need[k] = max(need.get(k, 0), v)
        for d in writes:
            if d.w:
                k, v = d.w
                need[k] = max(need.get(k, 0), v)
            for k, v in d.r.items():
                need[k] = max(need.get(k, 0), v)
        for k, v in need.items():
            if k == "pe" and e == "pe":
                continue
            self._wait(e, k, v)

    def op(self, e, fn, reads=(), writes=()):
        self._deps(e, reads, writes)
        inst = fn(self.E[e])
        self.cnt[e] += 1
        v = self.cnt[e]
        inst.then_inc(self.sem[e], 1)
        for d in reads:
            d.r[e] = v
        for d in writes:
            d.w = (e, v)
            d.r = {}
        return inst

    def dma(self, q, out, in_, reads=(), writes=(), **kw):
        ring = self.dmaq.setdefault(q, {"n": 0, "keys": []})
        n = ring["n"]
        slot = n % self.DMA_RING
        if len(ring["keys"]) <= slot:
            key = f"dma_{q}_{slot}"
            self.sem[key] = self.root_es.enter_context(self.nc.semaphore(key))
            ring["keys"].append(key)
        key = ring["keys"][slot]
        val = 16 * (n // self.DMA_RING + 1)
        if n >= self.DMA_RING:
            self._wait(q, key, val - 16)
        self._deps(q, reads, writes)
        inst = self.E[q].dma_start(out=out, in_=in_, **kw)
        inst.then_inc(self.sem[key], 16)
        ring["n"] += 1
        for d in reads:
            d.r[key] = val
        for d in writes:
            d.w = (key, val)
            d.r = {}
        return inst

    def coll(self, kind, op, ins, outs, reads=(), writes=()):
        q = "pool"
        ring = self.dmaq.setdefault("coll", {"n": 0, "keys": []})
        if not ring["keys"]:
            self.sem["coll"] = self.root_es.enter_context(self.nc.semaphore("coll"))
            ring["keys"].append("coll")
        n = ring["n"]
        val = 16 * (n + 1)
        if n >= 1:
            self._wait(q, "coll", val - 16)
        self._deps(q, reads, writes)
        inst = self.nc.gpsimd.collective_compute(kind, op, replica_groups=[list(range(NCORES))], ins=ins, outs=outs)
        inst.then_inc(self.sem["coll"], 16)
        ring["n"] += 1
        for d in reads:
            d.r["coll"] = val
        for d in writes:
            d.w = ("coll", val)
            d.r = {}
        return inst

    def finish(self, deps):
        for d in deps:
            if d.w:
                self._wait("sp", d.w[0], d.w[1])
        for q, ring in self.dmaq.items():
            n = ring["n"]
            for slot, key in enumerate(ring["keys"]):
                uses = n if q == "coll" else (n - slot + self.DMA_RING - 1) // self.DMA_RING
                if uses > 0:
                    self._wait("sp", key, 16 * uses)


def new_nc():
    return bass.Bass("TRN2", target_bir_lowering=False)


def run_spmd(nc, in_maps):
    import time, sys
    t0 = time.time()
    res = run_bass_kernel_spmd(nc, in_maps, core_ids=list(range(NCORES)))
    print(f'[launch] {time.time() - t0:.1f}s', file=sys.stderr, flush=True)
    return res.results


def build_l0():
    nc = new_nc()
    adaw = nc.dram_tensor("adaw", [2, 1024, 768], F32, kind="ExternalInput").ap()
    adab = nc.dram_tensor("adab", [128, 12], F32, kind="ExternalInput").ap()
    cT = nc.dram_tensor("cT", [1024, 3], F32, kind="ExternalInput").ap()
    out = nc.dram_tensor("modp", [128, 36], F32, kind="ExternalOutput").ap()
    with ExitStack() as es:
        k = KB(nc, es)
        w_sb = k.sb([128, 2, 8, 768], F32)
        b_sb = k.sb([128, 12], F32)
        c_sb = k.sb([128, 8, 3], F32)
        sc_sb = k.sb([128, 8, 3], F32)
        o_sb = k.sb([128, 12, 3], F32)
        pp = k.ps([128, 512], F32)
        dw = [Dep(), Dep()]
        db, dc, dsc, dps, do = Dep(), Dep(), Dep(), Dep(), Dep()
        for l in range(2):
            k.dma("sp", w_sb[:, l], adaw[l].rearrange("(k p) m -> p k m", p=128), writes=[dw[l]])
        k.dma("sp", b_sb[:], adab, writes=[db])
        k.dma("sp", c_sb[:], cT.rearrange("(k p) v -> p k v", p=128), writes=[dc])
        k.op("act", lambda e: e.activation(out=sc_sb[:], in_=c_sb[:], func=AF.Silu), reads=[dc], writes=[dsc])
        for l in range(2):
            for mc in range(6):
                for kc in range(8):
                    k.op("pe", lambda e: e.matmul(pp[:, (l * 6 + mc) * 3:(l * 6 + mc) * 3 + 3],
                                                  lhsT=w_sb[:, l, kc, mc * 128:(mc + 1) * 128],
                                                  rhs=sc_sb[:, kc, :], start=(kc == 0), stop=(kc == 7)),
                         reads=[dw[l], dsc], writes=[dps])
        for v in range(3):
            k.op("dve", lambda e: e.tensor_tensor(out=o_sb[:, :, v], in0=pp[:, 0:36].rearrange("p (a v) -> p a v", v=3)[:, :, v],
                                                  in1=b_sb[:], op=ALU.add), reads=[dps, db], writes=[do])
        k.dma("sp", out, o_sb[:].rearrange("p a v -> p (a v)"), reads=[do], writes=[Dep()])
        k.finish([])
    return nc


def run_l0(inp):
    ada_w, ada_b = inp["ada_w"], inp["ada_b"]
    cT = np.ascontiguousarray(np.concatenate([inp["c"], inp["c_ctx"][None]], 0).T)
    in_maps = []
    for j in range(NCORES):
        sl = slice(768 * j, 768 * (j + 1))
        adab = ada_b[:, sl].reshape(2, 6, 128).transpose(2, 0, 1).reshape(128, 12)
        in_maps.append({"adaw": np.ascontiguousarray(ada_w[:, :, sl]), "adab": np.ascontiguousarray(adab), "cT": cT})
    res = run_spmd(build_l0(), in_maps)
    mod = np.zeros((2, 3, 6144), np.float32)
    for j in range(NCORES):
        o = res[j]["modp"].reshape(128, 2, 6, 3)
        mod[:, :, 768 * j:768 * (j + 1)] = o.transpose(1, 3, 2, 0).reshape(2, 3, 768)
    return mod


RMS_EPS = 1e-6


def build_t1(front, nA=0, nB=0, gate_row=2, norm=None, h_layout="T", router=False, out_x=True, T=2048, nparts=8):
    nc = new_nc()
    NT = T // 128
    x = nc.dram_tensor("x", [T, 1024], F32, kind="ExternalInput").ap()
    mod = nc.dram_tensor("mod", [6, 1024], F32, kind="ExternalInput").ap()
    ident_d = nc.dram_tensor("ident", [128, 128], F32, kind="ExternalInput").ap()
    if front == "mix":
        mixA = nc.dram_tensor("mixA", [nA * 128, T], BF16, kind="ExternalInput").ap()
        wA = nc.dram_tensor("wA", [nA * 128, 1024], F32, kind="ExternalInput").ap()
        if nB:
            mixB = nc.dram_tensor("mixB", [nB * 128, T], BF16, kind="ExternalInput").ap()
            wB = nc.dram_tensor("wB", [nB * 128, 1024], F32, kind="ExternalInput").ap()
            ssd = nc.dram_tensor("ss", [T, 8], F32, kind="ExternalInput").ap()
    elif front == "parts":
        parts = nc.dram_tensor("parts", [nparts, T, 1024], BF16, kind="ExternalInput").ap()
    if norm is not None:
        ng = nc.dram_tensor("ng", [1024], F32, kind="ExternalInput").ap()
        if h_layout == "T":
            hout = nc.dram_tensor("hT", [T // 128, 128, 1024], BF16, kind="ExternalOutput").ap()
        else:
            hout = nc.dram_tensor("hN", [T, 1024], BF16, kind="ExternalOutput").ap()
    if router:
        wr = nc.dram_tensor("wr", [1024, 32], F32, kind="ExternalInput").ap()
        br = nc.dram_tensor("br", [32], F32, kind="ExternalInput").ap()
        gout = nc.dram_tensor("gatesT", [32, T], F32, kind="ExternalOutput").ap()
    if out_x:
        xout = nc.dram_tensor("xo", [T, 1024], F32, kind="ExternalOutput").ap()

    with ExitStack() as es:
        k = KB(nc, es)
        outd = []
        ident = k.sb([128, 128], F32)
        d_id = Dep()
        k.dma("sp", ident[:], ident_d, writes=[d_id])
        g_bc = k.sb([128, 1024], F32)
        d_g = Dep()
        if front is not None:
            k.dma("sp", g_bc[:], mod[gate_row].partition_broadcast(128), writes=[d_g])
        if norm is not None:
            A_bc = k.sb([128, 1024], F32)
            B_bc = k.sb([128, 1024], F32)
            ng_bc = k.sb([128, 1024], F32)
            d_A, d_B, d_ng = Dep(), Dep(), Dep()
            k.dma("sp", A_bc[:], mod[norm[1]].partition_broadcast(128), writes=[d_A])
            k.dma("sp", B_bc[:], mod[norm[0]].partition_broadcast(128), writes=[d_B])
            k.dma("sp", ng_bc[:], ng.partition_broadcast(128), writes=[d_ng])
            k.op("dve", lambda e: e.scalar_tensor_tensor(out=A_bc[:], in0=A_bc[:], scalar=1.0, in1=ng_bc[:],
                                                         op0=ALU.add, op1=ALU.mult), reads=[d_ng], writes=[d_A])
        if router:
            wr_sb = k.sb([128, 8, 32], F32)
            br_bc = k.sb([128, 32], F32)
            d_wr, d_br = Dep(), Dep()
            k.dma("sp", wr_sb[:], wr.rearrange("(k p) e -> p k e", p=128), writes=[d_wr])
            k.dma("sp", br_bc[:], br.partition_broadcast(128), writes=[d_br])
            wr_hi = k.sb([128, 8, 32], BF16)
            wr_lo = k.sb([128, 8, 32], BF16)
            k.op("dve", lambda e: e.tensor_copy(out=wr_hi[:], in_=wr_sb[:]), reads=[d_wr], writes=[d_wr])
            k.op("dve", lambda e: e.tensor_tensor(out=wr_lo[:], in0=wr_sb[:], in1=wr_hi[:], op=ALU.subtract), reads=[d_wr], writes=[d_wr])
        if front == "mix":
            stg = [k.sb([128, 1024], F32) for _ in range(2)]
            d_stg = [Dep(), Dep()]
            wA_bf = k.sb([128, nA, 1024], BF16)
            d_wA = Dep()
            n = 0
            for c in range(nA):
                k.dma("sp", stg[n % 2][:], wA[c * 128:(c + 1) * 128, :], writes=[d_stg[n % 2]])
                k.op("act", lambda e: e.activation(out=wA_bf[:, c, :], in_=stg[n % 2][:], func=AF.Copy),
                     reads=[d_stg[n % 2]], writes=[d_wA])
                n += 1
            if nB:
                wB_bf = k.sb([128, nB, 1024], BF16)
                d_wB = Dep()
                for c in range(nB):
                    k.dma("sp", stg[n % 2][:], wB[c * 128:(c + 1) * 128, :], writes=[d_stg[n % 2]])
                    k.op("act", lambda e: e.activation(out=wB_bf[:, c, :], in_=stg[n % 2][:], func=AF.Copy),
                         reads=[d_stg[n % 2]], writes=[d_wB])
                    n += 1
        NB_ = 2
        xt = [k.sb([128, 1024], F32) for _ in range(NB_)]
        d_x = [Dep() for _ in range(NB_)]
        tmp = [k.sb([128, 1024], F32) for _ in range(NB_)]
        d_tmp = [Dep() for _ in range(NB_)]
        if front == "mix":
            mA = [k.sb([128, nA, 128], BF16) for _ in range(NB_)]
            d_mA = [Dep() for _ in range(NB_)]
            if nB:
                mB = [k.sb([128, nB, 128], BF16) for _ in range(NB_)]
                d_mB = [Dep() for _ in range(NB_)]
                sst = [k.sb([128, 8], F32) for _ in range(NB_)]
                d_ss = [Dep() for _ in range(NB_)]
                rs = [k.sb([128, 4], F32) for _ in range(NB_)]
                d_rs = [Dep() for _ in range(NB_)]
                asb = [k.sb([128, 512], F32) for _ in range(NB_)]
                d_asb = [Dep() for _ in range(NB_)]
        if front == "parts":
            pt = [k.sb([128, nparts, 1024], BF16) for _ in range(NB_)]
            d_pt = [Dep() for _ in range(NB_)]
        if norm is not None:
            junk = [k.sb([128, 1024], BF16) for _ in range(NB_)]
            d_junk = [Dep() for _ in range(NB_)]
            st = [k.sb([128, 4], F32) for _ in range(NB_)]
            d_st = [Dep() for _ in range(NB_)]
            hf = [k.sb([128, 1024], F32) for _ in range(NB_)]
            d_hf = [Dep() for _ in range(NB_)]
            hb = [k.sb([128, 1024], BF16) for _ in range(NB_)]
            d_hb = [Dep() for _ in range(NB_)]
        if router:
            hTf = [k.sb([128, 8, 128], BF16) for _ in range(NB_)]
            d_hTf = [Dep() for _ in range(NB_)]
            lg = [k.sb([128, 32], F32) for _ in range(NB_)]
            d_lg = [Dep() for _ in range(NB_)]
            t8 = [k.sb([128, 8], F32) for _ in range(NB_)]
            d_t8 = [Dep() for _ in range(NB_)]
            ex = [k.sb([128, 32], F32) for _ in range(NB_)]
            d_ex = [Dep() for _ in range(NB_)]
            gt = [k.sb([128, 32], F32) for _ in range(NB_)]
            d_gt = [Dep() for _ in range(NB_)]
            gT = [k.sb([32, 128], F32) for _ in range(NB_)]
            d_gT = [Dep() for _ in range(NB_)]
        pa = [k.ps([128, 512]) for _ in range(2)]
        d_pa = [Dep(), Dep()]
        pb = [k.ps([128, 512]) for _ in range(2)]
        d_pb = [Dep(), Dep()]
        ptr = k.ps([128, 1024])
        d_ptr = Dep()
        plg = k.ps([128, 512])
        d_plg = Dep()

        for t in range(NT):
            b = t % NB_
            tok = slice(t * 128, (t + 1) * 128)
            k.dma("sp", xt[b][:], x[tok, :], writes=[d_x[b]])
            xn, d_xn = xt[b], d_x[b]
            if front == "mix":
                k.dma("sp", mA[b][:], mixA[:, tok].rearrange("(c p) t -> p c t", p=128), writes=[d_mA[b]])
                if nB:
                    k.dma("sp", mB[b][:], mixB[:, tok].rearrange("(c p) t -> p c t", p=128), writes=[d_mB[b]])
                    k.dma("sp", sst[b][:], ssd[tok, :], writes=[d_ss[b]])
                    k.op("dve", lambda e: e.tensor_reduce(out=rs[b][:, 0:1], in_=sst[b][:], axis=AX.X, op=ALU.add),
                         reads=[d_ss[b]], writes=[d_rs[b]])
                    k.op("act", lambda e: e.activation(out=rs[b][:, 1:2], in_=rs[b][:, 0:1], func=AF.Sqrt,
                                                       scale=1.0 / 1024, bias=eps_ap(k)), reads=[d_rs[b]], writes=[d_rs[b]])
                    k.op("dve", lambda e: e.reciprocal(out=rs[b][:, 2:3], in_=rs[b][:, 1:2]), reads=[d_rs[b]], writes=[d_rs[b]])
                for h in range(2):
                    cs = slice(h * 512, (h + 1) * 512)
                    for c in range(nA):
                        k.op("pe", lambda e: e.matmul(pa[h][:], lhsT=mA[b][:, c, :], rhs=wA_bf[:, c, cs],
                                                      start=(c == 0), stop=(c == nA - 1)),
                             reads=[d_mA[b], d_wA], writes=[d_pa[h]])
                    if nB:
                        for c in range(nB):
                            k.op("pe", lambda e: e.matmul(pb[h][:], lhsT=mB[b][:, c, :], rhs=wB_bf[:, c, cs],
                                                          start=(c == 0), stop=(c == nB - 1)),
                                 reads=[d_mB[b], d_wB], writes=[d_pb[h]])
                        k.op("act", lambda e: e.activation(out=asb[b][:], in_=pa[h][:], func=AF.Copy),
                             reads=[d_pa[h]], writes=[d_asb[b]])
                        k.op("dve", lambda e: e.scalar_tensor_tensor(out=tmp[b][:, cs], in0=pb[h][:], scalar=rs[b][:, 2:3],
                                                                     in1=asb[b][:], op0=ALU.mult, op1=ALU.add),
                             reads=[d_pb[h], d_rs[b], d_asb[b]], writes=[d_tmp[b]])
                        k.op("dve", lambda e: e.tensor_tensor(out=tmp[b][:, cs], in0=tmp[b][:, cs], in1=g_bc[:, cs], op=ALU.mult),
                             reads=[d_g], writes=[d_tmp[b]])
                    else:
                        k.op("dve", lambda e: e.tensor_tensor(out=tmp[b][:, cs], in0=pa[h][:], in1=g_bc[:, cs], op=ALU.mult),
                             reads=[d_pa[h], d_g], writes=[d_tmp[b]])
                k.op("dve", lambda e: e.tensor_tensor(out=xt[b][:], in0=xt[b][:], in1=tmp[b][:], op=ALU.add),
                     reads=[d_tmp[b]], writes=[d_x[b]])
            elif front == "parts":
                k.dma("sp", pt[b][:], parts[:, tok, :].rearrange("j t d -> t j d"), writes=[d_pt[b]])
                if nparts == 1:
                    k.op("dve", lambda e: e.tensor_copy(out=tmp[b][:], in_=pt[b][:, 0, :]), reads=[d_pt[b]], writes=[d_tmp[b]])
                else:
                    k.op("dve", lambda e: e.tensor_tensor(out=tmp[b][:], in0=pt[b][:, 0, :], in1=pt[b][:, 1, :], op=ALU.add),
                         reads=[d_pt[b]], writes=[d_tmp[b]])
                for j in range(2, nparts):
                    k.op("dve", lambda e: e.tensor_tensor(out=tmp[b][:], in0=tmp[b][:], in1=pt[b][:, j, :], op=ALU.add),
                         reads=[d_pt[b]], writes=[d_tmp[b]])
                k.op("dve", lambda e: e.tensor_tensor(out=tmp[b][:], in0=tmp[b][:], in1=g_bc[:], op=ALU.mult),
                     reads=[d_g], writes=[d_tmp[b]])
                k.op("dve", lambda e: e.tensor_tensor(out=xt[b][:], in0=xt[b][:], in1=tmp[b][:], op=ALU.add),
                     reads=[d_tmp[b]], writes=[d_x[b]])
            if out_x:
                do = Dep()
                k.dma("sp", xout[tok, :], xt[b][:], reads=[d_x[b]], writes=[do])
            if norm is not None:
                k.op("act", lambda e: e.activation(out=junk[b][:], in_=xt[b][:], func=AF.Square, accum_out=st[b][:, 0:1]),
                     reads=[d_x[b]], writes=[d_junk[b], d_st[b]])
                k.op("act", lambda e: e.activation(out=st[b][:, 1:2], in_=st[b][:, 0:1], func=AF.Sqrt,
                                                   scale=1.0 / 1024, bias=eps_ap(k)), reads=[d_st[b]], writes=[d_st[b]])
                k.op("dve", lambda e: e.reciprocal(out=st[b][:, 2:3], in_=st[b][:, 1:2]), reads=[d_st[b]], writes=[d_st[b]])
                k.op("dve", lambda e: e.scalar_tensor_tensor(out=hf[b][:], in0=xt[b][:], scalar=st[b][:, 2:3], in1=A_bc[:],
                                                             op0=ALU.mult, op1=ALU.mult),
                     reads=[d_x[b], d_st[b], d_A], writes=[d_hf[b]])
                k.op("dve", lambda e: e.tensor_tensor(out=hf[b][:], in0=hf[b][:], in1=B_bc[:], op=ALU.add),
                     reads=[d_B], writes=[d_hf[b]])
                if h_layout == "N":
                    k.op("act", lambda e: e.activation(out=hb[b][:], in_=hf[b][:], func=AF.Copy), reads=[d_hf[b]], writes=[d_hb[b]])
                    k.dma("sp", hout[tok, :], hb[b][:], reads=[d_hb[b]], writes=[Dep()])
                else:
                    for c in range(8):
                        k.op("pe", lambda e: e.transpose(out=ptr[:, c * 128:(c + 1) * 128], in_=hf[b][:, c * 128:(c + 1) * 128],
                                                         identity=ident[:]), reads=[d_hf[b], d_id], writes=[d_ptr])
                    k.op("dve", lambda e: e.tensor_copy(out=hb[b][:], in_=ptr[:]), reads=[d_ptr], writes=[d_hb[b]])
                    if router:
                        k.op("dve", lambda e: e.tensor_tensor(out=hTf[b][:].rearrange("p c t -> p (c t)"), in0=ptr[:], in1=hb[b][:], op=ALU.subtract),
                             reads=[d_ptr, d_hb[b]], writes=[d_hTf[b]])
                    k.dma("sp", hout[t], hb[b][:], reads=[d_hb[b]], writes=[Dep()])
                import os
                RL = int(os.environ.get("RL", "9"))
                if router and RL >= 1:
                    hbv = hb[b][:].rearrange("p (c t) -> p c t", c=8)
                    for c in range(8):
                        k.op("pe", lambda e: e.matmul(plg[:, 0:32], lhsT=hbv[:, c, :], rhs=wr_hi[:, c, :],
                                                      start=(c == 0), stop=False), reads=[d_hb[b], d_wr], writes=[d_plg])
                        k.op("pe", lambda e: e.matmul(plg[:, 0:32], lhsT=hbv[:, c, :], rhs=wr_lo[:, c, :],
                                                      start=False, stop=False), reads=[d_hb[b], d_wr], writes=[d_plg])
                        k.op("pe", lambda e: e.matmul(plg[:, 0:32], lhsT=hTf[b][:, c, :], rhs=wr_hi[:, c, :],
                                                      start=False, stop=(c == 7)), reads=[d_hTf[b], d_wr], writes=[d_plg])
                    k.op("dve", lambda e: e.tensor_tensor(out=lg[b][:], in0=plg[:, 0:32], in1=br_bc[:], op=ALU.add),
                         reads=[d_plg, d_br], writes=[d_lg[b]])
                if router and RL >= 2:
                    k.op("dve", lambda e: e.max(out=t8[b][:], in_=lg[b][:]), reads=[d_lg[b]], writes=[d_t8[b]])
                    k.op("dve", lambda e: e.tensor_scalar(out=ex[b][:], in0=lg[b][:], scalar1=t8[b][:, 0:1], scalar2=None,
                                                          op0=ALU.subtract), reads=[d_lg[b], d_t8[b]], writes=[d_ex[b]])
                    k.op("act", lambda e: e.activation(out=ex[b][:], in_=ex[b][:], func=AF.Exp), reads=[], writes=[d_ex[b]])
                    k.op("dve", lambda e: e.scalar_tensor_tensor(out=gt[b][:], in0=lg[b][:], scalar=t8[b][:, 3:4], in1=ex[b][:],
                                                                 op0=ALU.is_ge, op1=ALU.mult),
                         reads=[d_lg[b], d_t8[b], d_ex[b]], writes=[d_gt[b]])
                    k.op("dve", lambda e: e.tensor_reduce(out=t8[b][:, 4:5], in_=gt[b][:], axis=AX.X, op=ALU.add),
                         reads=[d_gt[b]], writes=[d_t8[b]])
                    k.op("dve", lambda e: e.reciprocal(out=t8[b][:, 5:6], in_=t8[b][:, 4:5]), reads=[], writes=[d_t8[b]])
                    k.op("dve", lambda e: e.tensor_scalar(out=gt[b][:], in0=gt[b][:], scalar1=t8[b][:, 5:6], scalar2=None,
                                                          op0=ALU.mult), reads=[d_t8[b]], writes=[d_gt[b]])
                if router and RL >= 3:
                    k.op("pe", lambda e: e.transpose(out=plg[0:32, 128:256], in_=gt[b][:], identity=ident[:]),
                         reads=[d_gt[b], d_id], writes=[d_plg])
                    k.op("dve", lambda e: e.tensor_copy(out=gT[b][:], in_=plg[0:32, 128:256]), reads=[d_plg], writes=[d_gT[b]])
                    k.dma("sp", gout[:, tok], gT[b][:], reads=[d_gT[b]], writes=[Dep()])
        k.finish([])
    return nc


def eps_ap(k):
    if not hasattr(k, "_eps"):
        k._eps = k.sb([128, 1], F32)
        k._eps_d = Dep()
        k.op("pool", lambda e: e.memset(k._eps[:], RMS_EPS), writes=[k._eps_d])
        for eng in ("act", "dve", "pe"):
            k._wait(eng, "pool", k.cnt["pool"])
    return k._eps[:]


def kb_barrier(k):
    cur = {}
    for e in ("pe", "act", "dve", "pool"):
        if k.cnt[e]:
            cur[e] = k.cnt[e]
    for q, ring in k.dmaq.items():
        n = ring["n"]
        for slot, key in enumerate(ring["keys"]):
            uses = n if q == "coll" else (n - slot + k.DMA_RING - 1) // k.DMA_RING
            if uses > 0:
                cur[key] = 16 * uses
    for e in ("pe", "act", "dve", "pool", "sp"):
        for key, v in cur.items():
            if key == e and e in ("pe", "sp"):
                continue
            k._wait(e, key, v)


KB.barrier = kb_barrier

SEQ = 8192
CTX = 256
SA = SEQ + CTX


def b_consts(SEQ=SEQ):
    p = np.arange(128)
    d = p % 64
    hi = d // 32
    dp = d % 32
    i = dp % 16
    freq = (10000.0 ** (-(i.astype(np.float32)) / 16)).astype(np.float32)
    t = np.arange(SEQ)
    pos = np.where(hi[:, None] == 0, (t // 64)[None, :], (t % 64)[None, :]).astype(np.float32)
    ang = pos * freq[:, None]
    cos = np.cos(ang).astype(np.float32)
    sin = np.sin(ang).astype(np.float32)
    prot = np.zeros((128, 128), np.float32)
    for m in range(128):
        if dp[m] < 16:
            prot[m + 16, m] = -1.0
        else:
            prot[m - 16, m] = 1.0
    blk = np.zeros((128, 128), np.float32)
    blk[:64, :64] = 1.0 / 64
    blk[64:, 64:] = 1.0 / 64
    masks = np.zeros((8, 128, 512), np.float32)
    tl = np.arange(512)[None, :]
    for q in range(4):
        s = (128 * q + np.arange(128))[:, None]
        masks[q] = np.where(s <= tl, 0.0, -30000.0)
        masks[4 + q] = np.where(s >= tl, 0.0, -30000.0)
    return {"cos": cos, "sin": sin, "prot": prot, "blk": blk, "masks": masks,
            "ident": np.eye(128, dtype=np.float32)}


def build_b(lam_init, nbatch=2, SEQ=SEQ, phases="PAS"):
    nc = new_nc()
    SA = SEQ + CTX
    NTOK = 2 * SEQ
    hT = nc.dram_tensor("hT", [NTOK // 512, 128, 8, 512], BF16, kind="ExternalInput").ap()
    hcT = nc.dram_tensor("hcT", [2, 128, 8, CTX], BF16, kind="ExternalInput").ap()
    W = nc.dram_tensor("W", [1024, 900], F32, kind="ExternalInput").ap()
    pp = nc.dram_tensor("pp", [128, 32], F32, kind="ExternalInput").ap()
    lamd = nc.dram_tensor("lam4", [4, 64], F32, kind="ExternalInput").ap()
    p4 = nc.dram_tensor("p4", [2, 4], F32, kind="ExternalInput").ap()
    cosd = nc.dram_tensor("cos", [128, SEQ], F32, kind="ExternalInput").ap()
    sind = nc.dram_tensor("sin", [128, SEQ], F32, kind="ExternalInput").ap()
    protd = nc.dram_tensor("prot", [128, 128], F32, kind="ExternalInput").ap()
    blkd = nc.dram_tensor("blk", [128, 128], F32, kind="ExternalInput").ap()
    trid = nc.dram_tensor("tri", [128, 128], F32, kind="ExternalInput").ap()
    maskd = nc.dram_tensor("masks", [8, 128, 512], F32, kind="ExternalInput").ap()
    identd = nc.dram_tensor("ident", [128, 128], F32, kind="ExternalInput").ap()
    oatt = nc.dram_tensor("oT", [128, NTOK], BF16, kind="ExternalOutput").ap()
    ygo = nc.dram_tensor("ygT", [128, NTOK], BF16, kind="ExternalOutput").ap()
    sso = nc.dram_tensor("ss", [1, NTOK], F32, kind="ExternalOutput").ap()
    urow = nc.dram_tensor("urow", [4, SEQ], F32, kind="Internal").ap()
    WQ, WK, WV, WDT, WZ, WX, WBm, WC = 0, 128, 256, 384, 388, 516, 644, 772
    NS = SA // 128
    NL = SEQ // 128
    BUFW = SEQ + 276
    LATR, CTXR = 6, SEQ + 14
    LATC, CTXC = 2, SEQ + 10

    def ccol(s):
        return LATC + 128 * s if s < NL else CTXC + 128 * (s - NL)

    with ExitStack() as es:
        k = KB(nc, es)
        eps = eps_ap(k)
        ident = k.sb([128, 128], F32)
        identb = k.sb([128, 128], BF16)
        prot = k.sb([128, 128], BF16)
        blk = k.sb([128, 128], BF16)
        ones = k.sb([128, 128], BF16)
        onesf = k.sb([128, 128], F32)
        tri = k.sb([128, 128], F32)
        pps = k.sb([128, 32], F32)
        lam = k.sb([128, 8], F32)
        lam_in = k.sb([128, 4, 64], F32)
        p4s = k.sb([128, 2, 4], F32)
        Wb = k.sb([128, 8, 900], BF16)
        dC = Dep()
        with ExitStack() as es2:
            k.es = es2
            stg = k.sb([128, 900], F32)
            k.dma("sp", ident[:], identd, writes=[dC])
            k.dma("sp", tri[:], trid, writes=[dC])
            k.dma("sp", stg[:, 0:128], protd, writes=[dC])
            k.op("dve", lambda e: e.tensor_copy(out=prot[:], in_=stg[:, 0:128]), reads=[dC], writes=[dC])
            k.dma("sp", stg[:, 0:128], blkd, writes=[dC])
            k.op("dve", lambda e: e.tensor_copy(out=blk[:], in_=stg[:, 0:128]), reads=[dC], writes=[dC])
            k.op("dve", lambda e: e.tensor_copy(out=identb[:], in_=ident[:]), reads=[dC], writes=[dC])
            k.op("dve", lambda e: e.memset(ones[:], 1.0), writes=[dC])
            k.op("dve", lambda e: e.memset(onesf[:], 1.0), writes=[dC])
            k.dma("sp", pps[:], pp, writes=[dC])
            k.dma("sp", p4s[:].rearrange("p a r -> p (a r)"), p4.rearrange("a r -> (a r)").partition_broadcast(128), writes=[dC])
            k.dma("sp", lam_in[:].rearrange("p a d -> p (a d)"), lamd.rearrange("a d -> (a d)").partition_broadcast(128), writes=[dC])
            k.op("dve", lambda e: e.tensor_tensor(out=lam_in[:, 0, :], in0=lam_in[:, 0, :], in1=lam_in[:, 1, :], op=ALU.mult), reads=[dC], writes=[dC])
            k.op("dve", lambda e: e.tensor_tensor(out=lam_in[:, 2, :], in0=lam_in[:, 2, :], in1=lam_in[:, 3, :], op=ALU.mult), reads=[dC], writes=[dC])
            k.op("dve", lambda e: e.tensor_reduce(out=lam[:, 0:1], in_=lam_in[:, 0, :], axis=AX.X, op=ALU.add), reads=[dC], writes=[dC])
            k.op("dve", lambda e: e.tensor_reduce(out=lam[:, 1:2], in_=lam_in[:, 2, :], axis=AX.X, op=ALU.add), reads=[dC], writes=[dC])
            k.op("act", lambda e: e.activation(out=lam[:, 0:2], in_=lam[:, 0:2], func=AF.Exp), reads=[dC], writes=[dC])
            k.op("dve", lambda e: e.tensor_tensor(out=lam[:, 2:3], in0=lam[:, 1:2], in1=lam[:, 0:1], op=ALU.subtract), reads=[dC], writes=[dC])
            k.op("dve", lambda e: e.tensor_scalar(out=lam[:, 3:4], in0=lam[:, 2:3], scalar1=-float(lam_init), scalar2=None, op0=ALU.add), reads=[dC], writes=[dC])
            k.op("dve", lambda e: e.tensor_scalar(out=pps[:, 23:24], in0=pps[:, 2:3], scalar1=float(1.0 - lam_init), scalar2=None, op0=ALU.mult), reads=[dC], writes=[dC])
            k.op("act", lambda e: e.activation(out=p4s[:, 1, :], in_=p4s[:, 1, :], func=AF.Exp), reads=[dC], writes=[dC])
            k.op("dve", lambda e: e.tensor_scalar(out=p4s[:, 1, :], in0=p4s[:, 1, :], scalar1=-1.0, scalar2=None, op0=ALU.mult), reads=[dC], writes=[dC])
            for c in range(8):
                k.dma("sp", stg[:], W[c * 128:(c + 1) * 128, :], writes=[dC])
                k.op("dve", lambda e: e.tensor_copy(out=Wb[:, c, :], in_=stg[:]), reads=[dC], writes=[dC])
            k.barrier()
        k.es = es

        xr = k.sb([128, BUFW], BF16)
        Br = k.sb([128, BUFW], BF16)
        Cr = k.sb([128, BUFW], BF16)
        sz = k.sb([128, SEQ], BF16)
        dtc = k.sb([128, NS, 4], F32)
        for b in range(nbatch):
            d_q, d_k, d_v, d_x, d_B, d_Cc, d_z, d_dt = (Dep() for _ in range(8))
            with ExitStack() as esm:
                k.es = esm
                qT = k.sb([128, SEQ], BF16)
                kT = k.sb([128, SA], BF16)
                vS = k.sb([128, NS, 128], BF16)
                with ExitStack() as es2:
                    k.es = es2
                    hb = [k.sb([128, 8, 512], BF16) for _ in range(2)]
                    d_hb = [Dep(), Dep()]
                    cs_ = [k.sb([128, 512], F32) for _ in range(2)]
                    sn_ = [k.sb([128, 512], F32) for _ in range(2)]
                    d_cs = [Dep(), Dep()]
                    sq = k.sb([128, 512], BF16); d_sq = Dep()
                    rst = k.sb([128, 512], F32); d_rst = Dep()
                    qn = k.sb([128, 512], F32); d_qn = Dep()
                    qnb = k.sb([128, 512], BF16); d_qnb = Dep()
                    t1 = k.sb([128, 512], F32); d_t1 = Dep()
                    pf = [k.ps([128, 512]) for _ in range(3)]
                    d_pf = [Dep() for _ in range(3)]
                    pms = k.ps([128, 512]); d_pms = Dep()
                    prt = k.ps([128, 512]); d_prt = Dep()
                    pv = k.ps([128, 512]); d_pv = Dep()
                    pdt = k.ps([128, 512]); d_pdt = Dep()
                    for buf, dd in ((xr, d_x), (Br, d_B), (Cr, d_Cc)):
                        k.op("dve", lambda e: e.memset(buf[:, 0:LATR], 0.0), writes=[dd])
                        k.op("dve", lambda e: e.memset(buf[:, LATR + SEQ:CTXR], 0.0), writes=[dd])
                        k.op("dve", lambda e: e.memset(buf[:, CTXR + CTX:BUFW], 0.0), writes=[dd])
                    nblk = SEQ // 512 + 1
                    ipf = [0]
                    for blk_i in range(nblk):
                        isctx = blk_i == nblk - 1
                        n = 256 if isctx else 512
                        hbuf = hb[blk_i % 2]
                        dh = d_hb[blk_i % 2]
                        if isctx:
                            src = hcT[b]
                        else:
                            src = hT[b * (SEQ // 512) + blk_i]
                        k.dma("sp", hbuf[:, :, 0:n], src, writes=[dh])
                        t0 = blk_i * 512
                        dcs = d_cs[blk_i % 2]
                        if not isctx:
                            k.dma("sp", cs_[blk_i % 2][:], cosd[:, t0:t0 + 512], writes=[dcs])
                            k.dma("sp", sn_[blk_i % 2][:], sind[:, t0:t0 + 512], writes=[dcs])

                        def proj(wcol, width=128):
                            j = ipf[0] % 3
                            ipf[0] += 1
                            for c in range(8):
                                k.op("pe", lambda e: e.matmul(pf[j][0:width, 0:n], lhsT=Wb[:, c, wcol:wcol + width], rhs=hbuf[:, c, 0:n],
                                                              start=(c == 0), stop=(c == 7)), reads=[dh], writes=[d_pf[j]])
                            return pf[j], d_pf[j]

                        for which in ("q", "k"):
                            if which == "q" and isctx:
                                continue
                            ps_, dps_ = proj(WQ if which == "q" else WK)
                            gcol = 0 if which == "q" else 1
                            k.op("act", lambda e: e.activation(out=sq[:, 0:n], in_=ps_[:, 0:n], func=AF.Square), reads=[dps_], writes=[d_sq])
                            k.op("pe", lambda e: e.matmul(pms[:, 0:n], lhsT=blk[:], rhs=sq[:, 0:n], start=True, stop=True), reads=[d_sq], writes=[d_pms])
                            k.op("act", lambda e: e.activation(out=rst[:, 0:n], in_=pms[:, 0:n], func=AF.Sqrt, bias=eps, scale=1.0), reads=[d_pms], writes=[d_rst])
                            k.op("dve", lambda e: e.reciprocal(out=rst[:, 0:n], in_=rst[:, 0:n]), reads=[], writes=[d_rst])
                            if isctx:
                                k.op("dve", lambda e: e.scalar_tensor_tensor(out=kT[:, SEQ:SA], in0=ps_[:, 0:n], scalar=pps[:, gcol:gcol + 1], in1=rst[:, 0:n],
                                                                             op0=ALU.mult, op1=ALU.mult), reads=[dps_, d_rst], writes=[d_k])
                                continue
                            k.op("dve", lambda e: e.scalar_tensor_tensor(out=qn[:], in0=ps_[:], scalar=pps[:, gcol:gcol + 1], in1=rst[:],
                                                                         op0=ALU.mult, op1=ALU.mult), reads=[dps_, d_rst], writes=[d_qn])
                            k.op("act", lambda e: e.activation(out=qnb[:], in_=qn[:], func=AF.Copy), reads=[d_qn], writes=[d_qnb])
                            k.op("pe", lambda e: e.matmul(prt[:], lhsT=prot[:], rhs=qnb[:], start=True, stop=True), reads=[d_qnb], writes=[d_prt])
                            k.op("dve", lambda e: e.tensor_tensor(out=t1[:], in0=prt[:], in1=sn_[blk_i % 2][:], op=ALU.mult), reads=[d_prt, dcs], writes=[d_t1])
                            k.op("dve", lambda e: e.tensor_tensor(out=qn[:], in0=qn[:], in1=cs_[blk_i % 2][:], op=ALU.mult), reads=[dcs], writes=[d_qn])
                            dst = qT if which == "q" else kT
                            dd = d_q if which == "q" else d_k
                            k.op("dve", lambda e: e.tensor_tensor(out=dst[:, t0:t0 + 512], in0=qn[:], in1=t1[:], op=ALU.add), reads=[d_qn, d_t1], writes=[dd])
                        c0 = (CTXR if isctx else LATR + t0)
                        for wcol, buf, dd in ((WX, xr, d_x), (WBm, Br, d_B), (WC, Cr, d_Cc)):
                            ps_, dps_ = proj(wcol)
                            k.op("act", lambda e: e.activation(out=buf[:, c0:c0 + n], in_=ps_[:, 0:n], func=AF.Copy), reads=[dps_], writes=[dd])
                        if not isctx:
                            ps_, dps_ = proj(WZ)
                            k.op("act", lambda e: e.activation(out=sz[:, t0:t0 + 512], in_=ps_[:], func=AF.Silu), reads=[dps_], writes=[d_z])
                        ti = (NL if isctx else blk_i * 4)
                        for tt in range(n // 128):
                            for c in range(8):
                                k.op("pe", lambda e: e.matmul(pv[:, tt * 128:(tt + 1) * 128], lhsT=hbuf[:, c, tt * 128:(tt + 1) * 128], rhs=Wb[:, c, WV:WV + 128],
                                                              start=(c == 0), stop=(c == 7)), reads=[dh], writes=[d_pv])
                            for c in range(8):
                                k.op("pe", lambda e: e.matmul(pdt[:, tt * 4:(tt + 1) * 4], lhsT=hbuf[:, c, tt * 128:(tt + 1) * 128], rhs=Wb[:, c, WDT:WDT + 4],
                                                              start=(c == 0), stop=(c == 7)), reads=[dh], writes=[d_pdt])
                        k.op("dve", lambda e: e.tensor_copy(out=vS[:, ti:ti + n // 128, :].rearrange("p a e -> p (a e)"), in_=pv[:, 0:n]), reads=[d_pv], writes=[d_v])
                        k.op("dve", lambda e: e.tensor_copy(out=dtc[:, ti:ti + n // 128, :].rearrange("p a r -> p (a r)"), in_=pdt[:, 0:n // 32]), reads=[d_pdt], writes=[d_dt])
                    k.barrier()
                k.es = esm
                with ExitStack() as es2:
                    k.es = es2
                    pS = [k.ps([128, 512]) for _ in range(3)]
                    d_pS = [Dep() for _ in range(3)]
                    po = [k.ps([128, 512]) for _ in range(2)]
                    psm = [k.ps([128, 512]) for _ in range(2)]
                    d_po = [Dep(), Dep()]
                    d_psm = [Dep(), Dep()]
                    Pt = [k.sb([128, 512], BF16) for _ in range(3)]
                    d_Pt = [Dep() for _ in range(3)]
                    r_ = [k.sb([128, 512], F32) for _ in range(2)]
                    d_r = [Dep(), Dep()]
                    o_ = k.sb([128, 512], F32); d_o = Dep()
                    osq = k.sb([128, 512], BF16); d_osq = Dep()
                    ob = k.sb([128, 512], BF16); d_ob = Dep()
                    for tb in range(SEQ // 512 if "A" in phases else 0):
                        tsl = slice(tb * 512, (tb + 1) * 512)
                        items = [(c, s) for c in range(2) for s in range(NS)]

                        def qk(i):
                            c, s = items[i]
                            j = i % 3
                            k.op("pe", lambda e: e.matmul(pS[j][:], lhsT=kT[c * 64:(c + 1) * 64, s * 128:(s + 1) * 128], rhs=qT[c * 64:(c + 1) * 64, tsl],
                                                          start=True, stop=True), reads=[d_q, d_k], writes=[d_pS[j]])
                            k.op("act", lambda e: e.activation(out=Pt[j][:], in_=pS[j][:], func=AF.Exp, scale=0.125), reads=[d_pS[j]], writes=[d_Pt[j]])

                        def pvm(i):
                            c, s = items[i]
                            j = i % 3
                            k.op("pe", lambda e: e.matmul(po[c][:], lhsT=vS[:, s, :], rhs=Pt[j][:], start=(s == 0), stop=(s == NS - 1)),
                                 reads=[d_v, d_Pt[j]], writes=[d_po[c]])
                            k.op("pe", lambda e: e.matmul(psm[c][:], lhsT=ones[:], rhs=Pt[j][:], start=(s == 0), stop=(s == NS - 1)),
                                 reads=[d_Pt[j]], writes=[d_psm[c]])

                        qk(0)
                        for i in range(len(items)):
                            if i + 1 < len(items):
                                qk(i + 1)
                            pvm(i)
                        for c in range(2):
                            k.op("dve", lambda e: e.reciprocal(out=r_[c][:], in_=psm[c][:]), reads=[d_psm[c]], writes=[d_r[c]])
                            k.op("dve", lambda e: e.tensor_tensor(out=r_[c][:], in0=po[c][:], in1=r_[c][:], op=ALU.mult), reads=[d_po[c]], writes=[d_r[c]])
                        k.op("dve", lambda e: e.scalar_tensor_tensor(out=o_[:], in0=r_[1][:], scalar=lam[:, 3:4], in1=r_[0][:], op0=ALU.mult, op1=ALU.add),
                             reads=[d_r[0], d_r[1]], writes=[d_o])
                        k.op("act", lambda e: e.activation(out=osq[:], in_=o_[:], func=AF.Square), reads=[d_o], writes=[d_osq])
                        k.op("pe", lambda e: e.matmul(pS[0][:], lhsT=ones[:], rhs=osq[:], start=True, stop=True), reads=[d_osq], writes=[d_pS[0]])
                        k.op("act", lambda e: e.activation(out=r_[0][:], in_=pS[0][:], func=AF.Sqrt, bias=eps, scale=1.0 / 128), reads=[d_pS[0]], writes=[d_r[0]])
                        k.op("dve", lambda e: e.reciprocal(out=r_[0][:], in_=r_[0][:]), reads=[], writes=[d_r[0]])
                        k.op("dve", lambda e: e.scalar_tensor_tensor(out=ob[:], in0=o_[:], scalar=pps[:, 23:24], in1=r_[0][:], op0=ALU.mult, op1=ALU.mult),
                             reads=[d_o, d_r[0]], writes=[d_ob])
                        k.dma("sp", oatt[:, b * SEQ + tb * 512: b * SEQ + (tb + 1) * 512], ob[:], reads=[d_ob], writes=[Dep()])
                    k.barrier()
                k.es = esm
            k.es = es
            with ExitStack() as es2:
              if "S" in phases:
                  k.es = es2
                  xtok = k.sb([128, NS, 128], BF16); d_xt = Dep()
                  acc = [k.sb([128, 2048], F32) for _ in range(2)]
                  d_acc = [Dep(), Dep()]
                  for (buf, wc0, dd) in ((xr, 3, d_x), (Br, 9, d_B), (Cr, 15, d_Cc)):
                      segs = [(LATR + s0, min(2048, SEQ)) for s0 in range(0, SEQ, 2048)] + [(CTXR, CTX)]
                      for si, (rc, n) in enumerate(segs):
                          a_ = acc[si % 2]; da = d_acc[si % 2]
                          k.op("dve", lambda e: e.tensor_scalar(out=a_[:, 0:n], in0=buf[:, rc - 2:rc - 2 + n], scalar1=pps[:, wc0:wc0 + 1], scalar2=None, op0=ALU.mult),
                               reads=[dd], writes=[da])
                          for tap in range(1, 5):
                              k.op("dve", lambda e: e.scalar_tensor_tensor(out=a_[:, 0:n], in0=buf[:, rc - 2 + tap:rc - 2 + tap + n], scalar=pps[:, wc0 + tap:wc0 + tap + 1],
                                                                           in1=a_[:, 0:n], op0=ALU.mult, op1=ALU.add), reads=[dd], writes=[da])
                          k.op("act", lambda e: e.activation(out=buf[:, rc - 4:rc - 4 + n], in_=a_[:, 0:n], func=AF.Silu, bias=pps[:, wc0 + 5:wc0 + 6], scale=1.0),
                               reads=[da], writes=[dd])
                  xc, Bc, Cc = xr, Br, Cr
                  dA = k.sb([128, NS, 4], F32); d_dA = Dep()
                  wi = k.sb([128, NS, 4], F32); d_wi = Dep()
                  tot = k.sb([128, NS, 4], F32); d_tot = Dep()
                  inc = k.sb([128, NS, 4], F32); d_inc = Dep()
                  ncol = k.sb([128, NS, 4], F32); d_nc = Dep()
                  on66 = k.sb([128, NS], F32); d_on = Dep()
                  pw = k.ps([128, 512]); d_pw = Dep()
                  pt_ = k.ps([128, 512]); d_pt = Dep()
                  k.op("dve", lambda e: e.memset(on66[:], 1.0), writes=[d_on])
                  for r in range(4):
                      k.op("dve", lambda e: e.tensor_scalar(out=dtc[:, :, r], in0=dtc[:, :, r], scalar1=p4s[:, 0, r:r + 1], scalar2=None, op0=ALU.add), reads=[d_dt], writes=[d_dt])
                  k.op("act", lambda e: e.activation(out=dtc[:].rearrange("p a r -> p (a r)"), in_=dtc[:].rearrange("p a r -> p (a r)"), func=AF.Exp), reads=[], writes=[d_dt])
                  k.op("act", lambda e: e.activation(out=dtc[:].rearrange("p a r -> p (a r)"), in_=dtc[:].rearrange("p a r -> p (a r)"), func=AF.Ln, bias=1.0, scale=1.0), reads=[], writes=[d_dt])
                  for r in range(4):
                      k.op("dve", lambda e: e.tensor_scalar(out=dA[:, :, r], in0=dtc[:, :, r], scalar1=p4s[:, 1, r:r + 1], scalar2=None, op0=ALU.mult), reads=[d_dt], writes=[d_dA])
                  k.op("pe", lambda e: e.matmul(pw[:, 0:NS * 4], lhsT=tri[:], rhs=dA[:].rearrange("p a r -> p (a r)"), start=True, stop=True), reads=[d_dA], writes=[d_pw])
                  k.op("pe", lambda e: e.matmul(pt_[:, 0:NS * 4], lhsT=onesf[:], rhs=dA[:].rearrange("p a r -> p (a r)"), start=True, stop=True), reads=[d_dA], writes=[d_pt])
                  k.op("dve", lambda e: e.tensor_copy(out=wi[:].rearrange("p a r -> p (a r)"), in_=pw[:, 0:NS * 4]), reads=[d_pw], writes=[d_wi])
                  k.op("dve", lambda e: e.tensor_copy(out=tot[:].rearrange("p a r -> p (a r)"), in_=pt_[:, 0:NS * 4]), reads=[d_pt], writes=[d_tot])
                  for r in range(2):
                      k.op("dve", lambda e: e.tensor_tensor_scan(out=inc[:, NL:NS, r], data0=on66[:, NL:NS], data1=tot[:, NL:NS, r], initial=0.0, op0=ALU.mult, op1=ALU.add),
                           reads=[d_tot, d_on], writes=[d_inc])
                      k.op("dve", lambda e: e.tensor_tensor_scan(out=inc[:, 0:NL, r], data0=on66[:, 0:NL], data1=tot[:, 0:NL, r], initial=inc[:, NS - 1, r:r + 1], op0=ALU.mult, op1=ALU.add),
                           reads=[d_tot, d_on], writes=[d_inc])
                  for r in range(2, 4):
                      k.op("dve", lambda e: e.tensor_tensor_scan(out=inc[:, :, r], data0=on66[:], data1=tot[:, :, r], initial=0.0, op0=ALU.mult, op1=ALU.add),
                           reads=[d_tot, d_on], writes=[d_inc])
                  k.op("dve", lambda e: e.tensor_tensor(out=ncol[:], in0=tot[:], in1=inc[:], op=ALU.subtract), reads=[d_tot, d_inc], writes=[d_nc])
                  k.op("dve", lambda e: e.tensor_tensor(out=ncol[:, :, 0:2], in0=ncol[:, :, 0:2], in1=wi[:, :, 0:2], op=ALU.subtract), reads=[d_wi], writes=[d_nc])
                  k.op("dve", lambda e: e.tensor_tensor(out=ncol[:, :, 2:4], in0=wi[:, :, 2:4], in1=ncol[:, :, 2:4], op=ALU.subtract), reads=[d_wi], writes=[d_nc])
                  k.op("dve", lambda e: e.tensor_tensor(out=ncol[:, :, 2:4], in0=ncol[:, :, 2:4], in1=dA[:, :, 2:4], op=ALU.subtract), reads=[d_dA], writes=[d_nc])
                  k.op("dve", lambda e: e.tensor_scalar(out=wi[:], in0=ncol[:], scalar1=-1.0, scalar2=None, op0=ALU.mult), reads=[d_nc], writes=[d_wi])
                  ur = [k.sb([4, 512], F32) for _ in range(2)]
                  d_ur = [Dep(), Dep()]
                  d_urow = Dep()
                  for g in range(NL // 4):
                      for q in range(4):
                          k.op("pe", lambda e: e.transpose(out=pw[0:4, q * 128:(q + 1) * 128], in_=wi[:, 4 * g + q, :], identity=ident[:]), reads=[d_wi], writes=[d_pw])
                      k.op("dve", lambda e: e.tensor_copy(out=ur[g % 2][:], in_=pw[0:4, :]), reads=[d_pw], writes=[d_ur[g % 2]])
                      k.dma("sp", urow[:, g * 512:(g + 1) * 512], ur[g % 2][:], reads=[d_ur[g % 2]], writes=[d_urow])
                  ptx = k.ps([128, 1024], BF16); d_ptx = Dep()
                  for s0 in range(0, NS, 8):
                      ns_ = min(8, NS - s0)
                      for s in range(ns_):
                          cc = ccol(s0 + s)
                          k.op("pe", lambda e: e.transpose(out=ptx[:, s * 128:(s + 1) * 128], in_=xc[:, cc:cc + 128], identity=identb[:]),
                               reads=[d_x], writes=[d_ptx])
                      k.op("dve", lambda e: e.tensor_copy(out=xtok[:, s0:s0 + ns_, :].rearrange("p a e -> p (a e)"), in_=ptx[:, 0:ns_ * 128]),
                           reads=[d_ptx], writes=[d_xt])
                  msk = k.sb([128, 8, 512], BF16); d_msk = Dep()
                  for q in range(8):
                      k.dma("sp", acc[q % 2][:, 0:512], maskd[q], writes=[d_acc[q % 2]])
                      k.op("dve", lambda e: e.tensor_copy(out=msk[:, q, :], in_=acc[q % 2][:, 0:512]), reads=[d_acc[q % 2]], writes=[d_msk])
                  ub = [k.sb([128, 4, 512], F32) for _ in range(2)]
                  d_ub = [Dep(), Dep()]
                  pCB = [k.ps([128, 512]) for _ in range(2)]
                  d_pCB = [Dep(), Dep()]
                  py = [k.ps([128, 512]) for _ in range(2)]
                  d_py = [Dep(), Dep()]
                  pss = k.ps([128, 512]); d_pss = Dep()
                  Et = [k.sb([128, 512], F32) for _ in range(2)]
                  d_Et = [Dep(), Dep()]
                  Wt = [k.sb([128, 512], BF16) for _ in range(2)]
                  d_Wt = [Dep(), Dep()]
                  mt = [k.sb([128, 512], F32) for _ in range(2)]
                  d_mt = [Dep(), Dep()]
                  yv = k.sb([128, 512], F32); d_yv = Dep()
                  ygb = k.sb([128, 512], BF16); d_ygb = Dep()
                  ysq = k.sb([128, 512], BF16); d_ysq = Dep()
                  ssr = k.sb([1, 512], F32); d_ssr = Dep()
                  for tb in range(SEQ // 512):
                      tsl = slice(tb * 512, (tb + 1) * 512)
                      csl = slice(LATC + tb * 512, LATC + (tb + 1) * 512)
                      u_ = ub[tb % 2]; du = d_ub[tb % 2]
                      for r in range(4):
                          k.dma("sp", u_[:, r, :], urow[r, tsl].partition_broadcast(128), reads=[d_urow], writes=[du])
                      items = []
                      for s in list(range(NL, NS)) + list(range(0, 4 * tb + 4)):
                          mq = (s - 4 * tb) if (s < NL and s >= 4 * tb) else None
                          items.append((0, s, mq))
                      for s in list(range(4 * tb, NL)) + list(range(NL, NS)):
                          mq = (4 + s - 4 * tb) if (s < 4 * tb + 4) else None
                          items.append((1, s, mq))
                      nit = len(items)

                      def cb(i):
                          dr_, s, mq = items[i]
                          j = i % 2
                          cc = ccol(s)
                          k.op("pe", lambda e: e.matmul(pCB[j][:], lhsT=Bc[:, cc:cc + 128], rhs=Cc[:, csl], start=True, stop=True),
                               reads=[d_B, d_Cc], writes=[d_pCB[j]])

                      def rest(i):
                          dr_, s, mq = items[i]
                          j = i % 2
                          for hd in range(2):
                              r = dr_ * 2 + hd
                              jj = hd
                              src = u_[:, r, :]
                              rd = [du]
                              if mq is not None:
                                  k.op("pool", lambda e: e.tensor_tensor(out=mt[jj][:], in0=u_[:, r, :], in1=msk[:, mq, :], op=ALU.add),
                                       reads=[du, d_msk], writes=[d_mt[jj]])
                                  src = mt[jj][:]
                                  rd = [d_mt[jj]]
                              k.op("act", lambda e: e.activation(out=Et[jj][:], in_=src, func=AF.Exp, bias=ncol[:, s, r:r + 1], scale=1.0),
                                   reads=rd + [d_nc], writes=[d_Et[jj]])
                              k.op("dve", lambda e: e.scalar_tensor_tensor(out=Wt[jj][:], in0=Et[jj][:], scalar=dtc[:, s, r:r + 1], in1=pCB[j][:],
                                                                           op0=ALU.mult, op1=ALU.mult), reads=[d_Et[jj], d_dt, d_pCB[j]], writes=[d_Wt[jj]])
                              k.op("pe", lambda e: e.matmul(py[hd][hd * 64:(hd + 1) * 64, :], lhsT=xtok[:, s, hd * 64:(hd + 1) * 64], rhs=Wt[jj][:],
                                                            start=(i == 0), stop=(i == nit - 1)),
                                   reads=[d_xt, d_Wt[jj]], writes=[d_py[hd]])

                      cb(0)
                      for i in range(nit):
                          if i + 1 < nit:
                              cb(i + 1)
                          rest(i)
                      for hd in range(2):
                          hs = slice(hd * 64, (hd + 1) * 64)
                          k.op("dve", lambda e: e.scalar_tensor_tensor(out=yv[hs, :], in0=xc[hs, csl], scalar=pps[hs, 21:22], in1=py[hd][hs, :],
                                                                       op0=ALU.mult, op1=ALU.add), reads=[d_x, d_py[hd]], writes=[d_yv])
                      k.op("dve", lambda e: e.tensor_tensor(out=yv[:], in0=yv[:], in1=sz[:, tsl], op=ALU.mult), reads=[d_z], writes=[d_yv])
                      k.op("act", lambda e: e.activation(out=ysq[:], in_=yv[:], func=AF.Square), reads=[d_yv], writes=[d_ysq])
                      k.op("pe", lambda e: e.matmul(pss[0:1, :], lhsT=ones[:, 0:1], rhs=ysq[:], start=True, stop=True), reads=[d_ysq], writes=[d_pss])
                      k.op("dve", lambda e: e.tensor_copy(out=ssr[:], in_=pss[0:1, :]), reads=[d_pss], writes=[d_ssr])
                      k.op("dve", lambda e: e.tensor_scalar(out=ygb[:], in0=yv[:], scalar1=pps[:, 22:23], scalar2=None, op0=ALU.mult), reads=[d_yv], writes=[d_ygb])
                      k.dma("sp", ygo[:, b * SEQ + tb * 512: b * SEQ + (tb + 1) * 512], ygb[:], reads=[d_ygb], writes=[Dep()])
                      k.dma("sp", sso[:, b * SEQ + tb * 512: b * SEQ + (tb + 1) * 512], ssr[:], reads=[d_ssr], writes=[Dep()])
                  k.barrier()
            k.es = es
        k.finish([])
    return nc


SW_LIMIT = 7.0
SW_ALPHA = 1.702


def build_d(NTOK=16384):
    nc = new_nc()
    h2T = nc.dram_tensor("h2T", [NTOK // 512, 128, 8, 512], BF16, kind="ExternalInput").ap()
    gT = nc.dram_tensor("gT", [4, NTOK], F32, kind="ExternalInput").ap()
    wgu = nc.dram_tensor("wgu", [4, 1024, 2048], F32, kind="ExternalInput").ap()
    bgu = nc.dram_tensor("bgu", [128, 64], F32, kind="ExternalInput").ap()
    wdn = nc.dram_tensor("wdn", [4, 1024, 1024], F32, kind="ExternalInput").ap()
    bdn = nc.dram_tensor("bdn", [4, 1024], F32, kind="ExternalInput").ap()
    part = nc.dram_tensor("part", [NTOK, 1024], BF16, kind="ExternalOutput").ap()
    scr = nc.dram_tensor("scr", [NTOK, 1024], F32, kind="Internal").ap()
    NBLK = NTOK // 512
    with ExitStack() as es:
        k = KB(nc, es)
        wg = k.sb([128, 2, 8, 2048], BF16); d_wg = Dep()
        wd = k.sb([128, 2, 8, 1024], BF16); d_wd = Dep()
        bd = k.sb([128, 2, 1024], BF16); d_bd = Dep()
        bg = k.sb([128, 64], F32); d_bg = Dep()
        stg = [k.sb([128, 2048], F32) for _ in range(2)]
        d_stg = [Dep(), Dep()]
        hb = [k.sb([128, 8, 512], BF16) for _ in range(2)]
        d_hb = [Dep(), Dep()]
        gb = [k.sb([128, 2, 512], F32) for _ in range(2)]
        d_gb = [Dep(), Dep()]
        gr = [k.sb([1, 2, 512], F32) for _ in range(2)]
        grb = [k.sb([128, 2, 512], BF16) for _ in range(2)]
        d_gr = [Dep(), Dep()]
        d_grb = [Dep(), Dep()]
        act = k.sb([128, 16, 512], BF16)
        d_act = [Dep() for _ in range(16)]
        tg = [k.sb([128, 512], F32) for _ in range(2)]
        ts_ = [k.sb([128, 512], F32) for _ in range(2)]
        tl = [k.sb([128, 512], F32) for _ in range(2)]
        d_tg = [Dep(), Dep()]
        d_ts = [Dep(), Dep()]
        d_tl = [Dep(), Dep()]
        ot = [k.sb([128, 1024], F32) for _ in range(2)]
        d_ot = [Dep(), Dep()]
        otb = [k.sb([128, 1024], BF16) for _ in range(2)]
        d_otb = [Dep(), Dep()]
        pvt = [k.sb([128, 1024], F32) for _ in range(2)]
        d_pvt = [Dep(), Dep()]
        pG = [k.ps([128, 512]) for _ in range(2)]
        pL = [k.ps([128, 512]) for _ in range(2)]
        d_pG = [Dep(), Dep()]
        d_pL = [Dep(), Dep()]
        pY = [k.ps([128, 512]) for _ in range(2)]
        d_pY = [Dep(), Dep()]
        d_part = [Dep() for _ in range(NBLK * 4)]
        k.dma("sp", bg[:], bgu, writes=[d_bg])
        k.op("dve", lambda e: e.memset(bd[:], 0.0), writes=[d_bd])
        for b_ in range(2):
            k.op("dve", lambda e: e.memset(grb[b_][:], 0.0), writes=[d_grb[b_]])
        nst = 0
        for ps_ in range(2):
            for el in range(2):
                e_ = 2 * ps_ + el
                for kc in range(8):
                    s_, ds_ = stg[nst % 2], d_stg[nst % 2]
                    nst += 1
                    k.dma("sp", s_[:], wgu[e_, kc * 128:(kc + 1) * 128, :], writes=[ds_])
                    v = s_[:].rearrange("p (f two) -> p two f", two=2)
                    k.op("act", lambda e: e.activation(out=wg[:, el, kc, 0:1024], in_=v[:, 0, :], func=AF.Copy), reads=[ds_], writes=[d_wg])
                    k.op("pool", lambda e: e.tensor_copy(out=wg[:, el, kc, 1024:2048], in_=v[:, 1, :]), reads=[ds_], writes=[d_wg])
                for fc in range(0, 8, 2):
                    s_, ds_ = stg[nst % 2], d_stg[nst % 2]
                    nst += 1
                    k.dma("sp", s_[:].rearrange("p (a d) -> p a d", a=2), wdn[e_, fc * 128:(fc + 2) * 128, :].rearrange("(a p) d -> p a d", p=128), writes=[ds_])
                    k.op("dve", lambda e: e.tensor_copy(out=wd[:, el, fc:fc + 2, :].rearrange("p a d -> p (a d)"), in_=s_[:]), reads=[ds_], writes=[d_wd])
                s_, ds_ = stg[nst % 2], d_stg[nst % 2]
                nst += 1
                k.dma("sp", s_[0:1, 0:1024], bdn[e_:e_ + 1, :], writes=[ds_])
                k.op("dve", lambda e: e.tensor_copy(out=bd[0:1, el, :], in_=s_[0:1, 0:1024]), reads=[ds_], writes=[d_bd])
            for blk_i in range(NBLK):
                b = blk_i % 2
                tsl = slice(blk_i * 512, (blk_i + 1) * 512)
                k.dma("sp", hb[b][:], h2T[blk_i], writes=[d_hb[b]])
                for el in range(2):
                    k.dma("sp", gb[b][:, el, :], gT[2 * ps_ + el, tsl].partition_broadcast(128), writes=[d_gb[b]])
                k.dma("sp", gr[b][:].rearrange("o a t -> o (a t)").rearrange("o (a t) -> o a t", a=2), gT[2 * ps_:2 * ps_ + 2, tsl].rearrange("(o a) t -> o a t", o=1), writes=[d_gr[b]])
                k.op("dve", lambda e: e.tensor_copy(out=grb[b][0:1, :, :], in_=gr[b][:]), reads=[d_gr[b]], writes=[d_grb[b]])
                gi = 0
                for el in range(2):
                    for fc in range(8):
                        j = gi % 2
                        gi += 1
                        for kc in range(8):
                            k.op("pe", lambda e: e.matmul(pG[j][:], lhsT=wg[:, el, kc, fc * 128:(fc + 1) * 128], rhs=hb[b][:, kc, :],
                                                          start=(kc == 0), stop=(kc == 7)), reads=[d_wg, d_hb[b]], writes=[d_pG[j]])
                        for kc in range(8):
                            k.op("pe", lambda e: e.matmul(pL[j][:], lhsT=wg[:, el, kc, 1024 + fc * 128:1024 + (fc + 1) * 128], rhs=hb[b][:, kc, :],
                                                          start=(kc == 0), stop=(kc == 7)), reads=[d_wg, d_hb[b]], writes=[d_pL[j]])
                        bc = (2 * ps_ + el) * 16 + fc
                        k.op("dve", lambda e: e.tensor_scalar(out=tg[j][:], in0=pG[j][:], scalar1=bg[:, bc:bc + 1], scalar2=SW_LIMIT, op0=ALU.add, op1=ALU.min),
                             reads=[d_pG[j], d_bg], writes=[d_tg[j]])
                        k.op("act", lambda e: e.activation(out=ts_[j][:], in_=tg[j][:], func=AF.Sigmoid, scale=SW_ALPHA), reads=[d_tg[j]], writes=[d_ts[j]])
                        k.op("dve", lambda e: e.tensor_scalar(out=tl[j][:], in0=pL[j][:], scalar1=bg[:, bc + 8:bc + 9], scalar2=SW_LIMIT, op0=ALU.add, op1=ALU.min),
                             reads=[d_pL[j], d_bg], writes=[d_tl[j]])
                        k.op("dve", lambda e: e.tensor_scalar(out=tl[j][:], in0=tl[j][:], scalar1=-SW_LIMIT, scalar2=1.0, op0=ALU.max, op1=ALU.add),
                             reads=[], writes=[d_tl[j]])
                        k.op("pool", lambda e: e.tensor_tensor(out=ts_[j][:], in0=ts_[j][:], in1=tg[j][:], op=ALU.mult), reads=[d_tg[j]], writes=[d_ts[j]])
                        k.op("pool", lambda e: e.tensor_tensor(out=ts_[j][:], in0=ts_[j][:], in1=tl[j][:], op=ALU.mult), reads=[d_tl[j]], writes=[d_ts[j]])
                        ai = el * 8 + fc
                        k.op("dve", lambda e: e.tensor_tensor(out=act[:, ai, :], in0=ts_[j][:], in1=gb[b][:, el, :], op=ALU.mult),
                             reads=[d_ts[j], d_gb[b]], writes=[d_act[ai]])
                for tt in range(4):
                    o_, do_ = (ot[tt % 2], d_ot[tt % 2]) if ps_ == 0 else (otb[tt % 2], d_otb[tt % 2])
                    row = slice(blk_i * 512 + tt * 128, blk_i * 512 + (tt + 1) * 128)
                    dp = d_part[blk_i * 4 + tt]
                    if ps_ == 1:
                        k.dma("sp", pvt[tt % 2][:], scr[row, :], reads=[dp], writes=[d_pvt[tt % 2]])
                    for h in range(2):
                        hs = slice(h * 512, (h + 1) * 512)
                        n = 0
                        for el in range(2):
                            for fc in range(8):
                                ai = el * 8 + fc
                                k.op("pe", lambda e: e.matmul(pY[h][:], lhsT=act[:, ai, tt * 128:(tt + 1) * 128], rhs=wd[:, el, fc, hs],
                                                              start=(n == 0), stop=False), reads=[d_act[ai], d_wd], writes=[d_pY[h]])
                                n += 1
                        for el in range(2):
                            k.op("pe", lambda e: e.matmul(pY[h][:], lhsT=grb[b][:, el, tt * 128:(tt + 1) * 128], rhs=bd[:, el, hs],
                                                          start=False, stop=(el == 1)), reads=[d_grb[b], d_bd], writes=[d_pY[h]])
                        if ps_ == 0:
                            k.op("act", lambda e: e.activation(out=o_[:, hs], in_=pY[h][:], func=AF.Copy), reads=[d_pY[h]], writes=[do_])
                        else:
                            k.op("dve", lambda e: e.tensor_tensor(out=o_[:, hs], in0=pY[h][:], in1=pvt[tt % 2][:, hs], op=ALU.add),
                                 reads=[d_pY[h], d_pvt[tt % 2]], writes=[do_])
                    k.dma("sp", (scr if ps_ == 0 else part)[row, :], o_[:], reads=[do_], writes=[dp])
        k.finish([])
    return nc


def d_inputs(inp, layer, j):
    es = slice(4 * j, 4 * j + 4)
    bgu = inp["b_gate_up"][layer][es]
    b = bgu.reshape(4, 8, 128, 2)
    bg = np.concatenate([b[..., 0], b[..., 1]], 1)
    bg = np.ascontiguousarray(bg.transpose(2, 0, 1).reshape(128, 64))
    return {"wgu": np.ascontiguousarray(inp["w_gate_up"][layer][es]), "bgu": bg,
            "wdn": np.ascontiguousarray(inp["w_down"][layer][es]), "bdn": np.ascontiguousarray(inp["b_down"][layer][es])}


def f_consts():
    import ml_dtypes
    bf = ml_dtypes.bfloat16
    c = np.arange(256)[:, None]
    m = np.arange(256)[None, :]
    ang = 2 * np.pi * ((c * m) % 256) / 256.0
    G = np.concatenate([np.cos(ang), -np.sin(ang)], 1).astype(np.float32)
    l1 = np.arange(64)[:, None]
    k1 = np.arange(64)[None, :]
    a = 2 * np.pi * ((l1 * k1) % 64) / 64.0
    cc, ss = np.cos(a), np.sin(a)
    F64 = np.block([[cc, -ss], [ss, cc]]).astype(np.float32)
    l2 = np.arange(128)[:, None]
    k2 = np.arange(128)[None, :]
    a2 = 2 * np.pi * ((l2 * k2) % 128) / 128.0
    C128, S128 = np.cos(a2).astype(np.float32), np.sin(a2).astype(np.float32)
    tw = 2 * np.pi * (np.arange(128)[:, None] * np.arange(64)[None, :]) / 8192.0
    tc = np.repeat(np.cos(tw)[:, None, :], 4, 1).astype(np.float32)
    ts = np.repeat(np.sin(tw)[:, None, :], 4, 1).astype(np.float32)
    return {"G": G.astype(bf), "F64": F64.astype(bf), "C128": C128.astype(bf), "S128": S128.astype(bf), "tc": tc, "ts": ts}


def build_f():
    nc = new_nc()
    L = 8192
    xT = nc.dram_tensor("xT", [256, L], BF16, kind="ExternalInput").ap()
    Gd = nc.dram_tensor("G", [256, 512], BF16, kind="ExternalInput").ap()
    F64d = nc.dram_tensor("F64", [128, 128], BF16, kind="ExternalInput").ap()
    C128d = nc.dram_tensor("C128", [128, 128], BF16, kind="ExternalInput").ap()
    S128d = nc.dram_tensor("S128", [128, 128], BF16, kind="ExternalInput").ap()
    tcd = nc.dram_tensor("tc", [128, 4, 64], F32, kind="ExternalInput").ap()
    tsd = nc.dram_tensor("ts", [128, 4, 64], F32, kind="ExternalInput").ap()
    fo = nc.dram_tensor("f", [L, 256], BF16, kind="ExternalOutput").ap()
    scale = 1.0 / math.sqrt(L * 256.0)
    with ExitStack() as es:
        k = KB(nc, es)
        xs = k.sb([128, 2, L], BF16); d_x = Dep()
        G = k.sb([128, 2, 512], BF16); d_G = Dep()
        F64 = k.sb([128, 128], BF16)
        C128 = k.sb([128, 128], BF16)
        S128 = k.sb([128, 128], BF16)
        tc = k.sb([128, 4, 64], F32)
        ts = k.sb([128, 4, 64], F32)
        d_c = Dep()
        Wl = k.sb([128, 128, 256], BF16); d_W = Dep()
        Tp = k.sb([128, 2, 64, 256], BF16); d_T = Dep()
        k.dma("sp", xs[:], xT.rearrange("(c p) l -> p c l", p=128), writes=[d_x])
        k.dma("sp", G[:], Gd.rearrange("(c p) n -> p c n", p=128), writes=[d_G])
        for t_, s_ in ((F64, F64d), (C128, C128d), (S128, S128d), (tc, tcd), (ts, tsd)):
            k.dma("sp", t_[:], s_, writes=[d_c])
        p0 = [k.ps([128, 512]) for _ in range(2)]
        d_p0 = [Dep(), Dep()]
        pA = [k.ps([128, 512]) for _ in range(2)]
        d_pA = [Dep(), Dep()]
        pB = [k.ps([128, 512]) for _ in range(2)]
        d_pB = [Dep(), Dep()]
        xv = xs[:].rearrange("p c (a b) -> p c b a", b=128)
        for l2 in range(128):
            j = (l2 // 2) % 2
            col = (l2 % 2) * 256
            for ri in range(2):
                for cc in range(2):
                    k.op("pe", lambda e: e.matmul(p0[j][ri * 64:(ri + 1) * 64, col:col + 256], lhsT=xv[:, cc, l2, :], rhs=G[:, cc, ri * 256:(ri + 1) * 256],
                                                  start=(cc == 0), stop=(cc == 1)), reads=[d_x, d_G], writes=[d_p0[j]])
            if l2 % 2 == 1:
                eng = "act" if (l2 // 2) % 2 == 0 else "dve"
                if eng == "act":
                    k.op("act", lambda e: e.activation(out=Wl[:, l2 - 1:l2 + 1, :].rearrange("p a m -> p (a m)"), in_=p0[j][:], func=AF.Copy), reads=[d_p0[j]], writes=[d_W])
                else:
                    k.op("dve", lambda e: e.tensor_copy(out=Wl[:, l2 - 1:l2 + 1, :].rearrange("p a m -> p (a m)"), in_=p0[j][:]), reads=[d_p0[j]], writes=[d_W])
        ta = [k.sb([128, 4, 64], F32) for _ in range(2)]
        tb_ = [k.sb([128, 4, 64], F32) for _ in range(2)]
        d_ta = [Dep(), Dep()]
        d_tb = [Dep(), Dep()]
        for g in range(64):
            j = g % 2
            for q in range(4):
                m = 4 * g + q
                k.op("pe", lambda e: e.matmul(pA[j][:, q * 128:(q + 1) * 128], lhsT=Wl[:, :, m], rhs=F64[:], start=True, stop=True),
                     reads=[d_W, d_c], writes=[d_pA[j]])
            pv = pA[j][:].rearrange("p (q r k) -> p q r k", q=4, r=2)
            Tre, Tim = pv[:, :, 0, :], pv[:, :, 1, :]
            o_re = Tp[:, 0, :, 4 * g:4 * g + 4].rearrange("p k m -> p m k")
            o_im = Tp[:, 1, :, 4 * g:4 * g + 4].rearrange("p k m -> p m k")
            k.op("dve", lambda e: e.tensor_tensor(out=ta[j][:], in0=Tre, in1=tc[:], op=ALU.mult), reads=[d_pA[j], d_c], writes=[d_ta[j]])
            k.op("dve", lambda e: e.tensor_tensor(out=tb_[j][:], in0=Tim, in1=ts[:], op=ALU.mult), reads=[d_pA[j], d_c], writes=[d_tb[j]])
            k.op("pool", lambda e: e.tensor_tensor(out=o_re, in0=ta[j][:], in1=tb_[j][:], op=ALU.add), reads=[d_ta[j], d_tb[j]], writes=[d_T])
            k.op("dve", lambda e: e.tensor_tensor(out=ta[j][:], in0=Tim, in1=tc[:], op=ALU.mult), reads=[d_pA[j], d_c], writes=[d_ta[j]])
            k.op("dve", lambda e: e.tensor_tensor(out=tb_[j][:], in0=Tre, in1=ts[:], op=ALU.mult), reads=[d_pA[j], d_c], writes=[d_tb[j]])
            k.op("pool", lambda e: e.tensor_tensor(out=o_im, in0=ta[j][:], in1=tb_[j][:], op=ALU.subtract), reads=[d_ta[j], d_tb[j]], writes=[d_T])
        ob = [k.sb([128, 512], BF16) for _ in range(2)]
        d_ob = [Dep(), Dep()]
        fv = fo.rearrange("(k2 k1) m -> k2 k1 m", k1=64)
        Tf = Tp[:].rearrange("p r k m -> p r (k m)")
        for blk_i in range(32):
            j = blk_i % 2
            cs = slice(blk_i * 512, (blk_i + 1) * 512)
            k.op("pe", lambda e: e.matmul(pB[j][:], lhsT=C128[:], rhs=Tf[:, 0, cs], start=True, stop=False), reads=[d_T, d_c], writes=[d_pB[j]])
            k.op("pe", lambda e: e.matmul(pB[j][:], lhsT=S128[:], rhs=Tf[:, 1, cs], start=False, stop=True), reads=[d_T, d_c], writes=[d_pB[j]])
            k.op("act", lambda e: e.activation(out=ob[j][:], in_=pB[j][:], func=AF.Copy, scale=scale), reads=[d_pB[j]], writes=[d_ob[j]])
            k.dma("sp", fv[:, 2 * blk_i:2 * blk_i + 2, :], ob[j][:].rearrange("p (k m) -> p k m", k=2), reads=[d_ob[j]], writes=[Dep()])
        k.finish([])
    return nc


D_MODEL = 1024
COL_Q, COL_Z, COL_C, COL_K, COL_V, COL_XB, COL_DT = 0, 1024, 2048, 2304, 3328, 4352, 5632


def _b_inputs(inp, j):
    g = j // 4
    w = inp["w_in"][0]
    dtc = [COL_DT + d * 16 + 2 * j + hd for d in range(2) for hd in range(2)]
    cols = [w[:, COL_Q + j * 128: COL_Q + (j + 1) * 128], w[:, COL_K + j * 128: COL_K + (j + 1) * 128],
            w[:, COL_V + j * 128:COL_V + (j + 1) * 128], w[:, dtc],
            w[:, COL_Z + j * 128:COL_Z + (j + 1) * 128], w[:, COL_XB + j * 128:COL_XB + (j + 1) * 128],
            w[:, COL_XB + 1024 + g * 128:COL_XB + 1024 + (g + 1) * 128], w[:, COL_C + g * 128:COL_C + (g + 1) * 128]]
    W = np.ascontiguousarray(np.concatenate(cols, 1))
    pp = np.zeros((128, 32), np.float32)
    p = np.arange(128)
    pp[:, 0] = inp["q_norm_g"][0][p % 64]
    pp[:, 1] = inp["k_norm_g"][0][p % 64]
    pp[:, 2] = inp["da_subln_g"][0]
    pp[:, 3:8] = inp["conv_xb_w"][0][:, j * 128:(j + 1) * 128].T
    pp[:, 8] = inp["conv_xb_b"][0][j * 128:(j + 1) * 128]
    pp[:, 9:14] = inp["conv_xb_w"][0][:, 1024 + g * 128:1024 + (g + 1) * 128].T
    pp[:, 14] = inp["conv_xb_b"][0][1024 + g * 128:1024 + (g + 1) * 128]
    pp[:, 15:20] = inp["conv_c_w"][0][:, g * 128:(g + 1) * 128].T
    pp[:, 20] = inp["conv_c_b"][0][g * 128:(g + 1) * 128]
    pp[:, 21] = inp["d_skip"][0][2 * j + p // 64]
    pp[:, 22] = inp["ssm_norm_g"][0][j * 128:(j + 1) * 128]
    p4 = np.zeros((2, 4), np.float32)
    for d in range(2):
        for hd in range(2):
            p4[0, d * 2 + hd] = inp["dt_bias"][0][d, 2 * j + hd]
            p4[1, d * 2 + hd] = inp["a_log"][0][d, 2 * j + hd]
    return {"W": W, "pp": pp, "p4": p4, "lam4": np.ascontiguousarray(inp["da_lambda"][0])}


def _moe_stage(inp, layer, h2T, gatesT):
    ntok = h2T.shape[1]
    masks = [(gatesT[4 * j:4 * j + 4] > 0).any(0) for j in range(NCORES)]
    idxs = [np.nonzero(m_)[0] for m_ in masks]
    cap = max(512, max((len(i_) + 511) // 512 * 512 for i_ in idxs))
    nc = build_d(cap)
    in_maps = []
    for j in range(NCORES):
        d = d_inputs(inp, layer, j)
        n_j = len(idxs[j])
        hj = np.zeros((1024, cap), h2T.dtype)
        hj[:, :n_j] = h2T[:, idxs[j]]
        gj = np.zeros((4, cap), np.float32)
        gj[:, :n_j] = gatesT[4 * j:4 * j + 4][:, idxs[j]]
        d["h2T"] = _to_blocks(hj, 512)
        d["gT"] = gj
        in_maps.append(d)
    res = run_spmd(nc, in_maps)
    mm = np.stack(masks, 0).astype(np.int64)
    slot = np.cumsum(mm, 0) - mm
    nslot = int(mm.sum(0).max())
    p0 = res[0]["part"]
    parts = np.zeros((max(nslot, 1), ntok, 1024), p0.dtype)
    for j in range(NCORES):
        n_j = len(idxs[j])
        if n_j:
            parts[slot[j, idxs[j]], idxs[j]] = res[j]["part"][:n_j]
    return parts


def _hT_from(res, n):
    blks = np.concatenate([res[r]["hT"] for r in range(n)], 0)
    nt = blks.shape[0]
    return np.ascontiguousarray(blks.reshape(nt, 128, 8, 128).transpose(2, 1, 0, 3).reshape(1024, nt * 128))


def _to_blocks(hT, bs):
    return np.ascontiguousarray(hT.reshape(8, 128, -1, bs).transpose(2, 1, 0, 3))


def kernel(**inp):
    inp = {k_: np.asarray(v) for k_, v in inp.items()}
    ident = np.eye(128, dtype=np.float32)
    TPC = 2048
    x0 = np.ascontiguousarray(inp["x"].reshape(16384, 1024))
    mod = run_l0(inp)
    modr = mod.reshape(2, 3, 6, 1024)
    nc = build_t1(front=None, norm=(0, 1), h_layout="T", router=False, out_x=False)
    res = run_spmd(nc, [{"x": x0[r * TPC:(r + 1) * TPC], "mod": np.ascontiguousarray(modr[0, r // 4]), "ident": ident,
                         "ng": inp["norm_g"][0, 0]} for r in range(NCORES)])
    hT0 = _hT_from(res, NCORES)
    ctxf = np.ascontiguousarray(inp["ctx"].reshape(512, 1024))
    nc = build_t1(front=None, norm=(0, 1), h_layout="T", router=False, out_x=False, T=128)
    res = run_spmd(nc, [{"x": ctxf[(r % 4) * 128:(r % 4 + 1) * 128], "mod": np.ascontiguousarray(modr[0, 2]), "ident": ident,
                         "ng": inp["norm_g"][0, 0]} for r in range(NCORES)])
    hcT = _hT_from(res, 4)
    lam_init = 0.8 - 0.6 * math.exp(-0.3 * 0)
    cst = b_consts()
    cst["tri"] = np.triu(np.ones((128, 128), np.float32))
    nc = build_b(lam_init)
    hT0b = _to_blocks(hT0, 512)
    hcTb = _to_blocks(hcT, CTX)
    in_maps = []
    for j in range(NCORES):
        d = {"hT": hT0b, "hcT": hcTb}
        d.update(_b_inputs(inp, j))
        d.update(cst)
        in_maps.append(d)
    res = run_spmd(nc, in_maps)
    oT = np.concatenate([res[j]["oT"] for j in range(NCORES)], 0)
    ygT = np.concatenate([res[j]["ygT"] for j in range(NCORES)], 0)
    ss = np.ascontiguousarray(np.concatenate([res[j]["ss"] for j in range(NCORES)], 0).T)
    del res
    nc = build_t1(front="mix", nA=8, nB=8, gate_row=2, norm=(3, 4), h_layout="T", router=True, out_x=True)
    in_maps = []
    for r in range(NCORES):
        ts_ = slice(r * TPC, (r + 1) * TPC)
        in_maps.append({"x": x0[ts_], "mod": np.ascontiguousarray(modr[0, r // 4]), "ident": ident,
                        "mixA": np.ascontiguousarray(oT[:, ts_]), "wA": np.ascontiguousarray(inp["w_out"][0][0:1024]),
                        "mixB": np.ascontiguousarray(ygT[:, ts_]), "wB": np.ascontiguousarray(inp["w_out"][0][1024:2048]),
                        "ss": np.ascontiguousarray(ss[ts_]), "ng": inp["norm_g"][0, 1],
                        "wr": inp["w_router"][0], "br": inp["b_router"][0]})
    res = run_spmd(nc, in_maps)
    x1 = [res[r]["xo"] for r in range(NCORES)]
    h2T = _hT_from(res, NCORES)
    gatesT = np.concatenate([res[r]["gatesT"] for r in range(NCORES)], 1)
    parts = _moe_stage(inp, 0, h2T, gatesT)
    nc = build_t1(front="parts", gate_row=5, norm=(0, 1), h_layout="T", router=False, out_x=True, nparts=parts.shape[0])
    in_maps = []
    for r in range(NCORES):
        ts_ = slice(r * TPC, (r + 1) * TPC)
        in_maps.append({"x": x1[r], "mod": np.ascontiguousarray(np.concatenate([modr[1, r // 4][0:5], modr[0, r // 4][5:6]], 0)),
                        "ident": ident, "parts": np.ascontiguousarray(parts[:, ts_]),
                        "ng": inp["norm_g"][1, 0]})
    res = run_spmd(nc, in_maps)
    del parts
    x2 = [res[r]["xo"] for r in range(NCORES)]
    hT1 = _hT_from(res, NCORES)
    fc = f_consts()
    nc = build_f()
    in_maps = []
    for r in range(NCORES):
        b, g = r // 4, r % 4
        d = dict(fc)
        d["xT"] = np.ascontiguousarray(hT1[g * 256:(g + 1) * 256, b * 8192:(b + 1) * 8192])
        in_maps.append(d)
    res = run_spmd(nc, in_maps)
    fT = np.concatenate([np.concatenate([res[b * 4 + g]["f"].T for g in range(4)], 0) for b in range(2)], 1)
    nc = build_t1(front="mix", nA=8, nB=0, gate_row=2, norm=(3, 4), h_layout="T", router=True, out_x=True)
    in_maps = []
    for r in range(NCORES):
        ts_ = slice(r * TPC, (r + 1) * TPC)
        in_maps.append({"x": x2[r], "mod": np.ascontiguousarray(modr[1, r // 4]), "ident": ident,
                        "mixA": np.ascontiguousarray(fT[:, ts_]), "wA": inp["w_fourier"][0], "ng": inp["norm_g"][1, 1],
                        "wr": inp["w_router"][1], "br": inp["b_router"][1]})
    res = run_spmd(nc, in_maps)
    x3 = [res[r]["xo"] for r in range(NCORES)]
    h2T = _hT_from(res, NCORES)
    gatesT = np.concatenate([res[r]["gatesT"] for r in range(NCORES)], 1)
    parts = _moe_stage(inp, 1, h2T, gatesT)
    nc = build_t1(front="parts", gate_row=5, norm=None, router=False, out_x=True, nparts=parts.shape[0])
    in_maps = []
    for r in range(NCORES):
        ts_ = slice(r * TPC, (r + 1) * TPC)
        in_maps.append({"x": x3[r], "mod": np.ascontiguousarray(modr[1, r // 4]), "ident": ident,
                        "parts": np.ascontiguousarray(parts[:, ts_])})
    res = run_spmd(nc, in_maps)
    out = np.concatenate([res[r]["xo"] for r in range(NCORES)], 0).reshape(2, 8192, 1024)
    return out.astype(np.float32)
```

```python
import math
from contextlib import ExitStack

import numpy as np
import concourse.bass as bass
import concourse.mybir as mybir
from concourse.bass_utils import run_bass_kernel_spmd

F32 = mybir.dt.float32
BF16 = mybir.dt.bfloat16
AF = mybir.ActivationFunctionType
ALU = mybir.AluOpType
AX = mybir.AxisListType
NCORES = 8


class Dep:
    __slots__ = ("w", "r")

    def __init__(self):
        self.w = None
        self.r = {}


class KB:
    DMA_RING = 8

    def __init__(self, nc, es):
        self.nc = nc
        self.es = es
        self.root_es = es
        self.E = {"pe": nc.tensor, "act": nc.scalar, "dve": nc.vector, "pool": nc.gpsimd, "sp": nc.sync}
        self.sem = {}
        for e in ("pe", "act", "dve", "pool"):
            self.sem[e] = es.enter_context(nc.semaphore(e))
        self.cnt = {e: 0 for e in ("pe", "act", "dve", "pool")}
        self.seen = {e: {} for e in self.E}
        self.dmaq = {}
        self.ntile = 0

    def sb(self, shape, dt, name=None):
        self.ntile += 1
        return self.es.enter_context(self.nc.sbuf_tensor(name or f"t{self.ntile}", list(shape), dt))

    def ps(self, shape, dt=F32, name=None):
        self.ntile += 1
        return self.es.enter_context(self.nc.psum_tensor(name or f"p{self.ntile}", list(shape), dt))

    def _wait(self, e, key, val):
        if self.seen[e].get(key, 0) >= val:
            return
        self.E[e].wait_ge(self.sem[key], val)
        self.seen[e][key] = val

    def _deps(self, e, reads, writes):
        need = {}
        for d in reads:
            if d.w:
                k, v = d.w
                need[k] = max(need.get(k, 0), v)
        for d in writes:
            if d.w:
                k, v = d.w
                need[k] = max(need.get(k, 0), v)
            for k, v in d.r.items():
                need[k] = max(need.get(k, 0), v)
        for k, v in need.items():
            if k == "pe" and e == "pe":
                continue
            self._wait(e, k, v)

    def op(self, e, fn, reads=(), writes=()):
        self._deps(e, reads, writes)
        inst = fn(self.E[e])
        self.cnt[e] += 1
        v = self.cnt[e]
        inst.then_inc(self.sem[e], 1)
        for d in reads:
            d.r[e] = v
        for d in writes:
            d.w = (e, v)
            d.r = {}
        return inst

    def dma(self, q, out, in_, reads=(), writes=(), **kw):
        ring = self.dmaq.setdefault(q, {"n": 0, "keys": []})
        n = ring["n"]
        slot = n % self.DMA_RING
        if len(ring["keys"]) <= slot:
            key = f"dma_{q}_{slot}"
            self.sem[key] = self.root_es.enter_context(self.nc.semaphore(key))
            ring["keys"].append(key)
        key = ring["keys"][slot]
        val = 16 * (n // self.DMA_RING + 1)
        if n >= self.DMA_RING:
            self._wait(q, key, val - 16)
        self._deps(q, reads, writes)
        inst = self.E[q].dma_start(out=out, in_=in_, **kw)
        inst.then_inc(self.sem[key], 16)
        ring["n"] += 1
        for d in reads:
            d.r[key] = val
        for d in writes:
            d.w = (key, val)
            d.r = {}
        return inst

    def coll(self, kind, op, ins, outs, reads=(), writes=()):
        q = "pool"
        ring = self.dmaq.setdefault("coll", {"n": 0, "keys": []})
        if not ring["keys"]:
            self.sem["coll"] = self.root_es.enter_context(self.nc.semaphore("coll"))
            ring["keys"].append("coll")
        n = ring["n"]
        val = 16 * (n + 1)
        if n >= 1:
            self._wait(q, "coll", val - 16)
        self._deps(q, reads, writes)
        inst = self.nc.gpsimd.collective_compute(kind, op, replica_groups=[list(range(NCORES))], ins=ins, outs=outs)
        inst.then_inc(self.sem["coll"], 16)
        ring["n"] += 1
        for d in reads:
            d.r["coll"] = val
        for d in writes:
            d.w = ("coll", val)
            d.r = {}
        return inst

    def finish(self, deps):
        for d in deps:
            if d.w:
                self._wait("sp", d.w[0], d.w[1])
        for q, ring in self.dmaq.items():
            n = ring["n"]
            for slot, key in enumerate(ring["keys"]):
                uses = n if q == "coll" else (n - slot + self.DMA_RING - 1) // self.DMA_RING
                if uses > 0:
                    self._wait("sp", key, 16 * uses)


def new_nc():
    return bass.Bass("TRN2", target_bir_lowering=False)


def run_spmd(nc, in_maps):
    import time, sys
    t0 = time.time()
    res = run_bass_kernel_spmd(nc, in_maps, core_ids=list(range(NCORES)))
    print(f'[launch] {time.time() - t0:.1f}s', file=sys.stderr, flush=True)
    return res.results


def build_l0():
    nc = new_nc()
    adaw = nc.dram_tensor("adaw", [2, 1024, 768], F32, kind="ExternalInput").ap()
    adab = nc.dram_tensor("adab", [128, 12], F32, kind="ExternalInput").ap()
    cT = nc.dram_tensor("cT", [1024, 3], F32, kind="ExternalInput").ap()
    out = nc.dram_tensor("modp", [128, 36], F32, kind="ExternalOutput").ap()
    with ExitStack() as es:
        k = KB(nc, es)
        w_sb = k.sb([128, 2, 8, 768], F32)
        b_sb = k.sb([128, 12], F32)
        c_sb = k.sb([128, 8, 3], F32)
        sc_sb = k.sb([128, 8, 3], F32)
        o_sb = k.sb([128, 12, 3], F32)
        pp = k.ps([128, 512], F32)
        dw = [Dep(), Dep()]
        db, dc, dsc, dps, do = Dep(), Dep(), Dep(), Dep(), Dep()
        for l in range(2):
            k.dma("sp", w_sb[:, l], adaw[l].rearrange("(k p) m -> p k m", p=128), writes=[dw[l]])
        k.dma("sp", b_sb[:], adab, writes=[db])
        k.dma("sp", c_sb[:], cT.rearrange("(k p) v -> p k v", p=128), writes=[dc])
        k.op("act", lambda e: e.activation(out=sc_sb[:], in_=c_sb[:], func=AF.Silu), reads=[dc], writes=[dsc])
        for l in range(2):
            for mc in range(6):
                for kc in range(8):
                    k.op("pe", lambda e: e.matmul(pp[:, (l * 6 + mc) * 3:(l * 6 + mc) * 3 + 3],
                                                  lhsT=w_sb[:, l, kc, mc * 128:(mc + 1) * 128],
                                                  rhs=sc_sb[:, kc, :], start=(kc == 0), stop=(kc == 7)),
                         reads=[dw[l], dsc], writes=[dps])
        for v in range(3):
            k.op("dve", lambda e: e.tensor_tensor(out=o_sb[:, :, v], in0=pp[:, 0:36].rearrange("p (a v) -> p a v", v=3)[:, :, v],
                                                  in1=b_sb[:], op=ALU.add), reads=[dps, db], writes=[do])
        k.dma("sp", out, o_sb[:].rearrange("p a v -> p (a v)"), reads=[do], writes=[Dep()])
        k.finish([])
    return nc


def run_l0(inp):
    ada_w, ada_b = inp["ada_w"], inp["ada_b"]
    cT = np.ascontiguousarray(np.concatenate([inp["c"], inp["c_ctx"][None]], 0).T)
    in_maps = []
    for j in range(NCORES):
        sl = slice(768 * j, 768 * (j + 1))
        adab = ada_b[:, sl].reshape(2, 6, 128).transpose(2, 0, 1).reshape(128, 12)
        in_maps.append({"adaw": np.ascontiguousarray(ada_w[:, :, sl]), "adab": np.ascontiguousarray(adab), "cT": cT})
    res = run_spmd(build_l0(), in_maps)
    mod = np.zeros((2, 3, 6144), np.float32)
    for j in range(NCORES):
        o = res[j]["modp"].reshape(128, 2, 6, 3)
        mod[:, :, 768 * j:768 * (j + 1)] = o.transpose(1, 3, 2, 0).reshape(2, 3, 768)
    return mod


RMS_EPS = 1e-6


def build_t1(front, nA=0, nB=0, gate_row=2, norm=None, h_layout="T", router=False, out_x=True, T=2048, nparts=8):
    nc = new_nc()
    NT = T // 128
    x = nc.dram_tensor("x", [T, 1024], F32, kind="ExternalInput").ap()
    mod = nc.dram_tensor("mod", [6, 1024], F32, kind="ExternalInput").ap()
    ident_d = nc.dram_tensor("ident", [128, 128], F32, kind="ExternalInput").ap()
    if front == "mix":
        mixA = nc.dram_tensor("mixA", [nA * 128, T], BF16, kind="ExternalInput").ap()
        wA = nc.dram_tensor("wA", [nA * 128, 1024], F32, kind="ExternalInput").ap()
        if nB:
            mixB = nc.dram_tensor("mixB", [nB * 128, T], BF16, kind="ExternalInput").ap()
            wB = nc.dram_tensor("wB", [nB * 128, 1024], F32, kind="ExternalInput").ap()
            ssd = nc.dram_tensor("ss", [T, 8], F32, kind="ExternalInput").ap()
    elif front == "parts":
        parts = nc.dram_tensor("parts", [nparts, T, 1024], BF16, kind="ExternalInput").ap()
    if norm is not None:
        ng = nc.dram_tensor("ng", [1024], F32, kind="ExternalInput").ap()
        if h_layout == "T":
            hout = nc.dram_tensor("hT", [T // 128, 128, 1024], BF16, kind="ExternalOutput").ap()
        else:
            hout = nc.dram_tensor("hN", [T, 1024], BF16, kind="ExternalOutput").ap()
    if router:
        wr = nc.dram_tensor("wr", [1024, 32], F32, kind="ExternalInput").ap()
        br = nc.dram_tensor("br", [32], F32, kind="ExternalInput").ap()
        gout = nc.dram_tensor("gatesT", [32, T], F32, kind="ExternalOutput").ap()
    if out_x:
        xout = nc.dram_tensor("xo", [T, 1024], F32, kind="ExternalOutput").ap()

    with ExitStack() as es:
        k = KB(nc, es)
        outd = []
        ident = k.sb([128, 128], F32)
        d_id = Dep()
        k.dma("sp", ident[:], ident_d, writes=[d_id])
        g_bc = k.sb([128, 1024], F32)
        d_g = Dep()
        if front is not None:
            k.dma("sp", g_bc[:], mod[gate_row].partition_broadcast(128), writes=[d_g])
        if norm is not None:
            A_bc = k.sb([128, 1024], F32)
            B_bc = k.sb([128, 1024], F32)
            ng_bc = k.sb([128, 1024], F32)
            d_A, d_B, d_ng = Dep(), Dep(), Dep()
            k.dma("sp", A_bc[:], mod[norm[1]].partition_broadcast(128), writes=[d_A])
            k.dma("sp", B_bc[:], mod[norm[0]].partition_broadcast(128), writes=[d_B])
            k.dma("sp", ng_bc[:], ng.partition_broadcast(128), writes=[d_ng])
            k.op("dve", lambda e: e.scalar_tensor_tensor(out=A_bc[:], in0=A_bc[:], scalar=1.0, in1=ng_bc[:],
                                                         op0=ALU.add, op1=ALU.mult), reads=[d_ng], writes=[d_A])
        if router:
            wr_sb = k.sb([128, 8, 32], F32)
            br_bc = k.sb([128, 32], F32)
            d_wr, d_br = Dep(), Dep()
            k.dma("sp", wr_sb[:], wr.rearrange("(k p) e -> p k e", p=128), writes=[d_wr])
            k.dma("sp", br_bc[:], br.partition_broadcast(128), writes=[d_br])
            wr_hi = k.sb([128, 8, 32], BF16)
            wr_lo = k.sb([128, 8, 32], BF16)
            k.op("dve", lambda e: e.tensor_copy(out=wr_hi[:], in_=wr_sb[:]), reads=[d_wr], writes=[d_wr])
            k.op("dve", lambda e: e.tensor_tensor(out=wr_lo[:], in0=wr_sb[:], in1=wr_hi[:], op=ALU.subtract), reads=[d_wr], writes=[d_wr])
        if front == "mix":
            stg = [k.sb([128, 1024], F32) for _ in range(2)]
            d_stg = [Dep(), Dep()]
            wA_bf = k.sb([128, nA, 1024], BF16)
            d_wA = Dep()
            n = 0
            for c in range(nA):
                k.dma("sp", stg[n % 2][:], wA[c * 128:(c + 1) * 128, :], writes=[d_stg[n % 2]])
                k.op("act", lambda e: e.activation(out=wA_bf[:, c, :], in_=stg[n % 2][:], func=AF.Copy),
                     reads=[d_stg[n % 2]], writes=[d_wA])
                n += 1
            if nB:
                wB_bf = k.sb([128, nB, 1024], BF16)
                d_wB = Dep()
                for c in range(nB):
                    k.dma("sp", stg[n % 2][:], wB[c * 128:(c + 1) * 128, :], writes=[d_stg[n % 2]])
                    k.op("act", lambda e: e.activation(out=wB_bf[:, c, :], in_=stg[n % 2][:], func=AF.Copy),
                         reads=[d_stg[n % 2]], writes=[d_wB])
                    n += 1
        NB_ = 2
        xt = [k.sb([128, 1024], F32) for _ in range(NB_)]
        d_x = [Dep() for _ in range(NB_)]
        tmp = [k.sb([128, 1024], F32) for _ in range(NB_)]
        d_tmp = [Dep() for _ in range(NB_)]
        if front == "mix":
            mA = [k.sb([128, nA, 128], BF16) for _ in range(NB_)]
            d_mA = [Dep() for _ in range(NB_)]
            if nB:
                mB = [k.sb([128, nB, 128], BF16) for _ in range(NB_)]
                d_mB = [Dep() for _ in range(NB_)]
                sst = [k.sb([128, 8], F32) for _ in range(NB_)]
                d_ss = [Dep() for _ in range(NB_)]
                rs = [k.sb([128, 4], F32) for _ in range(NB_)]
                d_rs = [Dep() for _ in range(NB_)]
                asb = [k.sb([128, 512], F32) for _ in range(NB_)]
                d_asb = [Dep() for _ in range(NB_)]
        if front == "parts":
            pt = [k.sb([128, nparts, 1024], BF16) for _ in range(NB_)]
            d_pt = [Dep() for _ in range(NB_)]
        if norm is not None:
            junk = [k.sb([128, 1024], BF16) for _ in range(NB_)]
            d_junk = [Dep() for _ in range(NB_)]
            st = [k.sb([128, 4], F32) for _ in range(NB_)]
            d_st = [Dep() for _ in range(NB_)]
            hf = [k.sb([128, 1024], F32) for _ in range(NB_)]
            d_hf = [Dep() for _ in range(NB_)]
            hb = [k.sb([128, 1024], BF16) for _ in range(NB_)]
            d_hb = [Dep() for _ in range(NB_)]
        if router:
            hTf = [k.sb([128, 8, 128], BF16) for _ in range(NB_)]
            d_hTf = [Dep() for _ in range(NB_)]
            lg = [k.sb([128, 32], F32) for _ in range(NB_)]
            d_lg = [Dep() for _ in range(NB_)]
            t8 = [k.sb([128, 8], F32) for _ in range(NB_)]
            d_t8 = [Dep() for _ in range(NB_)]
            ex = [k.sb([128, 32], F32) for _ in range(NB_)]
            d_ex = [Dep() for _ in range(NB_)]
            gt = [k.sb([128, 32], F32) for _ in range(NB_)]
            d_gt = [Dep() for _ in range(NB_)]
            gT = [k.sb([32, 128], F32) for _ in range(NB_)]
            d_gT = [Dep() for _ in range(NB_)]
        pa = [k.ps([128, 512]) for _ in range(2)]
        d_pa = [Dep(), Dep()]
        pb = [k.ps([128, 512]) for _ in range(2)]
        d_pb = [Dep(), Dep()]
        ptr = k.ps([128, 1024])
        d_ptr = Dep()
        plg = k.ps([128, 512])
        d_plg = Dep()

        for t in range(NT):
            b = t % NB_
            tok = slice(t * 128, (t + 1) * 128)
            k.dma("sp", xt[b][:], x[tok, :], writes=[d_x[b]])
            xn, d_xn = xt[b], d_x[b]
            if front == "mix":
                k.dma("sp", mA[b][:], mixA[:, tok].rearrange("(c p) t -> p c t", p=128), writes=[d_mA[b]])
                if nB:
                    k.dma("sp", mB[b][:], mixB[:, tok].rearrange("(c p) t -> p c t", p=128), writes=[d_mB[b]])
                    k.dma("sp", sst[b][:], ssd[tok, :], writes=[d_ss[b]])
                    k.op("dve", lambda e: e.tensor_reduce(out=rs[b][:, 0:1], in_=sst[b][:], axis=AX.X, op=ALU.add),
                         reads=[d_ss[b]], writes=[d_rs[b]])
                    k.op("act", lambda e: e.activation(out=rs[b][:, 1:2], in_=rs[b][:, 0:1], func=AF.Sqrt,
                                                       scale=1.0 / 1024, bias=eps_ap(k)), reads=[d_rs[b]], writes=[d_rs[b]])
                    k.op("dve", lambda e: e.reciprocal(out=rs[b][:, 2:3], in_=rs[b][:, 1:2]), reads=[d_rs[b]], writes=[d_rs[b]])
                for h in range(2):
                    cs = slice(h * 512, (h + 1) * 512)
                    for c in range(nA):
                        k.op("pe", lambda e: e.matmul(pa[h][:], lhsT=mA[b][:, c, :], rhs=wA_bf[:, c, cs],
                                                      start=(c == 0), stop=(c == nA - 1)),
                             reads=[d_mA[b], d_wA], writes=[d_pa[h]])
                    if nB:
                        for c in range(nB):
                            k.op("pe", lambda e: e.matmul(pb[h][:], lhsT=mB[b][:, c, :], rhs=wB_bf[:, c, cs],
                                                          start=(c == 0), stop=(c == nB - 1)),
                                 reads=[d_mB[b], d_wB], writes=[d_pb[h]])
                        k.op("act", lambda e: e.activation(out=asb[b][:], in_=pa[h][:], func=AF.Copy),
                             reads=[d_pa[h]], writes=[d_asb[b]])
                        k.op("dve", lambda e: e.scalar_tensor_tensor(out=tmp[b][:, cs], in0=pb[h][:], scalar=rs[b][:, 2:3],
                                                                     in1=asb[b][:], op0=ALU.mult, op1=ALU.add),
                             reads=[d_pb[h], d_rs[b], d_asb[b]], writes=[d_tmp[b]])
                        k.op("dve", lambda e: e.tensor_tensor(out=tmp[b][:, cs], in0=tmp[b][:, cs], in1=g_bc[:, cs], op=ALU.mult),
                             reads=[d_g], writes=[d_tmp[b]])
                    else:
                        k.op("dve", lambda e: e.tensor_tensor(out=tmp[b][:, cs], in0=pa[h][:], in1=g_bc[:, cs], op=ALU.mult),
                             reads=[d_pa[h], d_g], writes=[d_tmp[b]])
                k.op("dve", lambda e: e.tensor_tensor(out=xt[b][:], in0=xt[b][:], in1=tmp[b][:], op=ALU.add),
                     reads=[d_tmp[b]], writes=[d_x[b]])
            elif front == "parts":
                k.dma("sp", pt[b][:], parts[:, tok, :].rearrange("j t d -> t j d"), writes=[d_pt[b]])
                if nparts == 1:
                    k.op("dve", lambda e: e.tensor_copy(out=tmp[b][:], in_=pt[b][:, 0, :]), reads=[d_pt[b]], writes=[d_tmp[b]])
                else:
                    k.op("dve", lambda e: e.tensor_tensor(out=tmp[b][:], in0=pt[b][:, 0, :], in1=pt[b][:, 1, :], op=ALU.add),
                         reads=[d_pt[b]], writes=[d_tmp[b]])
                for j in range(2, nparts):
                    k.op("dve", lambda e: e.tensor_tensor(out=tmp[b][:], in0=tmp[b][:], in1=pt[b][:, j, :], op=ALU.add),
                         reads=[d_pt[b]], writes=[d_tmp[b]])
                k.op("dve", lambda e: e.tensor_tensor(out=tmp[b][:], in0=tmp[b][:], in1=g_bc[:], op=ALU.mult),
                     reads=[d_g], writes=[d_tmp[b]])
                k.op("dve", lambda e: e.tensor_tensor(out=xt[b][:], in0=xt[b][:], in1=tmp[b][:], op=ALU.add),
                     reads=[d_tmp[b]], writes=[d_x[b]])
            if out_x:
                do = Dep()
                k.dma("sp", xout[tok, :], xt[b][:], reads=[d_x[b]], writes=[do])
            if norm is not None:
                k.op("act", lambda e: e.activation(out=junk[b][:], in_=xt[b][:], func=AF.Square, accum_out=st[b][:, 0:1]),
                     reads=[d_x[b]], writes=[d_junk[b], d_st[b]])
                k.op("act", lambda e: e.activation(out=st[b][:, 1:2], in_=st[b][:, 0:1], func=AF.Sqrt,
                                                   scale=1.0 / 1024, bias=eps_ap(k)), reads=[d_st[b]], writes=[d_st[b]])
                k.op("dve", lambda e: e.reciprocal(out=st[b][:, 2:3], in_=st[b][:, 1:2]), reads=[d_st[b]], writes=[d_st[b]])
                k.op("dve", lambda e: e.scalar_tensor_tensor(out=hf[b][:], in0=xt[b][:], scalar=st[b][:, 2:3], in1=A_bc[:],
                                                             op0=ALU.mult, op1=ALU.mult),
                     reads=[d_x[b], d_st[b], d_A], writes=[d_hf[b]])
                k.op("dve", lambda e: e.tensor_tensor(out=hf[b][:], in0=hf[b][:], in1=B_bc[:], op=ALU.add),
                     reads=[d_B], writes=[d_hf[b]])
                if h_layout == "N":
                    k.op("act", lambda e: e.activation(out=hb[b][:], in_=hf[b][:], func=AF.Copy), reads=[d_hf[b]], writes=[d_hb[b]])
                    k.dma("sp", hout[tok, :], hb[b][:], reads=[d_hb[b]], writes=[Dep()])
                else:
                    for c in range(8):
                        k.op("pe", lambda e: e.transpose(out=ptr[:, c * 128:(c + 1) * 128], in_=hf[b][:, c * 128:(c + 1) * 128],
                                                         identity=ident[:]), reads=[d_hf[b], d_id], writes=[d_ptr])
                    k.op("dve", lambda e: e.tensor_copy(out=hb[b][:], in_=ptr[:]), reads=[d_ptr], writes=[d_hb[b]])
                    if router:
                        k.op("dve", lambda e: e.tensor_tensor(out=hTf[b][:].rearrange("p c t -> p (c t)"), in0=ptr[:], in1=hb[b][:], op=ALU.subtract),
                             reads=[d_ptr, d_hb[b]], writes=[d_hTf[b]])
                    k.dma("sp", hout[t], hb[b][:], reads=[d_hb[b]], writes=[Dep()])
                import os
                RL = int(os.environ.get("RL", "9"))
                if router and RL >= 1:
                    hbv = hb[b][:].rearrange("p (c t) -> p c t", c=8)
                    for c in range(8):
                        k.op("pe", lambda e: e.matmul(plg[:, 0:32], lhsT=hbv[:, c, :], rhs=wr_hi[:, c, :],
                                                      start=(c == 0), stop=False), reads=[d_hb[b], d_wr], writes=[d_plg])
                        k.op("pe", lambda e: e.matmul(plg[:, 0:32], lhsT=hbv[:, c, :], rhs=wr_lo[:, c, :],
                                                      start=False, stop=False), reads=[d_hb[b], d_wr], writes=[d_plg])
                        k.op("pe", lambda e: e.matmul(plg[:, 0:32], lhsT=hTf[b][:, c, :], rhs=wr_hi[:, c, :],
                                                      start=False, stop=(c == 7)), reads=[d_hTf[b], d_wr], writes=[d_plg])
                    k.op("dve", lambda e: e.tensor_tensor(out=lg[b][:], in0=plg[:, 0:32], in1=br_bc[:], op=ALU.add),
                         reads=[d_plg, d_br], writes=[d_lg[b]])
                if router and RL >= 2:
                    k.op("dve", lambda e: e.max(out=t8[b][:], in_=lg[b][:]), reads=[d_lg[b]], writes=[d_t8[b]])
                    k.op("dve", lambda e: e.tensor_scalar(out=ex[b][:], in0=lg[b][:], scalar1=t8[b][:, 0:1], scalar2=None,
                                                          op0=ALU.subtract), reads=[d_lg[b], d_t8[b]], writes=[d_ex[b]])
                    k.op("act", lambda e: e.activation(out=ex[b][:], in_=ex[b][:], func=AF.Exp), reads=[], writes=[d_ex[b]])
                    k.op("dve", lambda e: e.scalar_tensor_tensor(out=gt[b][:], in0=lg[b][:], scalar=t8[b][:, 3:4], in1=ex[b][:],
                                                                 op0=ALU.is_ge, op1=ALU.mult),
                         reads=[d_lg[b], d_t8[b], d_ex[b]], writes=[d_gt[b]])
                    k.op("dve", lambda e: e.tensor_reduce(out=t8[b][:, 4:5], in_=gt[b][:], axis=AX.X, op=ALU.add),
                         reads=[d_gt[b]], writes=[d_t8[b]])
                    k.op("dve", lambda e: e.reciprocal(out=t8[b][:, 5:6], in_=t8[b][:, 4:5]), reads=[], writes=[d_t8[b]])
                    k.op("dve", lambda e: e.tensor_scalar(out=gt[b][:], in0=gt[b][:], scalar1=t8[b][:, 5:6], scalar2=None,
                                                          op0=ALU.mult), reads=[d_t8[b]], writes=[d_gt[b]])
                if router and RL >= 3:
                    k.op("pe", lambda e: e.transpose(out=plg[0:32, 128:256], in_=gt[b][:], identity=ident[:]),
                         reads=[d_gt[b], d_id], writes=[d_plg])
                    k.op("dve", lambda e: e.tensor_copy(out=gT[b][:], in_=plg[0:32, 128:256]), reads=[d_plg], writes=[d_gT[b]])
                    k.dma("sp", gout[:, tok], gT[b][:], reads=[d_gT[b]], writes=[Dep()])
        k.finish([])
    return nc


def eps_ap(k):
    if not hasattr(k, "_eps"):
        k._eps = k.sb([128, 1], F32)
        k._eps_d = Dep()
        k.op("pool", lambda e: e.memset(k._eps[:], RMS_EPS), writes=[k._eps_d])
        for eng in ("act", "dve", "pe"):
            k._wait(eng, "pool", k.cnt["pool"])
    return k._eps[:]


def kb_barrier(k):
    cur = {}
    for e in ("pe", "act", "dve", "pool"):
        if k.cnt[e]:
            cur[e] = k.cnt[e]
    for q, ring in k.dmaq.items():
        n = ring["n"]
        for slot, key in enumerate(ring["keys"]):
            uses = n if q == "coll" else (n - slot + k.DMA_RING - 1) // k.DMA_RING
            if uses > 0:
                cur[key] = 16 * uses
    for e in ("pe", "act", "dve", "pool", "sp"):
        for key, v in cur.items():
            if key == e and e in ("pe", "sp"):
                continue
            k._wait(e, key, v)


KB.barrier = kb_barrier

SEQ = 8192
CTX = 256
SA = SEQ + CTX


def b_consts(SEQ=SEQ):
    p = np.arange(128)
    d = p % 64
    hi = d // 32
    dp = d % 32
    i = dp % 16
    freq = (10000.0 ** (-(i.astype(np.float32)) / 16)).astype(np.float32)
    t = np.arange(SEQ)
    pos = np.where(hi[:, None] == 0, (t // 64)[None, :], (t % 64)[None, :]).astype(np.float32)
    ang = pos * freq[:, None]
    cos = np.cos(ang).astype(np.float32)
    sin = np.sin(ang).astype(np.float32)
    prot = np.zeros((128, 128), np.float32)
    for m in range(128):
        if dp[m] < 16:
            prot[m + 16, m] = -1.0
        else:
            prot[m - 16, m] = 1.0
    blk = np.zeros((128, 128), np.float32)
    blk[:64, :64] = 1.0 / 64
    blk[64:, 64:] = 1.0 / 64
    masks = np.zeros((8, 128, 512), np.float32)
    tl = np.arange(512)[None, :]
    for q in range(4):
        s = (128 * q + np.arange(128))[:, None]
        masks[q] = np.where(s <= tl, 0.0, -30000.0)
        masks[4 + q] = np.where(s >= tl, 0.0, -30000.0)
    return {"cos": cos, "sin": sin, "prot": prot, "blk": blk, "masks": masks,
            "ident": np.eye(128, dtype=np.float32)}


def build_b(lam_init, nbatch=2, SEQ=SEQ, phases="PAS"):
    nc = new_nc()
    SA = SEQ + CTX
    NTOK = 2 * SEQ
    hT = nc.dram_tensor("hT", [NTOK // 512, 128, 8, 512], BF16, kind="ExternalInput").ap()
    hcT = nc.dram_tensor("hcT", [2, 128, 8, CTX], BF16, kind="ExternalInput").ap()
    W = nc.dram_tensor("W", [1024, 900], F32, kind="ExternalInput").ap()
    pp = nc.dram_tensor("pp", [128, 32], F32, kind="ExternalInput").ap()
    lamd = nc.dram_tensor("lam4", [4, 64], F32, kind="ExternalInput").ap()
    p4 = nc.dram_tensor("p4", [2, 4], F32, kind="ExternalInput").ap()
    cosd = nc.dram_tensor("cos", [128, SEQ], F32, kind="ExternalInput").ap()
    sind = nc.dram_tensor("sin", [128, SEQ], F32, kind="ExternalInput").ap()
    protd = nc.dram_tensor("prot", [128, 128], F32, kind="ExternalInput").ap()
    blkd = nc.dram_tensor("blk", [128, 128], F32, kind="ExternalInput").ap()
    trid = nc.dram_tensor("tri", [128, 128], F32, kind="ExternalInput").ap()
    maskd = nc.dram_tensor("masks", [8, 128, 512], F32, kind="ExternalInput").ap()
    identd = nc.dram_tensor("ident", [128, 128], F32, kind="ExternalInput").ap()
    oatt = nc.dram_tensor("oT", [128, NTOK], BF16, kind="ExternalOutput").ap()
    ygo = nc.dram_tensor("ygT", [128, NTOK], BF16, kind="ExternalOutput").ap()
    sso = nc.dram_tensor("ss", [1, NTOK], F32, kind="ExternalOutput").ap()
    urow = nc.dram_tensor("urow", [4, SEQ], F32, kind="Internal").ap()
    WQ, WK, WV, WDT, WZ, WX, WBm, WC = 0, 128, 256, 384, 388, 516, 644, 772
    NS = SA // 128
    NL = SEQ // 128
    BUFW = SEQ + 276
    LATR, CTXR = 6, SEQ + 14
    LATC, CTXC = 2, SEQ + 10

    def ccol(s):
        return LATC + 128 * s if s < NL else CTXC + 128 * (s - NL)

    with ExitStack() as es:
        k = KB(nc, es)
        eps = eps_ap(k)
        ident = k.sb([128, 128], F32)
        identb = k.sb([128, 128], BF16)
        prot = k.sb([128, 128], BF16)
        blk = k.sb([128, 128], BF16)
        ones = k.sb([128, 128], BF16)
        onesf = k.sb([128, 128], F32)
        tri = k.sb([128, 128], F32)
        pps = k.sb([128, 32], F32)
        lam = k.sb([128, 8], F32)
        lam_in = k.sb([128, 4, 64], F32)
        p4s = k.sb([128, 2, 4], F32)
        Wb = k.sb([128, 8, 900], BF16)
        dC = Dep()
        with ExitStack() as es2:
            k.es = es2
            stg = k.sb([128, 900], F32)
            k.dma("sp", ident[:], identd, writes=[dC])
            k.dma("sp", tri[:], trid, writes=[dC])
            k.dma("sp", stg[:, 0:128], protd, writes=[dC])
            k.op("dve", lambda e: e.tensor_copy(out=prot[:], in_=stg[:, 0:128]), reads=[dC], writes=[dC])
            k.dma("sp", stg[:, 0:128], blkd, writes=[dC])
            k.op("dve", lambda e: e.tensor_copy(out=blk[:], in_=stg[:, 0:128]), reads=[dC], writes=[dC])
            k.op("dve", lambda e: e.tensor_copy(out=identb[:], in_=ident[:]), reads=[dC], writes=[dC])
            k.op("dve", lambda e: e.memset(ones[:], 1.0), writes=[dC])
            k.op("dve", lambda e: e.memset(onesf[:], 1.0), writes=[dC])
            k.dma("sp", pps[:], pp, writes=[dC])
            k.dma("sp", p4s[:].rearrange("p a r -> p (a r)"), p4.rearrange("a r -> (a r)").partition_broadcast(128), writes=[dC])
            k.dma("sp", lam_in[:].rearrange("p a d -> p (a d)"), lamd.rearrange("a d -> (a d)").partition_broadcast(128), writes=[dC])
            k.op("dve", lambda e: e.tensor_tensor(out=lam_in[:, 0, :], in0=lam_in[:, 0, :], in1=lam_in[:, 1, :], op=ALU.mult), reads=[dC], writes=[dC])
            k.op("dve", lambda e: e.tensor_tensor(out=lam_in[:, 2, :], in0=lam_in[:, 2, :], in1=lam_in[:, 3, :], op=ALU.mult), reads=[dC], writes=[dC])
            k.op("dve", lambda e: e.tensor_reduce(out=lam[:, 0:1], in_=lam_in[:, 0, :], axis=AX.X, op=ALU.add), reads=[dC], writes=[dC])
            k.op("dve", lambda e: e.tensor_reduce(out=lam[:, 1:2], in_=lam_in[:, 2, :], axis=AX.X, op=ALU.add), reads=[dC], writes=[dC])
            k.op("act", lambda e: e.activation(out=lam[:, 0:2], in_=lam[:, 0:2], func=AF.Exp), reads=[dC], writes=[dC])
            k.op("dve", lambda e: e.tensor_tensor(out=lam[:, 2:3], in0=lam[:, 1:2], in1=lam[:, 0:1], op=ALU.subtract), reads=[dC], writes=[dC])
            k.op("dve", lambda e: e.tensor_scalar(out=lam[:, 3:4], in0=lam[:, 2:3], scalar1=-float(lam_init), scalar2=None, op0=ALU.add), reads=[dC], writes=[dC])
            k.op("dve", lambda e: e.tensor_scalar(out=pps[:, 23:24], in0=pps[:, 2:3], scalar1=float(1.0 - lam_init), scalar2=None, op0=ALU.mult), reads=[dC], writes=[dC])
            k.op("act", lambda e: e.activation(out=p4s[:, 1, :], in_=p4s[:, 1, :], func=AF.Exp), reads=[dC], writes=[dC])
            k.op("dve", lambda e: e.tensor_scalar(out=p4s[:, 1, :], in0=p4s[:, 1, :], scalar1=-1.0, scalar2=None, op0=ALU.mult), reads=[dC], writes=[dC])
            for c in range(8):
                k.dma("sp", stg[:], W[c * 128:(c + 1) * 128, :], writes=[dC])
                k.op("dve", lambda e: e.tensor_copy(out=Wb[:, c, :], in_=stg[:]), reads=[dC], writes=[dC])
            k.barrier()
        k.es = es

        xr = k.sb([128, BUFW], BF16)
        Br = k.sb([128, BUFW], BF16)
        Cr = k.sb([128, BUFW], BF16)
        sz = k.sb([128, SEQ], BF16)
        dtc = k.sb([128, NS, 4], F32)
        for b in range(nbatch):
            d_q, d_k, d_v, d_x, d_B, d_Cc, d_z, d_dt = (Dep() for _ in range(8))
            with ExitStack() as esm:
                k.es = esm
                qT = k.sb([128, SEQ], BF16)
                kT = k.sb([128, SA], BF16)
                vS = k.sb([128, NS, 128], BF16)
                with ExitStack() as es2:
                    k.es = es2
                    hb = [k.sb([128, 8, 512], BF16) for _ in range(2)]
                    d_hb = [Dep(), Dep()]
                    cs_ = [k.sb([128, 512], F32) for _ in range(2)]
                    sn_ = [k.sb([128, 512], F32) for _ in range(2)]
                    d_cs = [Dep(), Dep()]
                    sq = k.sb([128, 512], BF16); d_sq = Dep()
                    rst = k.sb([128, 512], F32); d_rst = Dep()
                    qn = k.sb([128, 512], F32); d_qn = Dep()
                    qnb = k.sb([128, 512], BF16); d_qnb = Dep()
                    t1 = k.sb([128, 512], F32); d_t1 = Dep()
                    pf = [k.ps([128, 512]) for _ in range(3)]
                    d_pf = [Dep() for _ in range(3)]
                    pms = k.ps([128, 512]); d_pms = Dep()
                    prt = k.ps([128, 512]); d_prt = Dep()
                    pv = k.ps([128, 512]); d_pv = Dep()
                    pdt = k.ps([128, 512]); d_pdt = Dep()
                    for buf, dd in ((xr, d_x), (Br, d_B), (Cr, d_Cc)):
                        k.op("dve", lambda e: e.memset(buf[:, 0:LATR], 0.0), writes=[dd])
                        k.op("dve", lambda e: e.memset(buf[:, LATR + SEQ:CTXR], 0.0), writes=[dd])
                        k.op("dve", lambda e: e.memset(buf[:, CTXR + CTX:BUFW], 0.0), writes=[dd])
                    nblk = SEQ // 512 + 1
                    ipf = [0]
                    for blk_i in range(nblk):
                        isctx = blk_i == nblk - 1
                        n = 256 if isctx else 512
                        hbuf = hb[blk_i % 2]
                        dh = d_hb[blk_i % 2]
                        if isctx:
                            src = hcT[b]
                        else:
                            src = hT[b * (SEQ // 512) + blk_i]
                        k.dma("sp", hbuf[:, :, 0:n], src, writes=[dh])
                        t0 = blk_i * 512
                        dcs = d_cs[blk_i % 2]
                        if not isctx:
                            k.dma("sp", cs_[blk_i % 2][:], cosd[:, t0:t0 + 512], writes=[dcs])
                            k.dma("sp", sn_[blk_i % 2][:], sind[:, t0:t0 + 512], writes=[dcs])

                        def proj(wcol, width=128):
                            j = ipf[0] % 3
                            ipf[0] += 1
                            for c in range(8):
                                k.op("pe", lambda e: e.matmul(pf[j][0:width, 0:n], lhsT=Wb[:, c, wcol:wcol + width], rhs=hbuf[:, c, 0:n],
                                                              start=(c == 0), stop=(c == 7)), reads=[dh], writes=[d_pf[j]])
                            return pf[j], d_pf[j]

                        for which in ("q", "k"):
                            if which == "q" and isctx:
                                continue
                            ps_, dps_ = proj(WQ if which == "q" else WK)
                            gcol = 0 if which == "q" else 1
                            k.op("act", lambda e: e.activation(out=sq[:, 0:n], in_=ps_[:, 0:n], func=AF.Square), reads=[dps_], writes=[d_sq])
                            k.op("pe", lambda e: e.matmul(pms[:, 0:n], lhsT=blk[:], rhs=sq[:, 0:n], start=True, stop=True), reads=[d_sq], writes=[d_pms])
                            k.op("act", lambda e: e.activation(out=rst[:, 0:n], in_=pms[:, 0:n], func=AF.Sqrt, bias=eps, scale=1.0), reads=[d_pms], writes=[d_rst])
                            k.op("dve", lambda e: e.reciprocal(out=rst[:, 0:n], in_=rst[:, 0:n]), reads=[], writes=[d_rst])
                            if isctx:
                                k.op("dve", lambda e: e.scalar_tensor_tensor(out=kT[:, SEQ:SA], in0=ps_[:, 0:n], scalar=pps[:, gcol:gcol + 1], in1=rst[:, 0:n],
                                                                             op0=ALU.mult, op1=ALU.mult), reads=[dps_, d_rst], writes=[d_k])
                                continue
                            k.op("dve", lambda e: e.scalar_tensor_tensor(out=qn[:], in0=ps_[:], scalar=pps[:, gcol:gcol + 1], in1=rst[:],
                                                                         op0=ALU.mult, op1=ALU.mult), reads=[dps_, d_rst], writes=[d_qn])
                            k.op("act", lambda e: e.activation(out=qnb[:], in_=qn[:], func=AF.Copy), reads=[d_qn], writes=[d_qnb])
                            k.op("pe", lambda e: e.matmul(prt[:], lhsT=prot[:], rhs=qnb[:], start=True, stop=True), reads=[d_qnb], writes=[d_prt])
                            k.op("dve", lambda e: e.tensor_tensor(out=t1[:], in0=prt[:], in1=sn_[blk_i % 2][:], op=ALU.mult), reads=[d_prt, dcs], writes=[d_t1])
                            k.op("dve", lambda e: e.tensor_tensor(out=qn[:], in0=qn[:], in1=cs_[blk_i % 2][:], op=ALU.mult), reads=[dcs], writes=[d_qn])
                            dst = qT if which == "q" else kT
                            dd = d_q if which == "q" else d_k
                            k.op("dve", lambda e: e.tensor_tensor(out=dst[:, t0:t0 + 512], in0=qn[:], in1=t1[:], op=ALU.add), reads=[d_qn, d_t1], writes=[dd])
                        c0 = (CTXR if isctx else LATR + t0)
                        for wcol, buf, dd in ((WX, xr, d_x), (WBm, Br, d_B), (WC, Cr, d_Cc)):
                            ps_, dps_ = proj(wcol)
                            k.op("act", lambda e: e.activation(out=buf[:, c0:c0 + n], in_=ps_[:, 0:n], func=AF.Copy), reads=[dps_], writes=[dd])
                        if not isctx:
                            ps_, dps_ = proj(WZ)
                            k.op("act", lambda e: e.activation(out=sz[:, t0:t0 + 512], in_=ps_[:], func=AF.Silu), reads=[dps_], writes=[d_z])
                        ti = (NL if isctx else blk_i * 4)
                        for tt in range(n // 128):
                            for c in range(8):
                                k.op("pe", lambda e: e.matmul(pv[:, tt * 128:(tt + 1) * 128], lhsT=hbuf[:, c, tt * 128:(tt + 1) * 128], rhs=Wb[:, c, WV:WV + 128],
                                                              start=(c == 0), stop=(c == 7)), reads=[dh], writes=[d_pv])
                            for c in range(8):
                                k.op("pe", lambda e: e.matmul(pdt[:, tt * 4:(tt + 1) * 4], lhsT=hbuf[:, c, tt * 128:(tt + 1) * 128], rhs=Wb[:, c, WDT:WDT + 4],
                                                              start=(c == 0), stop=(c == 7)), reads=[dh], writes=[d_pdt])
                        k.op("dve", lambda e: e.tensor_copy(out=vS[:, ti:ti + n // 128, :].rearrange("p a e -> p (a e)"), in_=pv[:, 0:n]), reads=[d_pv], writes=[d_v])
                        k.op("dve", lambda e: e.tensor_copy(out=dtc[:, ti:ti + n // 128, :].rearrange("p a r -> p (a r)"), in_=pdt[:, 0:n // 32]), reads=[d_pdt], writes=[d_dt])
                    k.barrier()
                k.es = esm
                with ExitStack() as es2:
                    k.es = es2
                    pS = [k.ps([128, 512]) for _ in range(4)]
                    d_pS = [Dep() for _ in range(4)]
                    po = [k.ps([128, 512]) for _ in range(2)]
                    psm = [k.ps([128, 512]) for _ in range(2)]
                    d_po = [Dep(), Dep()]
                    d_psm = [Dep(), Dep()]
                    Pt = [k.sb([128, 512], BF16) for _ in range(4)]
                    d_Pt = [Dep() for _ in range(4)]
                    r_ = [k.sb([128, 512], F32) for _ in range(2)]
                    d_r = [Dep(), Dep()]
                    o_ = k.sb([128, 512], F32); d_o = Dep()
                    osq = k.sb([128, 512], BF16); d_osq = Dep()
                    ob = k.sb([128, 512], BF16); d_ob = Dep()
                    for tb in range(SEQ // 512 if "A" in phases else 0):
                        tsl = slice(tb * 512, (tb + 1) * 512)
                        items = [(c, s) for c in range(2) for s in range(NS)]

                        def qk(i):
                            c, s = items[i]
                            j = i % 4
                            k.op("pe", lambda e: e.matmul(pS[j][:], lhsT=kT[c * 64:(c + 1) * 64, s * 128:(s + 1) * 128], rhs=qT[c * 64:(c + 1) * 64, tsl],
                                                          start=True, stop=True), reads=[d_q, d_k], writes=[d_pS[j]])
                            k.op("act", lambda e: e.activation(out=Pt[j][:], in_=pS[j][:], func=AF.Exp, scale=0.125), reads=[d_pS[j]], writes=[d_Pt[j]])

                        def pvm(i):
                            c, s = items[i]
                            j = i % 4
                            k.op("pe", lambda e: e.matmul(po[c][:], lhsT=vS[:, s, :], rhs=Pt[j][:], start=(s == 0), stop=(s == NS - 1)),
                                 reads=[d_v, d_Pt[j]], writes=[d_po[c]])
                            k.op("pe", lambda e: e.matmul(psm[c][:], lhsT=ones[:], rhs=Pt[j][:], start=(s == 0), stop=(s == NS - 1)),
                                 reads=[d_Pt[j]], writes=[d_psm[c]])

                        qk(0)
                        qk(1)
                        for i in range(len(items)):
                            if i + 2 < len(items):
                                qk(i + 2)
                            pvm(i)
                        for c in range(2):
                            k.op("dve", lambda e: e.reciprocal(out=r_[c][:], in_=psm[c][:]), reads=[d_psm[c]], writes=[d_r[c]])
                            k.op("dve", lambda e: e.tensor_tensor(out=r_[c][:], in0=po[c][:], in1=r_[c][:], op=ALU.mult), reads=[d_po[c]], writes=[d_r[c]])
                        k.op("dve", lambda e: e.scalar_tensor_tensor(out=o_[:], in0=r_[1][:], scalar=lam[:, 3:4], in1=r_[0][:], op0=ALU.mult, op1=ALU.add),
                             reads=[d_r[0], d_r[1]], writes=[d_o])
                        k.op("act", lambda e: e.activation(out=osq[:], in_=o_[:], func=AF.Square), reads=[d_o], writes=[d_osq])
                        k.op("pe", lambda e: e.matmul(pS[0][:], lhsT=ones[:], rhs=osq[:], start=True, stop=True), reads=[d_osq], writes=[d_pS[0]])
                        k.op("act", lambda e: e.activation(out=r_[0][:], in_=pS[0][:], func=AF.Sqrt, bias=eps, scale=1.0 / 128), reads=[d_pS[0]], writes=[d_r[0]])
                        k.op("dve", lambda e: e.reciprocal(out=r_[0][:], in_=r_[0][:]), reads=[], writes=[d_r[0]])
                        k.op("dve", lambda e: e.scalar_tensor_tensor(out=ob[:], in0=o_[:], scalar=pps[:, 23:24], in1=r_[0][:], op0=ALU.mult, op1=ALU.mult),
                             reads=[d_o, d_r[0]], writes=[d_ob])
                        k.dma("sp", oatt[:, b * SEQ + tb * 512: b * SEQ + (tb + 1) * 512], ob[:], reads=[d_ob], writes=[Dep()])
                    k.barrier()
                k.es = esm
            k.es = es
            with ExitStack() as es2:
              if "S" in phases:
                  k.es = es2
                  xtok = k.sb([128, NS, 128], BF16); d_xt = Dep()
                  acc = [k.sb([128, 2048], F32) for _ in range(2)]
                  d_acc = [Dep(), Dep()]
                  for (buf, wc0, dd) in ((xr, 3, d_x), (Br, 9, d_B), (Cr, 15, d_Cc)):
                      segs = [(LATR + s0, min(2048, SEQ)) for s0 in range(0, SEQ, 2048)] + [(CTXR, CTX)]
                      for si, (rc, n) in enumerate(segs):
                          a_ = acc[si % 2]; da = d_acc[si % 2]
                          k.op("dve", lambda e: e.tensor_scalar(out=a_[:, 0:n], in0=buf[:, rc - 2:rc - 2 + n], scalar1=pps[:, wc0:wc0 + 1], scalar2=None, op0=ALU.mult),
                               reads=[dd], writes=[da])
                          for tap in range(1, 5):
                              k.op("dve", lambda e: e.scalar_tensor_tensor(out=a_[:, 0:n], in0=buf[:, rc - 2 + tap:rc - 2 + tap + n], scalar=pps[:, wc0 + tap:wc0 + tap + 1],
                                                                           in1=a_[:, 0:n], op0=ALU.mult, op1=ALU.add), reads=[dd], writes=[da])
                          k.op("act", lambda e: e.activation(out=buf[:, rc - 4:rc - 4 + n], in_=a_[:, 0:n], func=AF.Silu, bias=pps[:, wc0 + 5:wc0 + 6], scale=1.0),
                               reads=[da], writes=[dd])
                  xc, Bc, Cc = xr, Br, Cr
                  dA = k.sb([128, NS, 4], F32); d_dA = Dep()
                  wi = k.sb([128, NS, 4], F32); d_wi = Dep()
                  tot = k.sb([128, NS, 4], F32); d_tot = Dep()
                  inc = k.sb([128, NS, 4], F32); d_inc = Dep()
                  ncol = k.sb([128, NS, 4], F32); d_nc = Dep()
                  on66 = k.sb([128, NS], F32); d_on = Dep()
                  pw = k.ps([128, 512]); d_pw = Dep()
                  pt_ = k.ps([128, 512]); d_pt = Dep()
                  k.op("dve", lambda e: e.memset(on66[:], 1.0), writes=[d_on])
                  for r in range(4):
                      k.op("dve", lambda e: e.tensor_scalar(out=dtc[:, :, r], in0=dtc[:, :, r], scalar1=p4s[:, 0, r:r + 1], scalar2=None, op0=ALU.add), reads=[d_dt], writes=[d_dt])
                  k.op("act", lambda e: e.activation(out=dtc[:].rearrange("p a r -> p (a r)"), in_=dtc[:].rearrange("p a r -> p (a r)"), func=AF.Exp), reads=[], writes=[d_dt])
                  k.op("act", lambda e: e.activation(out=dtc[:].rearrange("p a r -> p (a r)"), in_=dtc[:].rearrange("p a r -> p (a r)"), func=AF.Ln, bias=1.0, scale=1.0), reads=[], writes=[d_dt])
                  for r in range(4):
                      k.op("dve", lambda e: e.tensor_scalar(out=dA[:, :, r], in0=dtc[:, :, r], scalar1=p4s[:, 1, r:r + 1], scalar2=None, op0=ALU.mult), reads=[d_dt], writes=[d_dA])
                  k.op("pe", lambda e: e.matmul(pw[:, 0:NS * 4], lhsT=tri[:], rhs=dA[:].rearrange("p a r -> p (a r)"), start=True, stop=True), reads=[d_dA], writes=[d_pw])
                  k.op("pe", lambda e: e.matmul(pt_[:, 0:NS * 4], lhsT=onesf[:], rhs=dA[:].rearrange("p a r -> p (a r)"), start=True, stop=True), reads=[d_dA], writes=[d_pt])
                  k.op("dve", lambda e: e.tensor_copy(out=wi[:].rearrange("p a r -> p (a r)"), in_=pw[:, 0:NS * 4]), reads=[d_pw], writes=[d_wi])
                  k.op("dve", lambda e: e.tensor_copy(out=tot[:].rearrange("p a r -> p (a r)"), in_=pt_[:, 0:NS * 4]), reads=[d_pt], writes=[d_tot])
                  for r in range(2):
                      k.op("dve", lambda e: e.tensor_tensor_scan(out=inc[:, NL:NS, r], data0=on66[:, NL:NS], data1=tot[:, NL:NS, r], initial=0.0, op0=ALU.mult, op1=ALU.add),
                           reads=[d_tot, d_on], writes=[d_inc])
                      k.op("dve", lambda e: e.tensor_tensor_scan(out=inc[:, 0:NL, r], data0=on66[:, 0:NL], data1=tot[:, 0:NL, r], initial=inc[:, NS - 1, r:r + 1], op0=ALU.mult, op1=ALU.add),
                           reads=[d_tot, d_on], writes=[d_inc])
                  for r in range(2, 4):
                      k.op("dve", lambda e: e.tensor_tensor_scan(out=inc[:, :, r], data0=on66[:], data1=tot[:, :, r], initial=0.0, op0=ALU.mult, op1=ALU.add),
                           reads=[d_tot, d_on], writes=[d_inc])
                  k.op("dve", lambda e: e.tensor_tensor(out=ncol[:], in0=tot[:], in1=inc[:], op=ALU.subtract), reads=[d_tot, d_inc], writes=[d_nc])
                  k.op("dve", lambda e: e.tensor_tensor(out=ncol[:, :, 0:2], in0=ncol[:, :, 0:2], in1=wi[:, :, 0:2], op=ALU.subtract), reads=[d_wi], writes=[d_nc])
                  k.op("dve", lambda e: e.tensor_tensor(out=ncol[:, :, 2:4], in0=wi[:, :, 2:4], in1=ncol[:, :, 2:4], op=ALU.subtract), reads=[d_wi], writes=[d_nc])
                  k.op("dve", lambda e: e.tensor_tensor(out=ncol[:, :, 2:4], in0=ncol[:, :, 2:4], in1=dA[:, :, 2:4], op=ALU.subtract), reads=[d_dA], writes=[d_nc])
                  k.op("dve", lambda e: e.tensor_scalar(out=wi[:], in0=ncol[:], scalar1=-1.0, scalar2=None, op0=ALU.mult), reads=[d_nc], writes=[d_wi])
                  ur = [k.sb([4, 512], F32) for _ in range(2)]
                  d_ur = [Dep(), Dep()]
                  d_urow = Dep()
                  for g in range(NL // 4):
                      for q in range(4):
                          k.op("pe", lambda e: e.transpose(out=pw[0:4, q * 128:(q + 1) * 128], in_=wi[:, 4 * g + q, :], identity=ident[:]), reads=[d_wi], writes=[d_pw])
                      k.op("dve", lambda e: e.tensor_copy(out=ur[g % 2][:], in_=pw[0:4, :]), reads=[d_pw], writes=[d_ur[g % 2]])
                      k.dma("sp", urow[:, g * 512:(g + 1) * 512], ur[g % 2][:], reads=[d_ur[g % 2]], writes=[d_urow])
                  ptx = k.ps([128, 1024], BF16); d_ptx = Dep()
                  for s0 in range(0, NS, 8):
                      ns_ = min(8, NS - s0)
                      for s in range(ns_):
                          cc = ccol(s0 + s)
                          k.op("pe", lambda e: e.transpose(out=ptx[:, s * 128:(s + 1) * 128], in_=xc[:, cc:cc + 128], identity=identb[:]),
                               reads=[d_x], writes=[d_ptx])
                      k.op("dve", lambda e: e.tensor_copy(out=xtok[:, s0:s0 + ns_, :].rearrange("p a e -> p (a e)"), in_=ptx[:, 0:ns_ * 128]),
                           reads=[d_ptx], writes=[d_xt])
                  msk = k.sb([128, 8, 512], BF16); d_msk = Dep()
                  for q in range(8):
                      k.dma("sp", acc[q % 2][:, 0:512], maskd[q], writes=[d_acc[q % 2]])
                      k.op("dve", lambda e: e.tensor_copy(out=msk[:, q, :], in_=acc[q % 2][:, 0:512]), reads=[d_acc[q % 2]], writes=[d_msk])
                  ub = [k.sb([128, 4, 512], F32) for _ in range(2)]
                  d_ub = [Dep(), Dep()]
                  pCB = [k.ps([128, 512]) for _ in range(2)]
                  d_pCB = [Dep(), Dep()]
                  py = [k.ps([128, 512]) for _ in range(2)]
                  d_py = [Dep(), Dep()]
                  pss = k.ps([128, 512]); d_pss = Dep()
                  Et = [k.sb([128, 512], F32) for _ in range(4)]
                  d_Et = [Dep() for _ in range(4)]
                  Wt = [k.sb([128, 512], BF16) for _ in range(4)]
                  d_Wt = [Dep() for _ in range(4)]
                  mt = [k.sb([128, 512], F32) for _ in range(4)]
                  d_mt = [Dep() for _ in range(4)]
                  yv = k.sb([128, 512], F32); d_yv = Dep()
                  ygb = k.sb([128, 512], BF16); d_ygb = Dep()
                  ysq = k.sb([128, 512], BF16); d_ysq = Dep()
                  ssr = k.sb([1, 512], F32); d_ssr = Dep()
                  for tb in range(SEQ // 512):
                      tsl = slice(tb * 512, (tb + 1) * 512)
                      csl = slice(LATC + tb * 512, LATC + (tb + 1) * 512)
                      u_ = ub[tb % 2]; du = d_ub[tb % 2]
                      for r in range(4):
                          k.dma("sp", u_[:, r, :], urow[r, tsl].partition_broadcast(128), reads=[d_urow], writes=[du])
                      items = []
                      for s in list(range(NL, NS)) + list(range(0, 4 * tb + 4)):
                          mq = (s - 4 * tb) if (s < NL and s >= 4 * tb) else None
                          items.append((0, s, mq))
                      for s in list(range(4 * tb, NL)) + list(range(NL, NS)):
                          mq = (4 + s - 4 * tb) if (s < 4 * tb + 4) else None
                          items.append((1, s, mq))
                      nit = len(items)

                      def cb(i):
                          dr_, s, mq = items[i]
                          j = i % 2
                          cc = ccol(s)
                          k.op("pe", lambda e: e.matmul(pCB[j][:], lhsT=Bc[:, cc:cc + 128], rhs=Cc[:, csl], start=True, stop=True),
                               reads=[d_B, d_Cc], writes=[d_pCB[j]])

                      def rest(i):
                          dr_, s, mq = items[i]
                          j = i % 2
                          for hd in range(2):
                              r = dr_ * 2 + hd
                              jj = (2 * i + hd) % 4
                              src = u_[:, r, :]
                              rd = [du]
                              if mq is not None:
                                  k.op("pool", lambda e: e.tensor_tensor(out=mt[jj][:], in0=u_[:, r, :], in1=msk[:, mq, :], op=ALU.add),
                                       reads=[du, d_msk], writes=[d_mt[jj]])
                                  src = mt[jj][:]
                                  rd = [d_mt[jj]]
                              k.op("act", lambda e: e.activation(out=Et[jj][:], in_=src, func=AF.Exp, bias=ncol[:, s, r:r + 1], scale=1.0),
                                   reads=rd + [d_nc], writes=[d_Et[jj]])
                              k.op("dve", lambda e: e.scalar_tensor_tensor(out=Wt[jj][:], in0=Et[jj][:], scalar=dtc[:, s, r:r + 1], in1=pCB[j][:],
                                                                           op0=ALU.mult, op1=ALU.mult), reads=[d_Et[jj], d_dt, d_pCB[j]], writes=[d_Wt[jj]])
                              k.op("pe", lambda e: e.matmul(py[hd][hd * 64:(hd + 1) * 64, :], lhsT=xtok[:, s, hd * 64:(hd + 1) * 64], rhs=Wt[jj][:],
                                                            start=(i == 0), stop=(i == nit - 1)),
                                   reads=[d_xt, d_Wt[jj]], writes=[d_py[hd]])

                      cb(0)
                      for i in range(nit):
                          if i + 1 < nit:
                              cb(i + 1)
                          rest(i)
                      for hd in range(2):
                          hs = slice(hd * 64, (hd + 1) * 64)
                          k.op("dve", lambda e: e.scalar_tensor_tensor(out=yv[hs, :], in0=xc[hs, csl], scalar=pps[hs, 21:22], in1=py[hd][hs, :],
                                                                       op0=ALU.mult, op1=ALU.add), reads=[d_x, d_py[hd]], writes=[d_yv])
                      k.op("dve", lambda e: e.tensor_tensor(out=yv[:], in0=yv[:], in1=sz[:, tsl], op=ALU.mult), reads=[d_z], writes=[d_yv])
                      k.op("act", lambda e: e.activation(out=ysq[:], in_=yv[:], func=AF.Square), reads=[d_yv], writes=[d_ysq])
                      k.op("pe", lambda e: e.matmul(pss[0:1, :], lhsT=ones[:, 0:1], rhs=ysq[:], start=True, stop=True), reads=[d_ysq], writes=[d_pss])
                      k.op("dve", lambda e: e.tensor_copy(out=ssr[:], in_=pss[0:1, :]), reads=[d_pss], writes=[d_ssr])
                      k.op("dve", lambda e: e.tensor_scalar(out=ygb[:], in0=yv[:], scalar1=pps[:, 22:23], scalar2=None, op0=ALU.mult), reads=[d_yv], writes=[d_ygb])
                      k.dma("sp", ygo[:, b * SEQ + tb * 512: b * SEQ + (tb + 1) * 512], ygb[:], reads=[d_ygb], writes=[Dep()])
                      k.dma("sp", sso[:, b * SEQ + tb * 512: b * SEQ + (tb + 1) * 512], ssr[:], reads=[d_ssr], writes=[Dep()])
                  k.barrier()
            k.es = es
        k.finish([])
    return nc


SW_LIMIT = 7.0
SW_ALPHA = 1.702


def build_d(NTOK=16384):
    nc = new_nc()
    h2T = nc.dram_tensor("h2T", [NTOK // 512, 128, 8, 512], BF16, kind="ExternalInput").ap()
    gT = nc.dram_tensor("gT", [4, NTOK], F32, kind="ExternalInput").ap()
    wgu = nc.dram_tensor("wgu", [4, 1024, 2048], F32, kind="ExternalInput").ap()
    bgu = nc.dram_tensor("bgu", [128, 64], F32, kind="ExternalInput").ap()
    wdn = nc.dram_tensor("wdn", [4, 1024, 1024], F32, kind="ExternalInput").ap()
    bdn = nc.dram_tensor("bdn", [4, 1024], F32, kind="ExternalInput").ap()
    part = nc.dram_tensor("part", [NTOK, 1024], BF16, kind="ExternalOutput").ap()
    scr = nc.dram_tensor("scr", [NTOK, 1024], F32, kind="Internal").ap()
    NBLK = NTOK // 512
    with ExitStack() as es:
        k = KB(nc, es)
        wg = k.sb([128, 2, 8, 2048], BF16); d_wg = Dep()
        wd = k.sb([128, 2, 8, 1024], BF16); d_wd = Dep()
        bd = k.sb([128, 2, 1024], BF16); d_bd = Dep()
        bg = k.sb([128, 64], F32); d_bg = Dep()
        stg = [k.sb([128, 2048], F32) for _ in range(2)]
        d_stg = [Dep(), Dep()]
        hb = [k.sb([128, 8, 512], BF16) for _ in range(2)]
        d_hb = [Dep(), Dep()]
        gb = [k.sb([128, 2, 512], F32) for _ in range(2)]
        d_gb = [Dep(), Dep()]
        gr = [k.sb([1, 2, 512], F32) for _ in range(2)]
        grb = [k.sb([128, 2, 512], BF16) for _ in range(2)]
        d_gr = [Dep(), Dep()]
        d_grb = [Dep(), Dep()]
        act = k.sb([128, 16, 512], BF16)
        d_act = [Dep() for _ in range(16)]
        tg = [k.sb([128, 512], F32) for _ in range(3)]
        ts_ = [k.sb([128, 512], F32) for _ in range(3)]
        tl = [k.sb([128, 512], F32) for _ in range(3)]
        d_tg = [Dep() for _ in range(3)]
        d_ts = [Dep() for _ in range(3)]
        d_tl = [Dep() for _ in range(3)]
        ot = [k.sb([128, 1024], F32) for _ in range(2)]
        d_ot = [Dep(), Dep()]
        otb = [k.sb([128, 1024], BF16) for _ in range(2)]
        d_otb = [Dep(), Dep()]
        pvt = [k.sb([128, 1024], F32) for _ in range(2)]
        d_pvt = [Dep(), Dep()]
        pG = [k.ps([128, 512]) for _ in range(2)]
        pL = [k.ps([128, 512]) for _ in range(2)]
        d_pG = [Dep(), Dep()]
        d_pL = [Dep(), Dep()]
        pY = [k.ps([128, 512]) for _ in range(2)]
        d_pY = [Dep(), Dep()]
        d_part = [Dep() for _ in range(NBLK * 4)]
        k.dma("sp", bg[:], bgu, writes=[d_bg])
        k.op("dve", lambda e: e.memset(bd[:], 0.0), writes=[d_bd])
        for b_ in range(2):
            k.op("dve", lambda e: e.memset(grb[b_][:], 0.0), writes=[d_grb[b_]])
        nst = 0
        for ps_ in range(2):
            for el in range(2):
                e_ = 2 * ps_ + el
                for kc in range(8):
                    s_, ds_ = stg[nst % 2], d_stg[nst % 2]
                    nst += 1
                    k.dma("sp", s_[:], wgu[e_, kc * 128:(kc + 1) * 128, :], writes=[ds_])
                    v = s_[:].rearrange("p (f two) -> p two f", two=2)
                    k.op("act", lambda e: e.activation(out=wg[:, el, kc, 0:1024], in_=v[:, 0, :], func=AF.Copy), reads=[ds_], writes=[d_wg])
                    k.op("pool", lambda e: e.tensor_copy(out=wg[:, el, kc, 1024:2048], in_=v[:, 1, :]), reads=[ds_], writes=[d_wg])
                for fc in range(0, 8, 2):
                    s_, ds_ = stg[nst % 2], d_stg[nst % 2]
                    nst += 1
                    k.dma("sp", s_[:].rearrange("p (a d) -> p a d", a=2), wdn[e_, fc * 128:(fc + 2) * 128, :].rearrange("(a p) d -> p a d", p=128), writes=[ds_])
                    k.op("dve", lambda e: e.tensor_copy(out=wd[:, el, fc:fc + 2, :].rearrange("p a d -> p (a d)"), in_=s_[:]), reads=[ds_], writes=[d_wd])
                s_, ds_ = stg[nst % 2], d_stg[nst % 2]
                nst += 1
                k.dma("sp", s_[0:1, 0:1024], bdn[e_:e_ + 1, :], writes=[ds_])
                k.op("dve", lambda e: e.tensor_copy(out=bd[0:1, el, :], in_=s_[0:1, 0:1024]), reads=[ds_], writes=[d_bd])
            for blk_i in range(NBLK):
                b = blk_i % 2
                tsl = slice(blk_i * 512, (blk_i + 1) * 512)
                k.dma("sp", hb[b][:], h2T[blk_i], writes=[d_hb[b]])
                for el in range(2):
                    k.dma("sp", gb[b][:, el, :], gT[2 * ps_ + el, tsl].partition_broadcast(128), writes=[d_gb[b]])
                k.dma("sp", gr[b][:].rearrange("o a t -> o (a t)").rearrange("o (a t) -> o a t", a=2), gT[2 * ps_:2 * ps_ + 2, tsl].rearrange("(o a) t -> o a t", o=1), writes=[d_gr[b]])
                k.op("dve", lambda e: e.tensor_copy(out=grb[b][0:1, :, :], in_=gr[b][:]), reads=[d_gr[b]], writes=[d_grb[b]])
                gi = 0
                for el in range(2):
                    for fc in range(8):
                        j = gi % 2
                        j3 = gi % 3
                        gi += 1
                        for kc in range(8):
                            k.op("pe", lambda e: e.matmul(pG[j][:], lhsT=wg[:, el, kc, fc * 128:(fc + 1) * 128], rhs=hb[b][:, kc, :],
                                                          start=(kc == 0), stop=(kc == 7)), reads=[d_wg, d_hb[b]], writes=[d_pG[j]])
                        for kc in range(8):
                            k.op("pe", lambda e: e.matmul(pL[j][:], lhsT=wg[:, el, kc, 1024 + fc * 128:1024 + (fc + 1) * 128], rhs=hb[b][:, kc, :],
                                                          start=(kc == 0), stop=(kc == 7)), reads=[d_wg, d_hb[b]], writes=[d_pL[j]])
                        bc = (2 * ps_ + el) * 16 + fc
                        k.op("dve", lambda e: e.tensor_scalar(out=tg[j3][:], in0=pG[j][:], scalar1=bg[:, bc:bc + 1], scalar2=SW_LIMIT, op0=ALU.add, op1=ALU.min),
                             reads=[d_pG[j], d_bg], writes=[d_tg[j3]])
                        k.op("act", lambda e: e.activation(out=ts_[j3][:], in_=tg[j3][:], func=AF.Sigmoid, scale=SW_ALPHA), reads=[d_tg[j3]], writes=[d_ts[j3]])
                        k.op("dve", lambda e: e.tensor_scalar(out=tl[j3][:], in0=pL[j][:], scalar1=bg[:, bc + 8:bc + 9], scalar2=SW_LIMIT, op0=ALU.add, op1=ALU.min),
                             reads=[d_pL[j], d_bg], writes=[d_tl[j3]])
                        k.op("dve", lambda e: e.tensor_scalar(out=tl[j3][:], in0=tl[j3][:], scalar1=-SW_LIMIT, scalar2=1.0, op0=ALU.max, op1=ALU.add),
                             reads=[], writes=[d_tl[j3]])
                        k.op("pool", lambda e: e.tensor_tensor(out=ts_[j3][:], in0=ts_[j3][:], in1=tg[j3][:], op=ALU.mult), reads=[d_tg[j3]], writes=[d_ts[j3]])
                        k.op("pool", lambda e: e.tensor_tensor(out=ts_[j3][:], in0=ts_[j3][:], in1=tl[j3][:], op=ALU.mult), reads=[d_tl[j3]], writes=[d_ts[j3]])
                        ai = el * 8 + fc
                        k.op("dve", lambda e: e.tensor_tensor(out=act[:, ai, :], in0=ts_[j3][:], in1=gb[b][:, el, :], op=ALU.mult),
                             reads=[d_ts[j3], d_gb[b]], writes=[d_act[ai]])
                for tt in range(4):
                    o_, do_ = (ot[tt % 2], d_ot[tt % 2]) if ps_ == 0 else (otb[tt % 2], d_otb[tt % 2])
                    row = slice(blk_i * 512 + tt * 128, blk_i * 512 + (tt + 1) * 128)
                    dp = d_part[blk_i * 4 + tt]
                    if ps_ == 1:
                        k.dma("sp", pvt[tt % 2][:], scr[row, :], reads=[dp], writes=[d_pvt[tt % 2]])
                    for h in range(2):
                        hs = slice(h * 512, (h + 1) * 512)
                        n = 0
                        for el in range(2):
                            for fc in range(8):
                                ai = el * 8 + fc
                                k.op("pe", lambda e: e.matmul(pY[h][:], lhsT=act[:, ai, tt * 128:(tt + 1) * 128], rhs=wd[:, el, fc, hs],
                                                              start=(n == 0), stop=False), reads=[d_act[ai], d_wd], writes=[d_pY[h]])
                                n += 1
                        for el in range(2):
                            k.op("pe", lambda e: e.matmul(pY[h][:], lhsT=grb[b][:, el, tt * 128:(tt + 1) * 128], rhs=bd[:, el, hs],
                                                          start=False, stop=(el == 1)), reads=[d_grb[b], d_bd], writes=[d_pY[h]])
                        if ps_ == 0:
                            k.op("act", lambda e: e.activation(out=o_[:, hs], in_=pY[h][:], func=AF.Copy), reads=[d_pY[h]], writes=[do_])
                        else:
                            k.op("dve", lambda e: e.tensor_tensor(out=o_[:, hs], in0=pY[h][:], in1=pvt[tt % 2][:, hs], op=ALU.add),
                                 reads=[d_pY[h], d_pvt[tt % 2]], writes=[do_])
                    k.dma("sp", (scr if ps_ == 0 else part)[row, :], o_[:], reads=[do_], writes=[dp])
        k.finish([])
    return nc


def d_inputs(inp, layer, j):
    es = slice(4 * j, 4 * j + 4)
    bgu = inp["b_gate_up"][layer][es]
    b = bgu.reshape(4, 8, 128, 2)
    bg = np.concatenate([b[..., 0], b[..., 1]], 1)
    bg = np.ascontiguousarray(bg.transpose(2, 0, 1).reshape(128, 64))
    return {"wgu": np.ascontiguousarray(inp["w_gate_up"][layer][es]), "bgu": bg,
            "wdn": np.ascontiguousarray(inp["w_down"][layer][es]), "bdn": np.ascontiguousarray(inp["b_down"][layer][es])}


def f_consts():
    import ml_dtypes
    bf = ml_dtypes.bfloat16
    c = np.arange(256)[:, None]
    m = np.arange(256)[None, :]
    ang = 2 * np.pi * ((c * m) % 256) / 256.0
    G = np.concatenate([np.cos(ang), -np.sin(ang)], 1).astype(np.float32)
    l1 = np.arange(64)[:, None]
    k1 = np.arange(64)[None, :]
    a = 2 * np.pi * ((l1 * k1) % 64) / 64.0
    cc, ss = np.cos(a), np.sin(a)
    F64 = np.block([[cc, -ss], [ss, cc]]).astype(np.float32)
    l2 = np.arange(128)[:, None]
    k2 = np.arange(128)[None, :]
    a2 = 2 * np.pi * ((l2 * k2) % 128) / 128.0
    C128, S128 = np.cos(a2).astype(np.float32), np.sin(a2).astype(np.float32)
    tw = 2 * np.pi * (np.arange(128)[:, None] * np.arange(64)[None, :]) / 8192.0
    tc = np.repeat(np.cos(tw)[:, None, :], 4, 1).astype(np.float32)
    ts = np.repeat(np.sin(tw)[:, None, :], 4, 1).astype(np.float32)
    return {"G": G.astype(bf), "F64": F64.astype(bf), "C128": C128.astype(bf), "S128": S128.astype(bf), "tc": tc, "ts": ts}


def build_f():
    nc = new_nc()
    L = 8192
    xT = nc.dram_tensor("xT", [256, L], BF16, kind="ExternalInput").ap()
    Gd = nc.dram_tensor("G", [256, 512], BF16, kind="ExternalInput").ap()
    F64d = nc.dram_tensor("F64", [128, 128], BF16, kind="ExternalInput").ap()
    C128d = nc.dram_tensor("C128", [128, 128], BF16, kind="ExternalInput").ap()
    S128d = nc.dram_tensor("S128", [128, 128], BF16, kind="ExternalInput").ap()
    tcd = nc.dram_tensor("tc", [128, 4, 64], F32, kind="ExternalInput").ap()
    tsd = nc.dram_tensor("ts", [128, 4, 64], F32, kind="ExternalInput").ap()
    fo = nc.dram_tensor("f", [L, 256], BF16, kind="ExternalOutput").ap()
    scale = 1.0 / math.sqrt(L * 256.0)
    with ExitStack() as es:
        k = KB(nc, es)
        xs = k.sb([128, 2, L], BF16); d_x = Dep()
        G = k.sb([128, 2, 512], BF16); d_G = Dep()
        F64 = k.sb([128, 128], BF16)
        C128 = k.sb([128, 128], BF16)
        S128 = k.sb([128, 128], BF16)
        tc = k.sb([128, 4, 64], F32)
        ts = k.sb([128, 4, 64], F32)
        d_c = Dep()
        Wl = k.sb([128, 128, 256], BF16); d_W = Dep()
        Tp = k.sb([128, 2, 64, 256], BF16); d_T = Dep()
        k.dma("sp", xs[:], xT.rearrange("(c p) l -> p c l", p=128), writes=[d_x])
        k.dma("sp", G[:], Gd.rearrange("(c p) n -> p c n", p=128), writes=[d_G])
        for t_, s_ in ((F64, F64d), (C128, C128d), (S128, S128d), (tc, tcd), (ts, tsd)):
            k.dma("sp", t_[:], s_, writes=[d_c])
        p0 = [k.ps([128, 512]) for _ in range(2)]
        d_p0 = [Dep(), Dep()]
        pA = [k.ps([128, 512]) for _ in range(2)]
        d_pA = [Dep(), Dep()]
        pB = [k.ps([128, 512]) for _ in range(2)]
        d_pB = [Dep(), Dep()]
        xv = xs[:].rearrange("p c (a b) -> p c b a", b=128)
        for l2 in range(128):
            j = (l2 // 2) % 2
            col = (l2 % 2) * 256
            for ri in range(2):
                for cc in range(2):
                    k.op("pe", lambda e: e.matmul(p0[j][ri * 64:(ri + 1) * 64, col:col + 256], lhsT=xv[:, cc, l2, :], rhs=G[:, cc, ri * 256:(ri + 1) * 256],
                                                  start=(cc == 0), stop=(cc == 1)), reads=[d_x, d_G], writes=[d_p0[j]])
            if l2 % 2 == 1:
                eng = "act" if (l2 // 2) % 2 == 0 else "dve"
                if eng == "act":
                    k.op("act", lambda e: e.activation(out=Wl[:, l2 - 1:l2 + 1, :].rearrange("p a m -> p (a m)"), in_=p0[j][:], func=AF.Copy), reads=[d_p0[j]], writes=[d_W])
                else:
                    k.op("dve", lambda e: e.tensor_copy(out=Wl[:, l2 - 1:l2 + 1, :].rearrange("p a m -> p (a m)"), in_=p0[j][:]), reads=[d_p0[j]], writes=[d_W])
        ta = [k.sb([128, 4, 64], F32) for _ in range(2)]
        tb_ = [k.sb([128, 4, 64], F32) for _ in range(2)]
        d_ta = [Dep(), Dep()]
        d_tb = [Dep(), Dep()]
        for g in range(64):
            j = g % 2
            for q in range(4):
                m = 4 * g + q
                k.op("pe", lambda e: e.matmul(pA[j][:, q * 128:(q + 1) * 128], lhsT=Wl[:, :, m], rhs=F64[:], start=True, stop=True),
                     reads=[d_W, d_c], writes=[d_pA[j]])
            pv = pA[j][:].rearrange("p (q r k) -> p q r k", q=4, r=2)
            Tre, Tim = pv[:, :, 0, :], pv[:, :, 1, :]
            o_re = Tp[:, 0, :, 4 * g:4 * g + 4].rearrange("p k m -> p m k")
            o_im = Tp[:, 1, :, 4 * g:4 * g + 4].rearrange("p k m -> p m k")
            k.op("dve", lambda e: e.tensor_tensor(out=ta[j][:], in0=Tre, in1=tc[:], op=ALU.mult), reads=[d_pA[j], d_c], writes=[d_ta[j]])
            k.op("dve", lambda e: e.tensor_tensor(out=tb_[j][:], in0=Tim, in1=ts[:], op=ALU.mult), reads=[d_pA[j], d_c], writes=[d_tb[j]])
            k.op("pool", lambda e: e.tensor_tensor(out=o_re, in0=ta[j][:], in1=tb_[j][:], op=ALU.add), reads=[d_ta[j], d_tb[j]], writes=[d_T])
            k.op("dve", lambda e: e.tensor_tensor(out=ta[j][:], in0=Tim, in1=tc[:], op=ALU.mult), reads=[d_pA[j], d_c], writes=[d_ta[j]])
            k.op("dve", lambda e: e.tensor_tensor(out=tb_[j][:], in0=Tre, in1=ts[:], op=ALU.mult), reads=[d_pA[j], d_c], writes=[d_tb[j]])
            k.op("pool", lambda e: e.tensor_tensor(out=o_im, in0=ta[j][:], in1=tb_[j][:], op=ALU.subtract), reads=[d_ta[j], d_tb[j]], writes=[d_T])
        ob = [k.sb([128, 512], BF16) for _ in range(2)]
        d_ob = [Dep(), Dep()]
        fv = fo.rearrange("(k2 k1) m -> k2 k1 m", k1=64)
        Tf = Tp[:].rearrange("p r k m -> p r (k m)")
        for blk_i in range(32):
            j = blk_i % 2
            cs = slice(blk_i * 512, (blk_i + 1) * 512)
            k.op("pe", lambda e: e.matmul(pB[j][:], lhsT=C128[:], rhs=Tf[:, 0, cs], start=True, stop=False), reads=[d_T, d_c], writes=[d_pB[j]])
            k.op("pe", lambda e: e.matmul(pB[j][:], lhsT=S128[:], rhs=Tf[:, 1, cs], start=False, stop=True), reads=[d_T, d_c], writes=[d_pB[j]])
            k.op("act", lambda e: e.activation(out=ob[j][:], in_=pB[j][:], func=AF.Copy, scale=scale), reads=[d_pB[j]], writes=[d_ob[j]])
            k.dma("sp", fv[:, 2 * blk_i:2 * blk_i + 2, :], ob[j][:].rearrange("p (k m) -> p k m", k=2), reads=[d_ob[j]], writes=[Dep()])
        k.finish([])
    return nc


D_MODEL = 1024
COL_Q, COL_Z, COL_C, COL_K, COL_V, COL_XB, COL_DT = 0, 1024, 2048, 2304, 3328, 4352, 5632


def _b_inputs(inp, j):
    g = j // 4
    w = inp["w_in"][0]
    dtc = [COL_DT + d * 16 + 2 * j + hd for d in range(2) for hd in range(2)]
    cols = [w[:, COL_Q + j * 128: COL_Q + (j + 1) * 128], w[:, COL_K + j * 128: COL_K + (j + 1) * 128],
            w[:, COL_V + j * 128:COL_V + (j + 1) * 128], w[:, dtc],
            w[:, COL_Z + j * 128:COL_Z + (j + 1) * 128], w[:, COL_XB + j * 128:COL_XB + (j + 1) * 128],
            w[:, COL_XB + 1024 + g * 128:COL_XB + 1024 + (g + 1) * 128], w[:, COL_C + g * 128:COL_C + (g + 1) * 128]]
    W = np.ascontiguousarray(np.concatenate(cols, 1))
    pp = np.zeros((128, 32), np.float32)
    p = np.arange(128)
    pp[:, 0] = inp["q_norm_g"][0][p % 64]
    pp[:, 1] = inp["k_norm_g"][0][p % 64]
    pp[:, 2] = inp["da_subln_g"][0]
    pp[:, 3:8] = inp["conv_xb_w"][0][:, j * 128:(j + 1) * 128].T
    pp[:, 8] = inp["conv_xb_b"][0][j * 128:(j + 1) * 128]
    pp[:, 9:14] = inp["conv_xb_w"][0][:, 1024 + g * 128:1024 + (g + 1) * 128].T
    pp[:, 14] = inp["conv_xb_b"][0][1024 + g * 128:1024 + (g + 1) * 128]
    pp[:, 15:20] = inp["conv_c_w"][0][:, g * 128:(g + 1) * 128].T
    pp[:, 20] = inp["conv_c_b"][0][g * 128:(g + 1) * 128]
    pp[:, 21] = inp["d_skip"][0][2 * j + p // 64]
    pp[:, 22] = inp["ssm_norm_g"][0][j * 128:(j + 1) * 128]
    p4 = np.zeros((2, 4), np.float32)
    for d in range(2):
        for hd in range(2):
            p4[0, d * 2 + hd] = inp["dt_bias"][0][d, 2 * j + hd]
            p4[1, d * 2 + hd] = inp["a_log"][0][d, 2 * j + hd]
    return {"W": W, "pp": pp, "p4": p4, "lam4": np.ascontiguousarray(inp["da_lambda"][0])}


def _moe_stage(inp, layer, h2T, gatesT):
    ntok = h2T.shape[1]
    masks = [(gatesT[4 * j:4 * j + 4] > 0).any(0) for j in range(NCORES)]
    idxs = [np.nonzero(m_)[0] for m_ in masks]
    cap = max(512, max((len(i_) + 511) // 512 * 512 for i_ in idxs))
    nc = build_d(cap)
    in_maps = []
    for j in range(NCORES):
        d = d_inputs(inp, layer, j)
        n_j = len(idxs[j])
        hj = np.zeros((1024, cap), h2T.dtype)
        hj[:, :n_j] = h2T[:, idxs[j]]
        gj = np.zeros((4, cap), np.float32)
        gj[:, :n_j] = gatesT[4 * j:4 * j + 4][:, idxs[j]]
        d["h2T"] = _to_blocks(hj, 512)
        d["gT"] = gj
        in_maps.append(d)
    res = run_spmd(nc, in_maps)
    mm = np.stack(masks, 0).astype(np.int64)
    slot = np.cumsum(mm, 0) - mm
    nslot = int(mm.sum(0).max())
    p0 = res[0]["part"]
    parts = np.zeros((max(nslot, 1), ntok, 1024), p0.dtype)
    for j in range(NCORES):
        n_j = len(idxs[j])
        if n_j:
            parts[slot[j, idxs[j]], idxs[j]] = res[j]["part"][:n_j]
    return parts


def _hT_from(res, n):
    blks = np.concatenate([res[r]["hT"] for r in range(n)], 0)
    nt = blks.shape[0]
    return np.ascontiguousarray(blks.reshape(nt, 128, 8, 128).transpose(2, 1, 0, 3).reshape(1024, nt * 128))


def _to_blocks(hT, bs):
    return np.ascontiguousarray(hT.reshape(8, 128, -1, bs).transpose(2, 1, 0, 3))


def kernel(**inp):
    inp = {k_: np.asarray(v) for k_, v in inp.items()}
    ident = np.eye(128, dtype=np.float32)
    TPC = 2048
    x0 = np.ascontiguousarray(inp["x"].reshape(16384, 1024))
    mod = run_l0(inp)
    modr = mod.reshape(2, 3, 6, 1024)
    nc = build_t1(front=None, norm=(0, 1), h_layout="T", router=False, out_x=False)
    res = run_spmd(nc, [{"x": x0[r * TPC:(r + 1) * TPC], "mod": np.ascontiguousarray(modr[0, r // 4]), "ident": ident,
                         "ng": inp["norm_g"][0, 0]} for r in range(NCORES)])
    hT0 = _hT_from(res, NCORES)
    ctxf = np.ascontiguousarray(inp["ctx"].reshape(512, 1024))
    nc = build_t1(front=None, norm=(0, 1), h_layout="T", router=False, out_x=False, T=128)
    res = run_spmd(nc, [{"x": ctxf[(r % 4) * 128:(r % 4 + 1) * 128], "mod": np.ascontiguousarray(modr[0, 2]), "ident": ident,
                         "ng": inp["norm_g"][0, 0]} for r in range(NCORES)])
    hcT = _hT_from(res, 4)
    lam_init = 0.8 - 0.6 * math.exp(-0.3 * 0)
    cst = b_consts()
    cst["tri"] = np.triu(np.ones((128, 128), np.float32))
    nc = build_b(lam_init)
    hT0b = _to_blocks(hT0, 512)
    hcTb = _to_blocks(hcT, CTX)
    in_maps = []
    for j in range(NCORES):
        d = {"hT": hT0b, "hcT": hcTb}
        d.update(_b_inputs(inp, j))
        d.update(cst)
        in_maps.append(d)
    res = run_spmd(nc, in_maps)
    oT = np.concatenate([res[j]["oT"] for j in range(NCORES)], 0)
    ygT = np.concatenate([res[j]["ygT"] for j in range(NCORES)], 0)
    ss = np.ascontiguousarray(np.concatenate([res[j]["ss"] for j in range(NCORES)], 0).T)
    del res
    nc = build_t1(front="mix", nA=8, nB=8, gate_row=2, norm=(3, 4), h_layout="T", router=True, out_x=True)
    in_maps = []
    for r in range(NCORES):
        ts_ = slice(r * TPC, (r + 1) * TPC)
        in_maps.append({"x": x0[ts_], "mod": np.ascontiguousarray(modr[0, r // 4]), "ident": ident,
                        "mixA": np.ascontiguousarray(oT[:, ts_]), "wA": np.ascontiguousarray(inp["w_out"][0][0:1024]),
                        "mixB": np.ascontiguousarray(ygT[:, ts_]), "wB": np.ascontiguousarray(inp["w_out"][0][1024:2048]),
                        "ss": np.ascontiguousarray(ss[ts_]), "ng": inp["norm_g"][0, 1],
                        "wr": inp["w_router"][0], "br": inp["b_router"][0]})
    res = run_spmd(nc, in_maps)
    x1 = [res[r]["xo"] for r in range(NCORES)]
    h2T = _hT_from(res, NCORES)
    gatesT = np.concatenate([res[r]["gatesT"] for r in range(NCORES)], 1)
    parts = _moe_stage(inp, 0, h2T, gatesT)
    nc = build_t1(front="parts", gate_row=5, norm=(0, 1), h_layout="T", router=False, out_x=True, nparts=parts.shape[0])
    in_maps = []
    for r in range(NCORES):
        ts_ = slice(r * TPC, (r + 1) * TPC)
        in_maps.append({"x": x1[r], "mod": np.ascontiguousarray(np.concatenate([modr[1, r // 4][0:5], modr[0, r // 4][5:6]], 0)),
                        "ident": ident, "parts": np.ascontiguousarray(parts[:, ts_]),
                        "ng": inp["norm_g"][1, 0]})
    res = run_spmd(nc, in_maps)
    del parts
    x2 = [res[r]["xo"] for r in range(NCORES)]
    hT1 = _hT_from(res, NCORES)
    fc = f_consts()
    nc = build_f()
    in_maps = []
    for r in range(NCORES):
        b, g = r // 4, r % 4
        d = dict(fc)
        d["xT"] = np.ascontiguousarray(hT1[g * 256:(g + 1) * 256, b * 8192:(b + 1) * 8192])
        in_maps.append(d)
    res = run_spmd(nc, in_maps)
    fT = np.concatenate([np.concatenate([res[b * 4 + g]["f"].T for g in range(4)], 0) for b in range(2)], 1)
    nc = build_t1(front="mix", nA=8, nB=0, gate_row=2, norm=(3, 4), h_layout="T", router=True, out_x=True)
    in_maps = []
    for r in range(NCORES):
        ts_ = slice(r * TPC, (r + 1) * TPC)
        in_maps.append({"x": x2[r], "mod": np.ascontiguousarray(modr[1, r // 4]), "ident": ident,
                        "mixA": np.ascontiguousarray(fT[:, ts_]), "wA": inp["w_fourier"][0], "ng": inp["norm_g"][1, 1],
                        "wr": inp["w_router"][1], "br": inp["b_router"][1]})
    res = run_spmd(nc, in_maps)
    x3 = [res[r]["xo"] for r in range(NCORES)]
    h2T = _hT_from(res, NCORES)
    gatesT = np.concatenate([res[r]["gatesT"] for r in range(NCORES)], 1)
    parts = _moe_stage(inp, 1, h2T, gatesT)
    nc = build_t1(front="parts", gate_row=5, norm=None, router=False, out_x=True, nparts=parts.shape[0])
    in_maps = []
    for r in range(NCORES):
        ts_ = slice(r * TPC, (r + 1) * TPC)
        in_maps.append({"x": x3[r], "mod": np.ascontiguousarray(modr[1, r // 4]), "ident": ident,
                        "parts": np.ascontiguousarray(parts[:, ts_])})
    res = run_spmd(nc, in_maps)
    out = np.concatenate([res[r]["xo"] for r in range(NCORES)], 0).reshape(2, 8192, 1024)
    return out.astype(np.float32)
```
